# Optimizing a Trainium2 kernel written in Bass

```python
import math
import jax, jax.numpy as jnp
from jax import lax
import numpy as np

D_MODEL = 1024
BATCH = 8
SEQ = 2048
DEPTH = 4
DEC_BATCH = 128
DEC_SEQ = 8
PAST_LEN = 8192
PAGE_SIZE = 128

N_EVEN = (DEPTH + 1) // 2
N_ODD = DEPTH // 2
CONV_A_DIM = D_MODEL // 2
CONV_A_WIDTH = 31
HEAD_DIM = 64
N_Q_HEADS = (D_MODEL // 2) // HEAD_DIM
N_KV_HEADS = N_Q_HEADS // 4
WINDOW = 128
ROPE_DIM = HEAD_DIM // 4
ROPE_THETA = 500000.0
IN_EVEN = 2 * CONV_A_DIM + (N_Q_HEADS + 2 * N_KV_HEADS) * HEAD_DIM
OUT_EVEN_IN = CONV_A_DIM + N_Q_HEADS * HEAD_DIM
D_INNER = 2 * D_MODEL
SSM_HEAD_DIM = 64
SSM_HEADS = D_INNER // SSM_HEAD_DIM
SSM_STATE = 128
SSM_GROUPS = 4
SSM_CONV_WIDTH = 4
SSM_CONV_DIM = D_INNER + 2 * SSM_GROUPS * SSM_STATE
IN_ODD = D_INNER + SSM_CONV_DIM + SSM_HEADS
SSD_CHUNK = 128
D_FF = ((8 * D_MODEL // 3 + 255) // 256) * 256
FFN_CONV_WIDTH = 3
EPS = 1e-6

kernel_name = "hybrid_conformer_swa_mamba2_convffn_step"


def rmsnorm(x, g):
    xf = x.astype(jnp.float32)
    xf = xf * lax.rsqrt(jnp.mean(xf * xf, axis=-1, keepdims=True) + EPS)
    return xf.astype(x.dtype) * g


def layernorm(x, g, b):
    xf = x.astype(jnp.float32)
    mu = jnp.mean(xf, axis=-1, keepdims=True)
    var = jnp.mean(jnp.square(xf - mu), axis=-1, keepdims=True)
    return ((xf - mu) * lax.rsqrt(var + EPS)).astype(x.dtype) * g + b


def causal_dwconv(padded, w, b):
    c = padded.shape[-1]
    y = lax.conv_general_dilated(padded, w[:, None, :], (1,), 'VALID',
                                 dimension_numbers=('NWC', 'WIO', 'NWC'), feature_group_count=c)
    return y + b


def partial_rope(x, pos):
    half = ROPE_DIM // 2
    inv_freq = ROPE_THETA ** (-jnp.arange(0, ROPE_DIM, 2, dtype=jnp.float32) / ROPE_DIM)
    ang = pos.astype(jnp.float32)[:, None] * inv_freq[None, :]
    cos = jnp.cos(ang)[:, None, :]
    sin = jnp.sin(ang)[:, None, :]
    xr = x[..., :ROPE_DIM].astype(jnp.float32)
    x1, x2 = xr[..., :half], xr[..., half:]
    rot = jnp.concatenate([x1 * cos - x2 * sin, x2 * cos + x1 * sin], axis=-1).astype(x.dtype)
    return jnp.concatenate([rot, x[..., ROPE_DIM:]], axis=-1)


def sink_attention(q, k, v, q_pos, k_pos, sinks):
    b, n, tq, h, d = q.shape
    kv = k.shape[3]
    g = h // kv
    qg = q.reshape(b, n, tq, kv, g, d)
    s = jnp.einsum('bnqkgd,bnskd->bnkgqs', qg, k).astype(jnp.float32) * (d ** -0.5)
    rel = q_pos[:, :, None] - k_pos[:, None, :]
    valid = (rel >= 0) & (rel < WINDOW) & (k_pos[:, None, :] >= 0)
    s = jnp.where(valid[None, :, None, None], s, -jnp.inf)
    sink = jnp.broadcast_to(sinks.astype(jnp.float32).reshape(kv, g)[None, None, :, :, None, None],
                            s.shape[:-1] + (1,))
    p = jax.nn.softmax(jnp.concatenate([s, sink], axis=-1), axis=-1)[..., :-1].astype(v.dtype)
    o = jnp.einsum('bnkgqs,bnskd->bnqkgd', p, v)
    return o.reshape(b, n, tq, h, d)


def even_mixer(h, conv_hist, k_hist, v_hist, start, w_in, conv_w, conv_b, ln_g, ln_b, qn_g, kn_g, sinks, w_out):
    b, t, _ = h.shape
    qd, kd = N_Q_HEADS * HEAD_DIM, N_KV_HEADS * HEAD_DIM
    a_val, a_gate, q, k, v = jnp.split(
        h @ w_in, [CONV_A_DIM, 2 * CONV_A_DIM, 2 * CONV_A_DIM + qd, 2 * CONV_A_DIM + qd + kd], axis=-1)
    u = a_val * jax.nn.sigmoid(a_gate)
    padded = jnp.concatenate([conv_hist, u], axis=1)
    c = jax.nn.silu(layernorm(causal_dwconv(padded, conv_w, conv_b), ln_g, ln_b))
    new_conv = padded[:, -(CONV_A_WIDTH - 1):]
    pos = start + jnp.arange(t, dtype=jnp.int32)
    q = partial_rope(rmsnorm(q.reshape(b, t, N_Q_HEADS, HEAD_DIM), qn_g), pos)
    k = partial_rope(rmsnorm(k.reshape(b, t, N_KV_HEADS, HEAD_DIM), kn_g), pos)
    v = v.reshape(b, t, N_KV_HEADS, HEAD_DIM)
    if k_hist is None:
        nblk = t // WINDOW
        qb = q.reshape(b, nblk, WINDOW, N_Q_HEADS, HEAD_DIM)
        kb = k.reshape(b, nblk, WINDOW, N_KV_HEADS, HEAD_DIM)
        vb = v.reshape(b, nblk, WINDOW, N_KV_HEADS, HEAD_DIM)
        kk = jnp.concatenate([jnp.concatenate([jnp.zeros_like(kb[:, :1]), kb[:, :-1]], axis=1), kb], axis=2)
        vv = jnp.concatenate([jnp.concatenate([jnp.zeros_like(vb[:, :1]), vb[:, :-1]], axis=1), vb], axis=2)
        q_pos = pos.reshape(nblk, WINDOW)
        k_pos = (jnp.arange(nblk, dtype=jnp.int32) * WINDOW + start - WINDOW)[:, None] + \
            jnp.arange(2 * WINDOW, dtype=jnp.int32)[None, :]
        o = sink_attention(qb, kk, vv, q_pos, k_pos, sinks)
        k_all, v_all = k, v
    else:
        wb = k_hist.shape[1]
        k_all = jnp.concatenate([k_hist, k], axis=1)
        v_all = jnp.concatenate([v_hist, v], axis=1)
        k_pos = start - wb + jnp.arange(wb + t, dtype=jnp.int32)
        o = sink_attention(q[:, None], k_all[:, None], v_all[:, None], pos[None], k_pos[None], sinks)
    o = o.reshape(b, t, N_Q_HEADS * HEAD_DIM)
    out = jnp.concatenate([c, o], axis=-1) @ w_out
    return out, new_conv, k_all[:, -WINDOW:], v_all[:, -WINDOW:]


def ssd_scan(x, dt, a, bm, cm, h0):
    b, t, h, p = x.shape
    g, n = bm.shape[2], bm.shape[3]
    r = h // g
    dty = x.dtype
    L = SSD_CHUNK if t % SSD_CHUNK == 0 else t
    c = t // L
    xc = x.reshape(b, c, L, g, r, p)
    bc = bm.reshape(b, c, L, g, n)
    cc = cm.reshape(b, c, L, g, n)
    dtc = dt.reshape(b, c, L, g, r)
    acum = jnp.cumsum(dtc * a.reshape(g, r), axis=2)
    causal = jnp.tril(jnp.ones((L, L), dtype=bool))[None, None, :, :, None, None]
    diff = acum[:, :, :, None] - acum[:, :, None, :]
    decay = jnp.exp(jnp.where(causal, diff, -jnp.inf))
    cb = jnp.einsum('bclgn,bcsgn->bclsg', cc, bc).astype(jnp.float32)
    w_intra = (cb[..., None] * decay * dtc[:, :, None]).astype(dty)
    y_diag = jnp.einsum('bclsgr,bcsgrp->bclgrp', w_intra, xc)
    decay_end = jnp.exp(acum[:, :, -1:] - acum)
    states = jnp.einsum('bclgn,bclgr,bclgrp->bcgrpn', bc, (decay_end * dtc).astype(dty), xc)
    chunk_decay = jnp.exp(acum[:, :, -1])

    def step(hc, inp):
        dec, st = inp
        return dec[..., None, None] * hc + st, hc

    h_final, h_prev = lax.scan(step, h0.reshape(b, g, r, p, n).astype(jnp.float32),
                               (jnp.moveaxis(chunk_decay, 1, 0), jnp.moveaxis(states.astype(jnp.float32), 1, 0)))
    h_prev = jnp.moveaxis(h_prev, 0, 1).astype(dty)
    y_off = jnp.einsum('bclgn,bcgrpn,bclgr->bclgrp', cc, h_prev, jnp.exp(acum).astype(dty))
    y = (y_diag + y_off).reshape(b, t, h, p)
    return y, h_final.reshape(b, h, p, n).astype(h0.dtype)


def odd_mixer(h, conv_hist, ssm_hist, w_in, conv_w, conv_b, dt_bias, a_log, d_skip, gn_g, w_out):
    b, t, _ = h.shape
    gn = SSM_GROUPS * SSM_STATE
    z, xbc, dt = jnp.split(h @ w_in, [D_INNER, D_INNER + SSM_CONV_DIM], axis=-1)
    padded = jnp.concatenate([conv_hist, xbc], axis=1)
    xbc = jax.nn.silu(causal_dwconv(padded, conv_w, conv_b))
    new_conv = padded[:, -(SSM_CONV_WIDTH - 1):]
    xs, bm, cm = jnp.split(xbc, [D_INNER, D_INNER + gn], axis=-1)
    xs = xs.reshape(b, t, SSM_HEADS, SSM_HEAD_DIM)
    bm = bm.reshape(b, t, SSM_GROUPS, SSM_STATE)
    cm = cm.reshape(b, t, SSM_GROUPS, SSM_STATE)
    dt = jax.nn.softplus(dt.astype(jnp.float32) + dt_bias.astype(jnp.float32))
    a = -jnp.exp(a_log.astype(jnp.float32))
    y, new_ssm = ssd_scan(xs, dt, a, bm, cm, ssm_hist)
    y = (y + d_skip[:, None] * xs).reshape(b, t, D_INNER) * jax.nn.silu(z)
    gs = D_INNER // SSM_GROUPS
    y = rmsnorm(y.reshape(b, t, SSM_GROUPS, gs), gn_g.reshape(SSM_GROUPS, gs)).reshape(b, t, D_INNER)
    return y @ w_out, new_conv, new_ssm


def conv_ffn(h, hist, w_gate, w_up, conv_w, conv_b, w_down):
    gate = h @ w_gate
    padded = jnp.concatenate([hist, gate], axis=1)
    gc = causal_dwconv(padded, conv_w, conv_b)
    y = (jax.nn.silu(gc) * (h @ w_up)) @ w_down
    return y, padded[:, -(FFN_CONV_WIDTH - 1):]


def setup_inputs(seed: int = 0) -> dict:
    key = jax.random.key(seed)
    ks = iter(jax.random.split(key, 48))

    def nrm(shape, scale):
        return scale * jax.random.normal(next(ks), shape, jnp.float32)

    def gain(shape):
        return 1.0 + nrm(shape, 0.02)

    dt0 = jnp.exp(jax.random.uniform(next(ks), (N_ODD, SSM_HEADS), jnp.float32,
                                     minval=math.log(1e-3), maxval=math.log(1e-1)))
    a_init = jax.random.uniform(next(ks), (N_ODD, SSM_HEADS), jnp.float32, minval=1.0, maxval=16.0)
    return {
        "x_prompt": nrm((BATCH, SEQ, D_MODEL), 1.0),
        "x_sample": nrm((DEC_BATCH, DEC_SEQ, D_MODEL), 1.0),
        "state_conv_a": nrm((N_EVEN, DEC_BATCH, CONV_A_WIDTH - 1, CONV_A_DIM), 0.5),
        "cache_win_k": nrm((N_EVEN, DEC_BATCH, WINDOW, N_KV_HEADS, HEAD_DIM), 1.0),
        "cache_win_v": nrm((N_EVEN, DEC_BATCH, WINDOW, N_KV_HEADS, HEAD_DIM), 1.0),
        "state_conv_c": nrm((N_ODD, DEC_BATCH, SSM_CONV_WIDTH - 1, SSM_CONV_DIM), 1.0),
        "state_ssm": nrm((N_ODD, DEC_BATCH, SSM_HEADS, SSM_HEAD_DIM, SSM_STATE), 0.1),
        "state_ffn_conv": nrm((DEPTH, DEC_BATCH, FFN_CONV_WIDTH - 1, D_FF), 1.0),
        "norm_mix_e": gain((N_EVEN, D_MODEL)),
        "w_in_e": nrm((N_EVEN, D_MODEL, IN_EVEN), D_MODEL ** -0.5),
        "conv_a_w": nrm((N_EVEN, CONV_A_WIDTH, CONV_A_DIM), CONV_A_WIDTH ** -0.5),
        "conv_a_b": nrm((N_EVEN, CONV_A_DIM), 0.01),
        "ln_a_g": gain((N_EVEN, CONV_A_DIM)),
        "ln_a_b": nrm((N_EVEN, CONV_A_DIM), 0.01),
        "q_norm_g": gain((N_EVEN, HEAD_DIM)),
        "k_norm_g": gain((N_EVEN, HEAD_DIM)),
        "sinks": nrm((N_EVEN, N_Q_HEADS), 1.0),
        "w_out_e": nrm((N_EVEN, OUT_EVEN_IN, D_MODEL), OUT_EVEN_IN ** -0.5),
        "norm_mix_o": gain((N_ODD, D_MODEL)),
        "w_in_o": nrm((N_ODD, D_MODEL, IN_ODD), D_MODEL ** -0.5),
        "conv_c_w": nrm((N_ODD, SSM_CONV_WIDTH, SSM_CONV_DIM), SSM_CONV_WIDTH ** -0.5),
        "conv_c_b": nrm((N_ODD, SSM_CONV_DIM), 0.01),
        "dt_bias": dt0 + jnp.log(-jnp.expm1(-dt0)),
        "a_log": jnp.log(a_init),
        "d_skip": gain((N_ODD, SSM_HEADS)),
        "gnorm_c": gain((N_ODD, D_INNER)),
        "w_out_o": nrm((N_ODD, D_INNER, D_MODEL), D_INNER ** -0.5),
        "norm_ffn": gain((DEPTH, D_MODEL)),
        "w_gate": nrm((DEPTH, D_MODEL, D_FF), D_MODEL ** -0.5),
        "w_up": nrm((DEPTH, D_MODEL, D_FF), D_MODEL ** -0.5),
        "ffn_conv_w": nrm((DEPTH, FFN_CONV_WIDTH, D_FF), FFN_CONV_WIDTH ** -0.5),
        "ffn_conv_b": nrm((DEPTH, D_FF), 0.01),
        "w_down": nrm((DEPTH, D_FF, D_MODEL), D_FF ** -0.5),
    }


def reference(x_prompt, x_sample, state_conv_a, cache_win_k, cache_win_v, state_conv_c, state_ssm, state_ffn_conv,
              norm_mix_e, w_in_e, conv_a_w, conv_a_b, ln_a_g, ln_a_b, q_norm_g, k_norm_g, sinks, w_out_e,
              norm_mix_o, w_in_o, conv_c_w, conv_c_b, dt_bias, a_log, d_skip, gnorm_c, w_out_o,
              norm_ffn, w_gate, w_up, ffn_conv_w, ffn_conv_b, w_down):
    xp, xs = x_prompt, x_sample
    bp = xp.shape[0]
    dty = xp.dtype
    ca_p, ca_s, wk_p, wk_s, wv_p, wv_s = [], [], [], [], [], []
    cc_p, cc_s, ss_p, ss_s, ff_p, ff_s = [], [], [], [], [], []
    for layer in range(DEPTH):
        i = layer // 2
        if layer % 2 == 0:
            ew = (w_in_e[i], conv_a_w[i], conv_a_b[i], ln_a_g[i], ln_a_b[i], q_norm_g[i], k_norm_g[i], sinks[i], w_out_e[i])
            dp, c1, k1, v1 = even_mixer(rmsnorm(xp, norm_mix_e[i]),
                                        jnp.zeros((bp, CONV_A_WIDTH - 1, CONV_A_DIM), dty), None, None, 0, *ew)
            ds, c2, k2, v2 = even_mixer(rmsnorm(xs, norm_mix_e[i]), state_conv_a[i], cache_win_k[i], cache_win_v[i],
                                        PAST_LEN, *ew)
            ca_p.append(c1); ca_s.append(c2); wk_p.append(k1); wk_s.append(k2); wv_p.append(v1); wv_s.append(v2)
        else:
            ow = (w_in_o[i], conv_c_w[i], conv_c_b[i], dt_bias[i], a_log[i], d_skip[i], gnorm_c[i], w_out_o[i])
            dp, c1, s1 = odd_mixer(rmsnorm(xp, norm_mix_o[i]),
                                   jnp.zeros((bp, SSM_CONV_WIDTH - 1, SSM_CONV_DIM), dty),
                                   jnp.zeros((bp, SSM_HEADS, SSM_HEAD_DIM, SSM_STATE), dty), *ow)
            ds, c2, s2 = odd_mixer(rmsnorm(xs, norm_mix_o[i]), state_conv_c[i], state_ssm[i], *ow)
            cc_p.append(c1); cc_s.append(c2); ss_p.append(s1); ss_s.append(s2)
        xp = xp + dp
        xs = xs + ds
        fw = (w_gate[layer], w_up[layer], ffn_conv_w[layer], ffn_conv_b[layer], w_down[layer])
        fp, f1 = conv_ffn(rmsnorm(xp, norm_ffn[layer]), jnp.zeros((bp, FFN_CONV_WIDTH - 1, D_FF), dty), *fw)
        fs, f2 = conv_ffn(rmsnorm(xs, norm_ffn[layer]), state_ffn_conv[layer], *fw)
        ff_p.append(f1); ff_s.append(f2)
        xp = xp + fp
        xs = xs + fs
    return (xp, xs,
            jnp.stack(ca_p), jnp.stack(ca_s),
            jnp.stack(wk_p), jnp.stack(wk_s),
            jnp.stack(wv_p), jnp.stack(wv_s),
            jnp.stack(cc_p), jnp.stack(cc_s),
            jnp.stack(ss_p), jnp.stack(ss_s),
            jnp.stack(ff_p), jnp.stack(ff_s))
```

```python
import contextlib
import math
import os
import numpy as np
import concourse.bass as bass
import concourse.mybir as mybir
from concourse.bass_utils import run_bass_kernel_spmd

F32 = mybir.dt.float32
BF16 = mybir.dt.bfloat16
I32 = mybir.dt.int32
AF = mybir.ActivationFunctionType
ALU = mybir.AluOpType

NCORES = 8
D = 1024
TP = 2048
TS = 128
T = TP + TS
NSEQ = 16
DFF = 2816
NJ = DFF // 128
EPS = 1e-6
NEG = -30000.0
ARENA_WORDS = 24300


class Res:
    __slots__ = ("name", "last_w", "readers", "dsem", "dcount", "excl")

    def __init__(self, name, excl=False):
        self.name = name
        self.excl = excl
        self.last_w = None
        self.readers = []
        self.dsem = None
        self.dcount = 0


class Op:
    __slots__ = ("eng", "fn", "waits", "signal", "sigval", "dma_res", "dma_val")

    def __init__(self, eng, fn):
        self.eng = eng
        self.fn = fn
        self.waits = []
        self.signal = False
        self.sigval = 0
        self.dma_res = None
        self.dma_val = 0


class Sched:
    ENGS = ("pe", "act", "dve", "pool", "sp")

    def __init__(self, nc):
        self.nc = nc
        self.ops = []
        self.dma_res = []
        self.last = {e: None for e in self.ENGS}
        self.dma_since = []
        self.pending = {e: [] for e in self.ENGS}

    def op(self, eng, fn, reads=(), writes=(), dma=None):
        o = Op(eng, fn)
        writes = list(writes) + [r for r in reads if r.excl and r not in writes]
        deps = [(d, True) for d in self.pending[eng]]
        self.pending[eng] = []
        for r in reads:
            if r.last_w is not None:
                deps.append((r.last_w, True))
        for w in writes:
            if w.last_w is not None:
                deps.append((w.last_w, False))
            deps.extend((x, False) for x in w.readers)
        seen = set()
        for d, raw in deps:
            if d.dma_res is None and d.eng == eng:
                if eng in ("pe", "sp") or not raw:
                    continue
            if id(d) in seen:
                continue
            seen.add(id(d))
            o.waits.append(d)
            d.signal = True
        if dma is not None:
            o.dma_res = dma
            dma.dcount += 1
            o.dma_val = 16 * dma.dcount
            o.signal = True
            if dma.dsem is None:
                dma.dsem = True
                self.dma_res.append(dma)
            self.dma_since.append(o)
        else:
            self.last[eng] = o
        for w in writes:
            w.last_w = o
            w.readers = []
        for r in reads:
            r.readers.append(o)
        self.ops.append(o)
        return o

    def barrier(self):
        tg = [o for o in self.last.values() if o is not None] + self.dma_since
        self.dma_since = []
        for e in self.ENGS:
            self.pending[e] = list(tg)

    def emit(self, stack):
        nc = self.nc
        esem = {e: stack.enter_context(nc.semaphore("s_" + e)) for e in self.ENGS}
        for i, r in enumerate(self.dma_res):
            r.dsem = stack.enter_context(nc.semaphore("d%d" % i))
        cnt = {e: 0 for e in self.ENGS}
        per = {e: [] for e in self.ENGS}
        for o in self.ops:
            if o.dma_res is None and o.signal:
                cnt[o.eng] += 1
                o.sigval = cnt[o.eng]
            per[o.eng].append(o)
        block = stack.enter_context(nc.Block())

        def run(eng_name, eng):
            known = {}
            for o in per[eng_name]:
                for d in o.waits:
                    if d.dma_res is not None:
                        key, val = d.dma_res.dsem, d.dma_val
                    else:
                        key, val = esem[d.eng], d.sigval
                    if known.get(id(key), 0) >= val:
                        continue
                    known[id(key)] = val
                    eng.wait_ge(key, val)
                ins = o.fn(eng)
                if o.dma_res is not None:
                    ins.then_inc(o.dma_res.dsem, 16)
                elif o.signal:
                    ins.then_inc(esem[eng_name], 1)
            if eng_name == "sp":
                for r in self.dma_res:
                    if known.get(id(r.dsem), 0) < 16 * r.dcount:
                        eng.wait_ge(r.dsem, 16 * r.dcount)
                for e2 in ("pe", "act", "dve", "pool"):
                    if cnt[e2] > 0:
                        eng.wait_ge(esem[e2], cnt[e2])

        @block.tensor
        def _(e):
            run("pe", e)

        @block.scalar
        def _(e):
            run("act", e)

        @block.vector
        def _(e):
            run("dve", e)

        @block.gpsimd
        def _(e):
            run("pool", e)

        @block.sync
        def _(e):
            run("sp", e)


class Arena:
    def __init__(self, t, words):
        self.t = t
        self.words = words
        self.off = 0

    def reset(self):
        self.off = 0

    def f32(self, n, name):
        a = self.off
        self.off += n
        assert self.off <= self.words, (name, self.off, self.words)
        return self.t[:, a:a + n], Res(name)

    def bf(self, n, name):
        w = (n + 1) // 2
        a = self.off
        self.off += w
        assert self.off <= self.words, (name, self.off, self.words)
        return self.t[:, a:a + w].bitcast(BF16)[:, 0:n], Res(name)


def vec_layout():
    ent = []
    for i in range(2):
        ent += [("nme%d" % i, 8), ("nmo%d" % i, 8), ("caw%d" % i, 124), ("cab%d" % i, 4), ("lng%d" % i, 4),
                ("lnb%d" % i, 4), ("qg%d" % i, 1), ("kg%d" % i, 1), ("snk%d" % i, 4), ("ccw%d" % i, 96),
                ("ccb%d" % i, 24), ("dtb%d" % i, 1), ("alog%d" % i, 1), ("dsk%d" % i, 16), ("gn%d" % i, 16)]
    for l in range(4):
        ent += [("nf%d" % l, 8), ("fcw%d" % l, 66), ("fcb%d" % l, 22)]
    ent += [("invf", 1)]
    off = {}
    o = 0
    for n, c in ent:
        off[n] = o
        o += c
    return off, o


CF = dict(ident=0, tri=128, ut=256, tri_s=384, ut_s=512, pmat=640, seqmask=768, ones=784)
NCF = 912
CB = dict(ident=0, o1024=128, o512=256, bd64=384, mcur=512, mprev=640, msnew=768, mshist=896, cbm_p=1024,
          cbm_s=1152, ones=1280)
NCB = 1408


def host_consts():
    cf = np.zeros((128, NCF), np.float32)
    cb = np.zeros((128, NCB), np.float32)
    idx = np.arange(128)
    eye = np.eye(128, dtype=np.float32)
    tri = (idx[:, None] <= idx[None, :]).astype(np.float32)
    ut = (idx[:, None] > idx[None, :]).astype(np.float32)
    tt_, ss_ = idx // 16, idx % 16
    same = ss_[:, None] == ss_[None, :]
    tri_s = (same & (tt_[:, None] <= tt_[None, :])).astype(np.float32)
    ut_s = (same & (tt_[:, None] > tt_[None, :])).astype(np.float32)
    pm = np.zeros((128, 128), np.float32)
    for hb in (0, 64):
        for d in range(8):
            pm[hb + d + 8, hb + d] = -1.0
            pm[hb + d, hb + d + 8] = 1.0
    cf[:, 0:128] = eye
    cf[:, 128:256] = tri
    cf[:, 256:384] = ut
    cf[:, 384:512] = tri_s
    cf[:, 512:640] = ut_s
    cf[:, 640:768] = pm
    cf[:, 768:784] = (ss_[:, None] == np.arange(16)[None, :]).astype(np.float32)
    cf[:, 784:912] = 1.0
    cb[:, 0:128] = eye
    cb[:, 128:256] = 1.0 / 1024
    cb[:, 256:384] = 1.0 / 512
    bd = np.zeros((128, 128), np.float32)
    bd[0:64, 0:64] = 1.0 / 64
    bd[64:128, 64:128] = 1.0 / 64
    cb[:, 384:512] = bd
    cb[:, 512:640] = np.where(idx[None, :] >= idx[:, None], 0.0, NEG)
    cb[:, 640:768] = np.where(idx[:, None] > idx[None, :], 0.0, NEG)
    cb[:, 768:896] = np.where(same & (tt_[:, None] <= tt_[None, :]), 0.0, NEG)
    cb[:, 896:1024] = np.where(idx[:, None] > tt_[None, :], 0.0, NEG)
    cb[:, 1024:1152] = tri
    cb[:, 1152:1280] = tri_s
    cb[:, 1280:1408] = 1.0
    return cf, cb


def host_vecs(inp):
    off, nv = vec_layout()
    v = np.zeros((128, nv), np.float32)

    def put(name, arr):
        arr = np.asarray(arr, np.float32)
        v[:, off[name]:off[name] + arr.shape[1]] = arr

    def chunks(x):
        return np.asarray(x).reshape(-1, 128).T

    def taps(w):
        K, C = w.shape
        return np.asarray(w).reshape(K, C // 128, 128).transpose(2, 1, 0).reshape(128, -1)

    def pair(x, n):
        x = np.asarray(x)
        return np.concatenate([np.repeat(x[0::2][None, :], 64, 0), np.repeat(x[1::2][None, :], 64, 0)], 0)

    for i in range(2):
        put("nme%d" % i, chunks(inp["norm_mix_e"][i]))
        put("nmo%d" % i, chunks(inp["norm_mix_o"][i]))
        put("caw%d" % i, taps(inp["conv_a_w"][i]))
        put("cab%d" % i, chunks(inp["conv_a_b"][i]))
        put("lng%d" % i, chunks(inp["ln_a_g"][i]))
        put("lnb%d" % i, chunks(inp["ln_a_b"][i]))
        put("qg%d" % i, np.tile(np.asarray(inp["q_norm_g"][i]), 2)[:, None])
        put("kg%d" % i, np.tile(np.asarray(inp["k_norm_g"][i]), 2)[:, None])
        sk = np.asarray(inp["sinks"][i])
        put("snk%d" % i, np.concatenate([np.repeat(sk[None, 0:4], 64, 0), np.repeat(sk[None, 4:8], 64, 0)], 0))
        put("ccw%d" % i, taps(inp["conv_c_w"][i]))
        put("ccb%d" % i, chunks(inp["conv_c_b"][i]))
        dtb = np.zeros((128, 1), np.float32)
        dtb[0:32, 0] = inp["dt_bias"][i]
        dtb[32:64, 0] = inp["dt_bias"][i]
        put("dtb%d" % i, dtb)
        al = np.zeros((128, 1), np.float32)
        al[32:64, 0] = inp["a_log"][i]
        put("alog%d" % i, al)
        put("dsk%d" % i, pair(inp["d_skip"][i], 16))
        put("gn%d" % i, chunks(inp["gnorm_c"][i]))
    for l in range(4):
        put("nf%d" % l, chunks(inp["norm_ffn"][l]))
        put("fcw%d" % l, taps(inp["ffn_conv_w"][l]))
        put("fcb%d" % l, chunks(inp["ffn_conv_b"][l]))
    invf = np.zeros((128, 1), np.float32)
    f = (500000.0 ** (-np.arange(0, 16, 2, dtype=np.float32) / 16.0)).astype(np.float32)
    for hb in (0, 64):
        invf[hb:hb + 8, 0] = f
        invf[hb + 8:hb + 16, 0] = f
    put("invf", invf)
    return v


def build_program(phases=None):
    nc = bass.Bass("TRN2", target_bir_lowering=False)
    VO, NV = vec_layout()

    def din(name, shape):
        return nc.dram_tensor(name, list(shape), F32, kind="ExternalInput").ap()

    def dout(name, shape):
        return nc.dram_tensor(name, list(shape), F32, kind="ExternalOutput").ap()

    xp_d = din("xp", [TP, D])
    xs_d = din("xs", [TS, D])
    pos_d = din("posrow", [1, T])
    cf_d = din("cf", [128, NCF])
    cb_d = din("cb", [128, NCB])
    vec_d = din("vecs", [128, NV])
    st_ca = din("st_ca", [2, 480, 512])
    st_k = din("st_k", [2, NSEQ, 128, 128])
    st_v = din("st_v", [2, NSEQ, 128, 128])
    st_cc = din("st_cc", [2, 48, 3072])
    st_ssm = din("st_ssm", [2, NSEQ, 32, 64, 128])
    st_ff = din("st_ff", [4, 32, DFF])
    w_in_e = din("w_in_e", [2, D, 1792])
    w_out_e = din("w_out_e", [2, D, D])
    w_in_o = din("w_in_o", [2, D, 5152])
    w_dt2 = din("w_dt2", [2, D, 64])
    w_out_o = din("w_out_o", [2, 2048, D])
    w_gate = din("w_gate", [4, D, DFF])
    w_up = din("w_up", [4, D, DFF])
    w_down = din("w_down", [4, DFF, D])

    y_p = dout("y_p", [TP, D])
    y_s = dout("y_s", [TS, D])
    ca_p = dout("ca_p", [2, 30, 512])
    ca_s = dout("ca_s", [2, NSEQ, 30, 512])
    wk_p = dout("wk_p", [2, 128, 128])
    wk_s = dout("wk_s", [2, NSEQ, 128, 128])
    wv_p = dout("wv_p", [2, 128, 128])
    wv_s = dout("wv_s", [2, NSEQ, 128, 128])
    cc_p = dout("cc_p", [2, 3, 3072])
    cc_s = dout("cc_s", [2, NSEQ, 3, 3072])
    ss_p = dout("ss_p", [2, 32, 64, 128])
    ss_s = dout("ss_s", [2, NSEQ, 32, 64, 128])
    ff_p = dout("ff_p", [4, 2, DFF])
    ff_s = dout("ff_s", [4, NSEQ, 2, DFF])

    st = contextlib.ExitStack()
    S = Sched(nc)

    def sbt(name, shape, dt=F32):
        return st.enter_context(nc.sbuf_tensor(name, shape, dt))

    Xt = sbt("X", [128, 8, T])
    XNt = sbt("XN", [128, 8, T], BF16)
    CFt = sbt("CFt", [128, NCF])
    CBt = sbt("CBt", [128, NCB], BF16)
    VEC = sbt("VEC", [128, NV])
    MISC = sbt("MISC", [128, 16])
    ARt = sbt("ARENA", [128, ARENA_WORDS])
    AR = Arena(ARt, ARENA_WORDS)
    PB = [st.enter_context(nc.psum_tensor("pb%d" % i, [128, 512], F32)) for i in range(8)]
    PR = [Res("pb%d" % i, excl=True) for i in range(8)]
    pstate = [0]
    held = set()

    def bank(hold=False):
        while True:
            i = pstate[0] % 8
            pstate[0] += 1
            if i not in held:
                break
        if hold:
            held.add(i)
        return PB[i], PR[i]

    def release(pb):
        for i in range(8):
            if PB[i] is pb:
                held.discard(i)

    SEM = {}

    def sh(name):
        if name not in SEM:
            SEM[name] = Res("sem_" + name)
        return SEM[name]

    def scol(ap2d, b):
        return ap2d.rearrange("p (t s) -> p t s", s=NSEQ)[:, :, b]

    rX = [[Res("X%d_%d" % (c, j)) for j in range(5)] for c in range(8)]
    rXN = [Res("XN%d" % c) for c in range(8)]
    rCF, rCB, rVEC, rMISC = Res("cf"), Res("cb"), Res("vec"), Res("misc")
    TT = [(0, 512), (512, 512), (1024, 512), (1536, 512), (2048, 128)]

    def xres(c, c0, w):
        return [rX[c][j] for j in range(5) if not (TT[j][0] >= c0 + w or TT[j][0] + TT[j][1] <= c0)]

    def cf(name, n=128, rows=128):
        return CFt[0:rows, CF[name]:CF[name] + n]

    def cb(name, n=128, rows=128):
        return CBt[0:rows, CB[name]:CB[name] + n]

    def vcol(name, j=0, n=1, p0=0, p1=128):
        return VEC[p0:p1, VO[name] + j:VO[name] + j + n]

    def mm(out, lhsT, rhs, start, stop, R, W):
        S.op("pe", lambda e: e.matmul(out, lhsT=lhsT, rhs=rhs, start=start, stop=stop), reads=R, writes=W)

    def tr(out, in_, k, R, W):
        S.op("pe", lambda e: e.transpose(out, in_, CFt[0:k, 0:k]), reads=list(R) + [rCF], writes=W)

    def act(out, in_, func, R, W, bias=None, scale=None):
        kw = {}
        if bias is not None:
            kw["bias"] = bias
        if scale is not None:
            kw["scale"] = scale
        S.op("act", lambda e: e.activation(out=out, in_=in_, func=func, **kw), reads=R, writes=W)

    def cp(eng, out, in_, R, W):
        if eng == "act":
            S.op("act", lambda e: e.copy(out=out, in_=in_), reads=R, writes=W)
        else:
            S.op(eng, lambda e: e.tensor_copy(out=out, in_=in_), reads=R, writes=W)

    def tt_(eng, out, in0, in1, op, R, W):
        S.op(eng, lambda e: e.tensor_tensor(out=out, in0=in0, in1=in1, op=op), reads=R, writes=W)

    def ts_(eng, out, in0, s1, op0, R, W, s2=None, op1=None):
        if op1 is None:
            S.op(eng, lambda e: e.tensor_scalar(out=out, in0=in0, scalar1=s1, scalar2=None, op0=op0), reads=R, writes=W)
        else:
            S.op(eng, lambda e: e.tensor_scalar(out=out, in0=in0, scalar1=s1, scalar2=s2, op0=op0, op1=op1),
                 reads=R, writes=W)

    def stt(out, in0, scalar, in1, op0, op1, R, W):
        S.op("dve", lambda e: e.scalar_tensor_tensor(out=out, in0=in0, scalar=scalar, in1=in1, op0=op0, op1=op1),
             reads=R, writes=W)

    def dma(eng, out, in_, R, W, sem):
        S.op(eng, lambda e: e.dma_start(out=out, in_=in_), reads=R, writes=W, dma=sh(sem))

    def memset(eng, ap, val, W):
        S.op(eng, lambda e: e.memset(ap, val), writes=W)

    def rstd_from(out, psum_ap, R, W):
        act(out, psum_ap, AF.Sqrt, R + [rMISC], W, bias=MISC[:, 0:1], scale=1.0)
        S.op("dve", lambda e: e.reciprocal(out=out, in_=out), reads=W, writes=W)

    def wload(dst3, src2, res, sem):
        dma("pool", dst3, src2.rearrange("(k p) n -> p k n", p=128), [], [res], sem)

    dma("sp", CFt[:], cf_d, [], [rCF], "cf")
    dma("pool", CBt[:], cb_d, [], [rCB], "cb")
    dma("sp", VEC[:], vec_d, [], [rVEC], "vec")
    memset("pool", MISC[:], 0.0, [rMISC])
    memset("pool", MISC[:, 0:1], EPS, [rMISC])
    for i in range(2):
        act(MISC[32:64, 10 + i:11 + i], vcol("alog%d" % i, p0=32, p1=64), AF.Exp, [rVEC, rMISC], [rMISC])
        ts_("pool", MISC[32:64, 10 + i:11 + i], MISC[32:64, 10 + i:11 + i], -1.0, ALU.mult, [rMISC], [rMISC])
        act(MISC[:, 2 + 4 * i:6 + 4 * i], vcol("snk%d" % i, n=4), AF.Exp, [rVEC, rMISC], [rMISC])

    AR.reset()
    xin = [AR.f32(1024, "xin%d" % k) for k in range(2)]
    for rt in range(17):
        buf, rb = xin[rt % 2]
        src = xp_d[rt * 128:(rt + 1) * 128, :] if rt < 16 else xs_d
        dma("sp", buf, src, [], [rb], "xin%d" % (rt % 2))
        for half in range(2):
            pb, pr = bank()
            for q in range(4):
                c = half * 4 + q
                tr(pb[:, q * 128:(q + 1) * 128], buf[:, c * 128:(c + 1) * 128], 128, [rb], [pr])
            if rt < 16:
                dst = Xt[:, half * 4:half * 4 + 4, rt * 128:(rt + 1) * 128]
                src_ps = pb[:].rearrange("p (c t) -> p c t", c=4)
            else:
                dst = Xt[:, half * 4:half * 4 + 4, TP:T].rearrange("p c (t s) -> p c t s", s=NSEQ)
                src_ps = pb[:].rearrange("p (c s t) -> p c t s", c=4, s=NSEQ)
            W = [rX[half * 4 + q][min(rt // 4, 4)] for q in range(4)]
            cp("act" if half == 0 else "dve", dst, src_ps, [pr], W)

    def norm_phase(gname):
        S.barrier()
        AR.reset()
        sq = [AR.bf(T, "sq%d" % k) for k in range(2)]
        rs, rrs = AR.f32(T, "rs")
        banks = [bank(hold=True) for _ in range(5)]
        for c in range(8):
            b, rb = sq[c % 2]
            if c % 2 == 0:
                act(b, Xt[:, c, :], AF.Square, rX[c], [rb])
            else:
                tt_("pool", b, Xt[:, c, :], Xt[:, c, :], ALU.mult, rX[c], [rb])
            for j, (c0, w) in enumerate(TT):
                pb, pr = banks[j]
                mm(pb[:, 0:w], cb("o1024"), b[:, c0:c0 + w], c == 0, c == 7, [rCB, rb], [pr])
        for j, (c0, w) in enumerate(TT):
            pb, pr = banks[j]
            rstd_from(rs[:, c0:c0 + w], pb[:, 0:w], [pr], [rrs])
            release(pb)
        for c in range(8):
            stt(XNt[:, c, :], Xt[:, c, :], vcol(gname, c), rs, ALU.mult, ALU.mult, rX[c] + [rVEC, rrs], [rXN[c]])

    def resid_add(n, c0, w, pb, pr):
        tt_("dve", Xt[:, n, c0:c0 + w], Xt[:, n, c0:c0 + w], pb[:, 0:w], ALU.add, xres(n, c0, w) + [pr], xres(n, c0, w))

    def ffn_phase(l):
        norm_phase("nf%d" % l)
        S.barrier()
        AR.reset()
        GW = 2210
        WG = [AR.bf(8 * 256, "wg%d" % k) for k in range(2)]
        WU = [AR.bf(8 * 256, "wu%d" % k) for k in range(2)]
        WD = [AR.bf(2 * 1024, "wd%d" % k) for k in range(2)]
        HS = [AR.f32(256, "hs%d" % k) for k in range(2)]
        G = [AR.f32(GW, "g%d" % k) for k in range(2)]
        U = [AR.f32(T, "u%d" % k) for k in range(2)]
        ACC, rACC = AR.f32(T, "acc")
        AT = [AR.bf(2 * T, "at%d" % k) for k in range(2)]
        SP_, rSP = AR.f32(256, "stgp")
        SS_, rSS = AR.f32(256, "stgs")
        for k in range(2):
            memset("pool", G[k][0][:, 0:2], 0.0, [G[k][1]])

        def loadw(g):
            s = g % 2
            wload(WG[s][0].rearrange("p (k n) -> p k n", k=8), w_gate[l][:, g * 256:(g + 1) * 256], WG[s][1], "wg%d" % s)
            wload(WU[s][0].rearrange("p (k n) -> p k n", k=8), w_up[l][:, g * 256:(g + 1) * 256], WU[s][1], "wu%d" % s)
            dma("sp", HS[s][0][0:32, :], st_ff[l][:, g * 256:(g + 1) * 256], [], [HS[s][1]], "hs%d" % s)

        def loadwd(g):
            s = g % 2
            wload(WD[s][0].rearrange("p (k n) -> p k n", k=2), w_down[l][g * 256:(g + 1) * 256, :], WD[s][1], "wd%d" % s)

        def down(g):
            s = g % 2
            wd3 = WD[s][0].rearrange("p (k n) -> p k n", k=2)
            at3 = AT[s][0].rearrange("p (j t) -> p j t", j=2)
            for n in range(8):
                for (c0, w) in TT:
                    pb, pr = bank()
                    for jj in range(2):
                        mm(pb[:, 0:w], wd3[:, jj, n * 128:(n + 1) * 128], at3[:, jj, c0:c0 + w], jj == 0, jj == 1,
                           [WD[s][1], AT[s][1]], [pr])
                    resid_add(n, c0, w, pb, pr)

        loadw(0)
        loadwd(0)
        for g in range(NJ // 2):
            s = g % 2
            if g + 1 < NJ // 2:
                loadw(g + 1)
            wg3 = WG[s][0].rearrange("p (k n) -> p k n", k=8)
            wu3 = WU[s][0].rearrange("p (k n) -> p k n", k=8)
            wd3 = WD[s][0].rearrange("p (k n) -> p k n", k=2)
            at3 = AT[s][0].rearrange("p (j t) -> p j t", j=2)
            for jj in range(2):
                j = 2 * g + jj
                gb, rg = G[j % 2]
                ub, ru = U[j % 2]
                pb, pr = bank()
                tr(pb[:, 0:32], HS[s][0][0:32, jj * 128:(jj + 1) * 128], 32, [HS[s][1]], [pr])
                cp("dve", gb[:, 2050:2082].rearrange("p (k s) -> p k s", s=NSEQ),
                   pb[:, 0:32].rearrange("p (s k) -> p k s", k=2), [pr], [rg])
                for ti, (c0, w) in enumerate(TT):
                    pb, pr = bank()
                    for k in range(8):
                        mm(pb[:, 0:w], wg3[:, k, jj * 128:(jj + 1) * 128], XNt[:, k, c0:c0 + w], k == 0, k == 7,
                           [WG[s][1], rXN[k]], [pr])
                    dst = gb[:, 2 + c0:2 + c0 + w] if ti < 4 else gb[:, 2082:2210]
                    cp("act", dst, pb[:, 0:w], [pr], [rg])
                for ti, (c0, w) in enumerate(TT):
                    pb, pr = bank()
                    for k in range(8):
                        mm(pb[:, 0:w], wu3[:, k, jj * 128:(jj + 1) * 128], XNt[:, k, c0:c0 + w], k == 0, k == 7,
                           [WU[s][1], rXN[k]], [pr])
                    cp("dve", ub[:, c0:c0 + w], pb[:, 0:w], [pr], [ru])
                w0, w1, w2 = (vcol("fcw%d" % l, j * 3 + k) for k in range(3))
                bcol = vcol("fcb%d" % l, j)
                act(ACC[:, 0:TP], gb[:, 0:TP], AF.Identity, [rg, rVEC], [rACC], bias=bcol, scale=w0)
                act(ACC[:, TP:T], gb[:, 2050:2178], AF.Identity, [rg, rVEC], [rACC], bias=bcol, scale=w0)
                stt(ACC[:, 0:TP], gb[:, 1:1 + TP], w1, ACC[:, 0:TP], ALU.mult, ALU.add, [rg, rVEC, rACC], [rACC])
                stt(ACC[:, TP:T], gb[:, 2066:2194], w1, ACC[:, TP:T], ALU.mult, ALU.add, [rg, rVEC, rACC], [rACC])
                stt(ACC[:, 0:TP], gb[:, 2:2 + TP], w2, ACC[:, 0:TP], ALU.mult, ALU.add, [rg, rVEC, rACC], [rACC])
                stt(ACC[:, TP:T], gb[:, 2082:2210], w2, ACC[:, TP:T], ALU.mult, ALU.add, [rg, rVEC, rACC], [rACC])
                act(ACC, ACC, AF.Silu, [rACC], [rACC])
                tt_("pool", at3[:, jj, :], ACC, ub, ALU.mult, [rACC, ru], [AT[s][1]])
                pb, pr = bank()
                tr(pb[0:2, 0:128], gb[:, 2048:2050], 128, [rg], [pr])
                cp("act", SP_[0:2, jj * 128:(jj + 1) * 128], pb[0:2, 0:128], [pr], [rSP])
                pb, pr = bank()
                tr(pb[0:32, 0:128], gb[:, 2178:2210], 128, [rg], [pr])
                cp("act", SS_[0:32, jj * 128:(jj + 1) * 128], pb[0:32, 0:128], [pr], [rSS])
            dma("sp", ff_p[l][:, g * 256:(g + 1) * 256], SP_[0:2, :], [rSP], [], "sp")
            dma("sp", ff_s[l][:, :, g * 256:(g + 1) * 256].rearrange("s t d -> t s d"), SS_[0:32, :], [rSS], [], "ss")
            if g > 0:
                down(g - 1)
            if g + 1 < NJ // 2:
                loadwd(g + 1)
        down(NJ // 2 - 1)

    def even_phase(i, stop=None):
        norm_phase("nme%d" % i)
        S.barrier()
        AR.reset()
        UW = 30 + TP + 480 + TS
        UWW = (UW + 1) // 2
        WA = [AR.bf(8 * 256, "wa%d" % k) for k in range(2)]
        WO, rWO = AR.bf(4 * 1024, "woA")
        RG, rUP = AR.f32(UWW + 2048 + 1024, "up_hst_diag_co")
        UP = RG[:, 0:UWW].bitcast(BF16)[:, 0:UW]
        HST = [RG[:, UWW + 512 * k:UWW + 512 * (k + 1)] for k in range(4)]
        DIAG = RG[:, UWW + 2048:UWW + 3072].bitcast(BF16).rearrange("p (k n) -> p k n", k=16)
        CO = [RG[:, 1088 * c:1088 * (c + 1)].bitcast(BF16) for c in range(4)]
        UF, rUF = AR.f32(160, "uf")
        SIG = [AR.f32(512, "sig%d" % k) for k in range(1)] * 2
        CV = [AR.f32(T, "cv%d" % c) for c in range(4)]
        MU, rMU = AR.f32(T, "mu")
        CVB = [AR.bf(T, "cvb%d" % k) for k in range(2)]
        STG, rSTG = AR.f32(512, "stgA")
        STS, rSTS = AR.f32(512, "stsA")
        memset("pool", UP[:, 0:30], 0.0, [rUP])
        for k in range(4):
            rows = 128 if k < 3 else 96
            dma("sp", HST[k][0:rows, :], st_ca[i][k * 128:k * 128 + rows, :], [], [rUP], "up")
        dma("pool", WO.rearrange("p (k n) -> p k n", k=4), w_out_e[i][0:512, :].rearrange("(k p) n -> p k n", p=128),
            [], [rWO], "woA")
        dma("sp", ca_s[i][:, 0:22, :], st_ca[i].rearrange("(s k) c -> s k c", k=30)[:, 8:30, :], [], [], "dd")

        def loadA(c):
            s = c % 2
            w3 = WA[s][0].rearrange("p (k n) -> p k n", k=8)
            dma("pool", w3[:, :, 0:128], w_in_e[i][:, c * 128:(c + 1) * 128].rearrange("(k p) n -> p k n", p=128),
                [], [WA[s][1]], "wa%d" % s)
            dma("pool", w3[:, :, 128:256],
                w_in_e[i][:, 512 + c * 128:512 + (c + 1) * 128].rearrange("(k p) n -> p k n", p=128),
                [], [WA[s][1]], "wa%d" % s)

        loadA(0)
        SB = 30 + TP
        for c in range(4):
            s = c % 2
            if c < 3:
                loadA(c + 1)
            w3 = WA[s][0].rearrange("p (k n) -> p k n", k=8)
            hdst = UP[:, SB:SB + 480].rearrange("p (k s) -> p s k", s=NSEQ)
            for k in range(4):
                rows = 128 if k < 3 else 96
                pb, pr = bank()
                tr(pb[:, 0:rows], HST[k][0:rows, c * 128:(c + 1) * 128], rows, [rUP], [pr])
                r0 = k * 128
                r = r0
                while r < r0 + rows:
                    sq_, kk = divmod(r, 30)
                    n = min(30 - kk, r0 + rows - r)
                    cp("dve", hdst[:, sq_, kk:kk + n], pb[:, r - r0:r - r0 + n], [pr], [rUP])
                    r += n
            for ti, (c0, w) in enumerate(TT):
                pv, prv = bank()
                pg, prg = bank()
                for k in range(8):
                    mm(pv[:, 0:w], w3[:, k, 0:128], XNt[:, k, c0:c0 + w], k == 0, k == 7, [WA[s][1], rXN[k]], [prv])
                for k in range(8):
                    mm(pg[:, 0:w], w3[:, k, 128:256], XNt[:, k, c0:c0 + w], k == 0, k == 7, [WA[s][1], rXN[k]], [prg])
                sg, rsg = SIG[ti % 2]
                act(sg[:, 0:w], pg[:, 0:w], AF.Sigmoid, [prg], [rsg])
                dst = UP[:, 30 + c0:30 + c0 + w] if ti < 4 else UP[:, SB + 480:UW]
                tt_("dve", dst, pv[:, 0:w], sg[:, 0:w], ALU.mult, [prv, rsg], [rUP])
                if ti == 3:
                    tt_("dve", UF[:, 0:30], pv[:, 482:512], sg[:, 482:512], ALU.mult, [prv, rsg], [rUF])
                if ti == 4:
                    tt_("dve", UF[:, 32:160], pv[:, 0:128], sg[:, 0:128], ALU.mult, [prv, rsg], [rUF])
            pb, pr = bank()
            tr(pb[0:30, 0:128], UF[:, 0:30], 128, [rUF], [pr])
            cp("act", STG[0:30, c * 128:(c + 1) * 128], pb[0:30, 0:128], [pr], [rSTG])
            pb, pr = bank()
            tr(pb[:, 0:128], UF[:, 32:160], 128, [rUF], [pr])
            cp("act", STS[:, c * 128:(c + 1) * 128], pb[:, 0:128], [pr], [rSTS])
            cv, rcv = CV[c]
            bcol = vcol("cab%d" % i, c)
            banks = [bank(hold=True) for _ in range(5)]
            for half in range(2):
                taps = list(range(16 * half, min(16 * half + 16, 31)))
                for j, k in enumerate(taps):
                    ts_("dve", DIAG[:, j, :], cb("ident"), vcol("caw%d" % i, c * 31 + k), ALU.mult, [rCB, rVEC], [rUP])
                for ti, (c0, w) in enumerate(TT):
                    pb, pr = banks[ti]
                    for j, k in enumerate(taps):
                        src = UP[:, k + c0:k + c0 + w] if ti < 4 else UP[:, SB + 16 * k:SB + 16 * k + 128]
                        mm(pb[:, 0:w], DIAG[:, j, :], src, k == 0, k == 30, [rUP], [pr])
            for ti, (c0, w) in enumerate(TT):
                pb, pr = banks[ti]
                act(cv[:, c0:c0 + w], pb[:, 0:w], AF.Identity, [pr, rVEC], [rcv], bias=bcol, scale=1.0)
                release(pb)
        if stop == "A0":
            return
        dma("sp", ca_p[i], STG[0:30, :], [rSTG], [], "stg")
        dma("sp", ca_s[i][:, 22:30, :].rearrange("s t d -> t s d"), STS[:, :], [rSTS], [], "sts")
        if stop == "A1":
            return
        banks = [bank(hold=True) for _ in range(5)]
        for c in range(4):
            b, rb = CVB[c % 2]
            cp("pool", b, CV[c][0], [CV[c][1]], [rb])
            for j, (c0, w) in enumerate(TT):
                mm(banks[j][0][:, 0:w], cb("o512"), b[:, c0:c0 + w], c == 0, c == 3, [rCB, rb], [banks[j][1]])
        for j, (c0, w) in enumerate(TT):
            cp("act", MU[:, c0:c0 + w], banks[j][0][:, 0:w], [banks[j][1]], [rMU])
            release(banks[j][0])
        for c in range(4):
            tt_("pool", CV[c][0], CV[c][0], MU, ALU.subtract, [CV[c][1], rMU], [CV[c][1]])
        banks = [bank(hold=True) for _ in range(5)]
        for c in range(4):
            b, rb = CVB[c % 2]
            act(b, CV[c][0], AF.Square, [CV[c][1]], [rb])
            for j, (c0, w) in enumerate(TT):
                mm(banks[j][0][:, 0:w], cb("o512"), b[:, c0:c0 + w], c == 0, c == 3, [rCB, rb], [banks[j][1]])
        for j, (c0, w) in enumerate(TT):
            rstd_from(MU[:, c0:c0 + w], banks[j][0][:, 0:w], [banks[j][1]], [rMU])
            release(banks[j][0])
        for c in range(4):
            tt_("dve", CV[c][0], CV[c][0], MU, ALU.mult, [CV[c][1], rMU], [CV[c][1]])
            act(CO[c], CV[c][0], AF.Silu, [CV[c][1], rVEC], [rUP], bias=vcol("lnb%d" % i, c),
                scale=vcol("lng%d" % i, c))
        wo3 = WO.rearrange("p (k n) -> p k n", k=4)
        for n in range(8):
            for (c0, w) in TT:
                pb, pr = bank()
                for k in range(4):
                    mm(pb[:, 0:w], wo3[:, k, n * 128:(n + 1) * 128], CO[k][:, c0:c0 + w], k == 0, k == 3,
                       [rWO, rUP], [pr])
                resid_add(n, c0, w, pb, pr)

        if stop == "A":
            return
        S.barrier()
        AR.reset()
        WQ = [AR.bf(8 * 128, "wq%d" % k) for k in range(2)]
        WO2, rWO2 = AR.bf(4 * 1024, "woB")
        COS, rCOS = AR.f32(T, "cos")
        SIN, rSIN = AR.f32(T, "sin")
        OT = [COS[:, 0:1088].bitcast(BF16), COS[:, 1088:2176].bitcast(BF16),
              SIN[:, 0:1088].bitcast(BF16), SIN[:, 1088:2176].bitcast(BF16)]
        rOT = [rCOS, rCOS, rSIN, rSIN]
        QR = [AR.bf(T, "qr%d" % c) for c in range(4)]
        KR, rKR = AR.bf(T, "kr")
        KF, rKF = AR.f32(256, "kf")
        VB, rVB = AR.bf(17 * 128, "vb")
        VF, rVF = AR.f32(256, "vf")
        TMP = [AR.f32(512, "tmpb%d" % k) for k in range(6)]
        TI, rTI = AR.f32(512, "ti")
        SQB, rSQB = AR.bf(512, "sqb")
        QNB, rQNB = AR.f32(512, "qnb")
        EB = [AR.bf(256, "eb%d" % k) for k in range(2)]
        REC = [AR.f32(128, "rec%d" % k) for k in range(2)]
        KHS = [AR.f32(128, "khs%d" % k) for k in range(2)]
        KHT, rKHT = AR.bf(NSEQ * 128, "kht")
        VH, rVH = AR.bf(NSEQ * 128, "vh")
        OA, rOA = AR.f32(128, "oa")
        KO, rKO = AR.f32(256, "ko")
        dma("pool", WO2.rearrange("p (k n) -> p k n", k=4), w_out_e[i][512:1024, :].rearrange("(k p) n -> p k n", p=128),
            [], [rWO2], "woB")
        vh3 = VH.rearrange("p (s d) -> p s d", s=NSEQ)
        dma("pool", vh3, st_v[i].rearrange("s k d -> k s d"), [], [rVH], "vh")
        dma("sp", wk_s[i][:, 0:120, :], st_k[i][:, 8:128, :], [], [], "ddk")
        dma("sp", wv_s[i][:, 0:120, :], st_v[i][:, 8:128, :], [], [], "ddk")
        ti_i = TI.bitcast(I32)
        for (c0, w) in TT:
            pz, rpz = TMP[1]
            dma("sp", pz[:, 0:w], pos_d[:, c0:c0 + w].partition_broadcast(128), [], [rpz], "pos")
            for (dst, rdst, ph) in ((SIN, rSIN, 0.0), (COS, rCOS, math.pi / 2)):
                a, ra = TMP[0]
                c_, rc_ = TMP[2]
                ts_("dve", a[:, 0:w], pz[:, 0:w], vcol("invf"), ALU.mult, [rpz, rVEC], [ra], s2=ph, op1=ALU.add)
                ts_("dve", c_[:, 0:w], a[:, 0:w], 1.0 / (2 * math.pi), ALU.mult, [ra], [rc_])
                cp("dve", ti_i[:, 0:w], c_[:, 0:w], [rc_], [rTI])
                cp("dve", c_[:, 0:w], ti_i[:, 0:w], [rTI], [rc_])
                stt(a[:, 0:w], c_[:, 0:w], -2 * math.pi, a[:, 0:w], ALU.mult, ALU.add, [rc_, ra], [ra])
                ts_("dve", c_[:, 0:w], a[:, 0:w], math.pi, ALU.is_gt, [ra], [rc_], s2=-2 * math.pi, op1=ALU.mult)
                tt_("dve", a[:, 0:w], a[:, 0:w], c_[:, 0:w], ALU.add, [ra, rc_], [ra])
                act(dst[:, c0:c0 + w], a[:, 0:w], AF.Sin, [ra], [rdst])
        if stop == "B0":
            return
        kht3 = KHT.rearrange("p (s k) -> p s k", s=NSEQ)
        for sq_ in range(NSEQ):
            hb_, rhb = KHS[sq_ % 2]
            dma("sp", hb_, st_k[i][sq_], [], [rhb], "khs%d" % (sq_ % 2))
            pb, pr = bank()
            tr(pb[:, 0:128], hb_, 128, [rhb], [pr])
            cp("act", kht3[:, sq_, :], pb[:, 0:128], [pr], [rKHT])

        def loadQ(idx):
            s = idx % 2
            col = 1024 + idx * 128
            wload(WQ[s][0].rearrange("p (k n) -> p k n", k=8), w_in_e[i][:, col:col + 128], WQ[s][1], "wq%d" % s)

        loadQ(0)
        for idx in range(5):
            s = idx % 2
            loadQ(idx + 1)
            w3 = WQ[s][0].rearrange("p (k n) -> p k n", k=8)
            gcol = vcol("qg%d" % i) if idx < 4 else vcol("kg%d" % i)
            for ti, (c0, w) in enumerate(TT):
                pb, pr = bank()
                for k in range(8):
                    mm(pb[:, 0:w], w3[:, k, :], XNt[:, k, c0:c0 + w], k == 0, k == 7, [WQ[s][1], rXN[k]], [pr])
                act(SQB[:, 0:w], pb[:, 0:w], AF.Square, [pr], [rSQB])
                p2, pr2 = bank()
                mm(p2[:, 0:w], cb("bd64"), SQB[:, 0:w], True, True, [rCB, rSQB], [pr2])
                r_, rr_ = TMP[3]
                rstd_from(r_[:, 0:w], p2[:, 0:w], [pr2], [rr_])
                stt(QNB[:, 0:w], pb[:, 0:w], gcol, r_[:, 0:w], ALU.mult, ALU.mult, [pr, rVEC, rr_], [rQNB])
                p3, pr3 = bank()
                mm(p3[:, 0:w], cf("pmat"), QNB[:, 0:w], True, True, [rCF, rQNB], [pr3])
                t1, rt1 = TMP[4]
                t2, rt2 = TMP[5]
                tt_("dve", t1[:, 0:w], p3[:, 0:w], SIN[:, c0:c0 + w], ALU.mult, [pr3, rSIN], [rt1])
                tt_("pool", t2[:, 0:w], QNB[:, 0:w], COS[:, c0:c0 + w], ALU.mult, [rQNB, rCOS], [rt2])
                if idx < 4:
                    tt_("pool", QR[idx][0][:, c0:c0 + w], t1[:, 0:w], t2[:, 0:w], ALU.add, [rt1, rt2], [QR[idx][1]])
                else:
                    tt_("pool", KR[:, c0:c0 + w], t1[:, 0:w], t2[:, 0:w], ALU.add, [rt1, rt2], [rKR])
                    if ti == 3:
                        tt_("pool", KF[:, 0:128], t1[:, 384:512], t2[:, 384:512], ALU.add, [rt1, rt2], [rKF])
                    if ti == 4:
                        tt_("pool", KF[:, 128:256], t1[:, 0:128], t2[:, 0:128], ALU.add, [rt1, rt2], [rKF])
        if stop == "B1":
            return
        pb, pr = bank()
        tr(pb[:, 0:128], KF[:, 0:128], 128, [rKF], [pr])
        tr(pb[:, 128:256], KF[:, 128:256], 128, [rKF], [pr])
        cp("act", KO, pb[:, 0:256], [pr], [rKO])
        dma("sp", wk_p[i], KO[:, 0:128], [rKO], [], "ko")
        dma("sp", wk_s[i][:, 120:128, :].rearrange("s t d -> t s d"), KO[:, 128:256], [rKO], [], "ko")
        if stop == "B15":
            return
        s = 5 % 2
        w3 = WQ[s][0].rearrange("p (k n) -> p k n", k=8)
        vb3 = VB.rearrange("p (b d) -> p b d", b=17)
        for blk in range(17):
            pb, pr = bank()
            for k in range(8):
                mm(pb[:, 0:128], XNt[:, k, blk * 128:(blk + 1) * 128], w3[:, k, :], k == 0, k == 7, [WQ[s][1], rXN[k]], [pr])
            cp("act", vb3[:, blk, :], pb[:, 0:128], [pr], [rVB])
            if blk == 15:
                cp("dve", VF[:, 0:128], pb[:, 0:128], [pr], [rVF])
            if blk == 16:
                cp("dve", VF[:, 128:256], pb[:, 0:128], [pr], [rVF])
        dma("sp", wv_p[i], VF[:, 0:128], [rVF], [], "vf")
        dma("sp", wv_s[i][:, 120:128, :].rearrange("s t d -> t s d"), VF[:, 128:256], [rVF], [], "vf")

        if stop == "B2":
            return
        def attend(blk):
            c0 = blk * 128
            for c in range(4):
                for h2 in range(2):
                    ps_, prs = bank()
                    lo, hi = 64 * h2, 64 * h2 + 64
                    qv = QR[c][0][lo:hi, c0:c0 + 128]
                    mm(ps_[:, 128:256], KR[lo:hi, c0:c0 + 128], qv, True, False, [rKR, QR[c][1]], [prs])
                    mm(ps_[:, 128:256], cb("ident"), cb("mcur") if blk < 16 else cb("msnew"), False, True, [rCB], [prs])
                    if 0 < blk < 16:
                        mm(ps_[:, 0:128], KR[lo:hi, c0 - 128:c0], qv, True, False, [rKR, QR[c][1]], [prs])
                        mm(ps_[:, 0:128], cb("ident"), cb("mprev"), False, True, [rCB], [prs])
                    if blk == 16:
                        for sq_ in range(NSEQ):
                            mm(scol(ps_[:, 0:128], sq_), kht3[lo:hi, sq_, :], scol(QR[c][0][lo:hi, c0:c0 + 128], sq_),
                               True, True, [rKHT, QR[c][1]], [prs])
                    e, re = EB[(c * 2 + h2) % 2]
                    if blk < 16:
                        first = 128 if blk == 0 else 0
                        act(e[:, first:256], ps_[:, first:256], AF.Exp, [prs], [re], scale=0.125)
                    else:
                        act(e[:, 128:256], ps_[:, 128:256], AF.Exp, [prs], [re], scale=0.125)
                        t1, rt1 = TMP[4]
                        tt_("dve", t1[:, 0:128], ps_[:, 0:128], cb("mshist"), ALU.add, [prs, rCB], [rt1])
                        act(e[:, 0:128], t1[:, 0:128], AF.Exp, [rt1], [re], scale=0.125)
                    po, pro = bank()
                    rc, rrc = REC[(c * 2 + h2) % 2]
                    esk = MISC[lo:hi, 2 + 4 * i + c:3 + 4 * i + c]
                    if blk < 16:
                        if blk > 0:
                            mm(po[:, 0:128], vb3[:, blk - 1, :], e[:, 0:128], True, False, [rVB, re], [pro])
                        mm(po[:, 0:128], vb3[:, blk, :], e[:, 128:256], blk == 0, True, [rVB, re], [pro])
                        if blk > 0:
                            mm(po[:, 128:256], cb("ones"), e[:, 0:128], True, False, [rCB, re], [pro])
                        mm(po[:, 128:256], cb("ones"), e[:, 128:256], blk == 0, True, [rCB, re], [pro])
                        ts_("dve", rc[lo:hi, :], po[lo:hi, 128:256], esk, ALU.add, [pro, rMISC], [rrc])
                        S.op("dve", lambda e_, rc=rc, lo=lo, hi=hi: e_.reciprocal(out=rc[lo:hi, :], in_=rc[lo:hi, :]),
                             reads=[rrc], writes=[rrc])
                        tt_("dve", OT[c][lo:hi, c0:c0 + 128], po[lo:hi, 0:128], rc[lo:hi, :], ALU.mult,
                            [pro, rrc], [rOT[c]])
                    else:
                        mm(po[:, 0:128], vb3[:, 16, :], e[:, 128:256], True, True, [rVB, re], [pro])
                        mm(po[:, 128:256], cb("ones"), e[:, 128:256], True, True, [rCB, re], [pro])
                        for sq_ in range(NSEQ):
                            mm(scol(po[:, 256:384], sq_), vh3[:, sq_, :], scol(e[:, 0:128], sq_), True, True, [rVH, re], [pro])
                        mm(po[:, 384:512], cb("ones"), e[:, 0:128], True, True, [rCB, re], [pro])
                        cp("act", OA[lo:hi, :], po[lo:hi, 256:384], [pro], [rOA])
                        t2, rt2 = TMP[5]
                        cp("act", t2[lo:hi, 0:128], po[lo:hi, 384:512], [pro], [rt2])
                        stt(rc[lo:hi, :], po[lo:hi, 128:256], esk, t2[lo:hi, 0:128], ALU.add, ALU.add,
                            [pro, rMISC, rt2], [rrc])
                        S.op("dve", lambda e_, rc=rc, lo=lo, hi=hi: e_.reciprocal(out=rc[lo:hi, :], in_=rc[lo:hi, :]),
                             reads=[rrc], writes=[rrc])
                        tt_("dve", OA[lo:hi, :], OA[lo:hi, :], po[lo:hi, 0:128], ALU.add, [rOA, pro], [rOA])
                        tt_("dve", OT[c][lo:hi, c0:c0 + 128], OA[lo:hi, :], rc[lo:hi, :], ALU.mult,
                            [rOA, rrc], [rOT[c]])

        for blk in range(17):
            if stop == "B3" and blk == 16:
                return
            attend(blk)
        wo3 = WO2.rearrange("p (k n) -> p k n", k=4)
        for n in range(8):
            for (c0, w) in TT:
                pb, pr = bank()
                for k in range(4):
                    mm(pb[:, 0:w], wo3[:, k, n * 128:(n + 1) * 128], OT[k][:, c0:c0 + w], k == 0, k == 3,
                       [rWO2, rOT[k]], [pr])
                resid_add(n, c0, w, pb, pr)

    def odd_phase(i):
        norm_phase("nmo%d" % i)
        S.barrier()
        AR.reset()
        TW = 256
        TILES = [(k * TW, TW) for k in range(8)] + [(TP, TS)]
        WIN, rWIN = AR.bf(10 * 1024, "win")
        rWINt = [Res("win%d" % t) for t in range(10)]
        WOo, rWOo = AR.bf(4 * 1024, "woo")
        WDT, rWDT = AR.bf(8 * 64, "wdt")
        DTT, rDTT = AR.f32(17 * 64, "dtt")
        D2, rD2 = AR.f32(TW, "d2")
        XPAD = [AR.f32(3 + TW, "xpad%d" % k) for k in range(6)]
        ACC = [AR.f32(TW, "acco%d" % k) for k in range(1)] * 2
        XS = [[AR.bf(TW, "xs%d_%d" % (p, k)) for k in range(4)] for p in range(2)]
        BT = [AR.bf(TW, "bt%d" % p) for p in range(2)]
        CT = [AR.bf(TW, "ct%d" % p) for p in range(2)]
        ZS = [[AR.bf(TW, "zs%d_%d" % (p, k)) for k in range(4)] for p in range(2)]
        YN = [AR.bf(TW, "yn%d" % k) for k in range(4)]
        TLP, rTLP = AR.f32(6 * 3, "tlp")
        TLS, rTLS = AR.f32(6 * 48, "tls")
        STT_, rSTT = AR.f32(768, "sttail")
        XDT = [AR.bf(512, "xdt%d" % p) for p in range(2)]
        BTOK = [AR.bf(128, "btok%d" % p) for p in range(2)]
        DEND = [AR.f32(8, "dend%d" % p) for p in range(2)]
        CBM = [AR.f32(128, "cbm%d" % p) for p in range(2)]
        WT_ = [[AR.bf(512, "wt%d_%d" % (p, h)) for h in range(2)] for p in range(2)]
        CS_ = [[AR.bf(512, "cs%d_%d" % (p, h)) for h in range(2)] for p in range(2)]
        CDEC = [[AR.f32(4, "cdec%d_%d" % (p, h)) for h in range(2)] for p in range(2)]
        RR = [AR.f32(512, "rr%d" % h) for h in range(2)]
        DEC = [AR.bf(512, "dec%d" % h) for h in range(2)]
        EXPA = [AR.f32(512, "expa%d" % h) for h in range(2)]
        XDD, rXDD = AR.bf(256, "xdd")
        YV2 = [[AR.f32(128, "yv%d_%d" % (p, k)) for k in range(4)] for p in range(2)]
        SQY, rSQY = AR.bf(512, "sqy")
        RSTD, rRSTD = AR.f32(128, "rstd")
        u0 = AR.off
        HT, rHT = AR.f32(512, "ht")
        HTB, rHTB = AR.bf(512, "htb")
        TMPH, rTMPH = AR.f32(256, "tmph")
        SOUT, rSOUT = AR.f32(512, "sout")
        u1 = AR.off
        AR.off = u0
        H0 = [AR.f32(256, "h0%d" % k) for k in range(2)]
        H0T = [AR.bf(256, "h0t%d" % k) for k in range(2)]
        HFIN = [AR.f32(256, "hfin%d" % k) for k in range(3)]
        BM = [AR.bf(128, "bm%d" % k) for k in range(2)]
        CD, rCD = AR.f32(32, "cd")
        DTAX, rDTAX = AR.f32(256, "dtax")
        YOFF, rYOFF = AR.f32(512, "yoff")
        AR.off = max(AR.off, u1)
        dtt3 = DTT.rearrange("p (c h) -> p c h", c=17)

        wload(WDT.rearrange("p (k n) -> p k n", k=8), w_dt2[i], rWDT, "wdt")
        wdt3 = WDT.rearrange("p (k n) -> p k n", k=8)
        for (c0, w) in TILES:
            pb, pr = bank()
            for k in range(8):
                mm(pb[0:64, 0:w], wdt3[:, k, :], XNt[:, k, c0:c0 + w], k == 0, k == 7, [rWDT, rXN[k]], [pr])
            act(D2[0:64, 0:w], pb[0:64, 0:w], AF.Exp, [pr, rVEC], [rD2], bias=vcol("dtb%d" % i, p1=64))
            act(D2[0:64, 0:w], D2[0:64, 0:w], AF.Ln, [rD2], [rD2], bias=1.0)
            ts_("pool", D2[32:64, 0:w], D2[32:64, 0:w], MISC[32:64, 10 + i:11 + i], ALU.mult, [rD2, rMISC], [rD2])
            for q in range(w // 128):
                ch = (c0 + q * 128) // 128
                pb2, pr2 = bank()
                tr(pb2[:, 0:64], D2[0:64, q * 128:(q + 1) * 128], 64, [rD2], [pr2])
                cp("act", dtt3[:, ch, :], pb2[:, 0:64], [pr2], [rDTT])

        win3 = WIN.rearrange("p (t k n) -> p t k n", t=10, k=8)
        wo3 = WOo.rearrange("p (k n) -> p k n", k=4)

        def produce(g, ti, cidx):
            c0, w = TILES[ti]
            par = ti % 2
            sample = ti == 8
            for k in range(6):
                xp_, rxp = XPAD[k]
                if sample:
                    ccol = cidx[k] * 128
                    dma("sp", STT_[0:48, 0:128], st_cc[i][:, ccol:ccol + 128], [], [rSTT], "stt")
                    pb, pr = bank()
                    tr(pb[:, 0:48], STT_[0:48, 0:128], 48, [rSTT], [pr])
                    cp("dve", xp_[:, 0:48].rearrange("p (k s) -> p k s", s=NSEQ),
                       pb[:, 0:48].rearrange("p (s k) -> p k s", k=3), [pr], [rxp])
                pb, pr = bank()
                for kk in range(8):
                    mm(pb[:, 0:w], win3[:, k, kk, :], XNt[:, kk, c0:c0 + w], kk == 0, kk == 7, [rWINt[k], rXN[kk]], [pr])
                doff = 48 if sample else 3
                cp("act", xp_[:, doff:doff + w], pb[:, 0:w], [pr], [rxp])
                step = 16 if sample else 1
                acc, racc = ACC[k % 2]
                act(acc[:, 0:w], xp_[:, 0:w], AF.Identity, [rxp, rVEC], [racc], bias=vcol("ccb%d" % i, cidx[k]),
                    scale=vcol("ccw%d" % i, cidx[k] * 4))
                for kq in range(1, 4):
                    stt(acc[:, 0:w], xp_[:, kq * step:kq * step + w], vcol("ccw%d" % i, cidx[k] * 4 + kq), acc[:, 0:w],
                        ALU.mult, ALU.add, [rxp, rVEC, racc], [racc])
                dst, rdst = (XS[par][k] if k < 4 else (BT[par] if k == 4 else CT[par]))
                act(dst[:, 0:w], acc[:, 0:w], AF.Silu, [racc], [rdst])
                if ti == 7:
                    cp("pool", TLP[:, k * 3:(k + 1) * 3], xp_[:, TW:TW + 3], [rxp], [rTLP])
                if sample:
                    cp("pool", TLS[:, k * 48:(k + 1) * 48], xp_[:, 48 + 80:48 + 128], [rxp], [rTLS])
                elif ti < 7:
                    cp("pool", xp_[:, 0:3], xp_[:, TW:TW + 3], [rxp], [rxp])
            for q in range(4):
                pb, pr = bank()
                for kk in range(8):
                    mm(pb[:, 0:w], win3[:, 6 + q, kk, :], XNt[:, kk, c0:c0 + w], kk == 0, kk == 7, [rWINt[6 + q], rXN[kk]], [pr])
                act(ZS[par][q][0][:, 0:w], pb[:, 0:w], AF.Silu, [pr], [ZS[par][q][1]])

        def prep(g, ch):
            ti, q = (ch // 2, ch % 2) if ch < 16 else (8, 0)
            par, cp_ = ti % 2, ch % 2
            cs_ = slice(q * 128, q * 128 + 128)
            sample = ch == 16
            tri = cf("tri_s") if sample else cf("tri")
            ut = cf("ut_s") if sample else cf("ut")
            cbm = cb("cbm_s") if sample else cb("cbm_p")
            dt_g = dtt3[:, ch, 8 * g:8 * g + 8]
            dta_g = dtt3[:, ch, 32 + 8 * g:32 + 8 * g + 8]
            bt, rbt = BT[par]
            ct, rct = CT[par]
            xdt, rxdt = XDT[cp_]
            btok, rbtok = BTOK[cp_]
            dend, rdend = DEND[cp_]
            cbmb, rcbm = CBM[cp_]
            pb, pr = bank()
            mm(pb[:, 0:128], bt[:, cs_], ct[:, cs_], True, True, [rbt, rct], [pr])
            tt_("dve", cbmb, pb[:, 0:128], cbm, ALU.mult, [pr, rCB], [rcbm])
            pb, pr = bank()
            for cc in range(4):
                mm(pb[:, cc * 128:(cc + 1) * 128], XS[par][cc][0][:, cs_], cb("ident"), True, True, [XS[par][cc][1], rCB], [pr])
            tt_("dve", xdt.rearrange("p (h d) -> p h d", h=8), pb[:].rearrange("p (h d) -> p h d", h=8),
                dt_g.unsqueeze(2).broadcast_to([128, 8, 64]), ALU.mult, [pr, rDTT], [rxdt])
            pb, pr = bank()
            mm(pb[:, 0:128], bt[:, cs_], cb("ident"), True, True, [rbt, rCB], [pr])
            cp("act", btok, pb[:, 0:128], [pr], [rbtok])
            pb, pr = bank()
            mm(pb[:, 0:8], ut, dta_g, True, True, [rCF, rDTT], [pr])
            act(dend, pb[:, 0:8], AF.Exp, [pr], [rdend])
            for hh in range(2):
                hs = slice(4 * hh, 4 * hh + 4)
                rr, rrr = RR[hh]
                dec, rdec = DEC[hh]
                expa, rexpa = EXPA[hh]
                wt, rwt = WT_[cp_][hh]
                cs2, rcs = CS_[cp_][hh]
                cdec, rcdec = CDEC[cp_][hh]
                tt_("pool", rr.rearrange("p (h l) -> p h l", h=4), tri.unsqueeze(1).broadcast_to([128, 4, 128]),
                    dta_g[:, hs].unsqueeze(2).broadcast_to([128, 4, 128]), ALU.mult, [rCF, rDTT], [rrr])
                pd, prd = bank()
                pa, pra = bank()
                mm(pd[:], ut, rr, True, True, [rCF, rrr], [prd])
                mm(pa[:], cf("ones"), rr, True, True, [rCF, rrr], [pra])
                act(dec, pd[:], AF.Exp, [prd], [rdec])
                act(expa, pa[:], AF.Exp, [pra], [rexpa])
                tt_("dve", wt.rearrange("p (h l) -> p h l", h=4), dec.rearrange("p (h l) -> p h l", h=4),
                    cbmb.unsqueeze(1).broadcast_to([128, 4, 128]), ALU.mult, [rdec, rcbm], [rwt])
                tt_("pool", cs2.rearrange("p (h l) -> p h l", h=4), expa.rearrange("p (h l) -> p h l", h=4),
                    ct[:, cs_].unsqueeze(1).broadcast_to([128, 4, 128]), ALU.mult, [rexpa, rct], [rcs])
                cp("pool", cdec, expa.rearrange("p (h l) -> p h l", h=4)[:, :, 127], [rexpa], [rcdec])

        def stage_b(g, ch):
            ti, q = (ch // 2, ch % 2) if ch < 16 else (8, 0)
            par, cp_ = ti % 2, ch % 2
            cs_ = slice(q * 128, q * 128 + 128)
            sample = ch == 16
            first = ch == 0
            YV = YV2[cp_]
            dta_g = dtt3[:, ch, 32 + 8 * g:32 + 8 * g + 8]
            xdt, rxdt = XDT[cp_]
            btok, rbtok = BTOK[cp_]
            dend, rdend = DEND[cp_]
            for hh in range(2):
                hs = slice(4 * hh, 4 * hh + 4)
                wt, rwt = WT_[cp_][hh]
                cs2, rcs = CS_[cp_][hh]
                cdec, rcdec = CDEC[cp_][hh]
                py, pry = bank(hold=True)
                use_off = (not sample) and (not first)
                for h in range(4):
                    j2 = (4 * hh + h) // 2
                    mm(py[:, h * 128:(h + 1) * 128], xdt[:, j2 * 128:(j2 + 1) * 128], wt[:, h * 128:(h + 1) * 128],
                       True, not use_off, [rxdt, rwt], [pry])
                    if use_off:
                        mm(py[:, h * 128:(h + 1) * 128], HTB[:, j2 * 128:(j2 + 1) * 128], cs2[:, h * 128:(h + 1) * 128],
                           False, True, [rHTB, rcs], [pry])
                tt_("pool", XDD.rearrange("p (h d) -> p h d", h=4), xdt.rearrange("p (h d) -> p h d", h=8)[:, hs, :],
                    dend[:, hs].unsqueeze(2).broadcast_to([128, 4, 64]), ALU.mult, [rxdt, rdend], [rXDD])
                if sample:
                    tt_("pool", DTAX.rearrange("p (h d) -> p h d", h=4),
                        dta_g[:, hs].unsqueeze(2).broadcast_to([128, 4, 64]),
                        cf("ones", 64).unsqueeze(1).broadcast_to([128, 4, 64]), ALU.mult, [rDTT, rCF], [rDTAX])
                    pc, prc = bank()
                    for pr_i in range(2):
                        mm(pc[:, pr_i * 16:(pr_i + 1) * 16], DTAX[:, pr_i * 128:(pr_i + 1) * 128], cf("seqmask", 16),
                           True, True, [rDTAX, rCF], [prc])
                    act(CD, pc[:, 0:32], AF.Exp, [prc], [rCD])
                    po_, pro_ = bank(hold=True)
                    hd0 = 8 * g + 4 * hh
                    def h0_load(b):
                        src = st_ssm[i][b, hd0:hd0 + 4].rearrange("(j h2) p n -> (h2 p) j n", h2=2)
                        dma("sp", H0[b % 2][0].rearrange("p (j n) -> p j n", j=2), src, [], [H0[b % 2][1]], "h0%d" % (b % 2))

                    def hf_store(b):
                        hf, rhf = HFIN[b % 3]
                        dst = ss_s[i][b, hd0:hd0 + 4].rearrange("(j h2) p n -> (h2 p) j n", h2=2)
                        dma("sp", dst, hf.rearrange("p (j n) -> p j n", j=2), [rhf], [], "hf%d" % (b % 3))

                    h0_load(0)
                    for b in range(NSEQ):
                        h0, rh0 = H0[b % 2]
                        h0t, rh0t = H0T[b % 2]
                        hf, rhf = HFIN[b % 3]
                        bm, rbm = BM[b % 2]
                        pt, prt = bank()
                        for pr_i in range(2):
                            tr(pt[:, pr_i * 128:(pr_i + 1) * 128], h0[:, pr_i * 128:(pr_i + 1) * 128], 128, [rh0], [prt])
                        cp("act", h0t, pt[:, 0:256], [prt], [rh0t])
                        for h in range(4):
                            pr_i = h // 2
                            mm(scol(po_[:, h * 128:(h + 1) * 128], b), h0t[:, pr_i * 128:(pr_i + 1) * 128],
                               scol(cs2[:, h * 128:(h + 1) * 128], b), True, True, [rh0t, rcs], [pro_])
                        ts_("pool", bm, btok, cf("seqmask", 16)[:, b:b + 1], ALU.mult, [rbtok, rCF], [rbm])
                        pf, prf = bank()
                        for pr_i in range(2):
                            mm(pf[:, pr_i * 128:(pr_i + 1) * 128], XDD[:, pr_i * 128:(pr_i + 1) * 128], bm, True, True,
                               [rXDD, rbm], [prf])
                        for pr_i in range(2):
                            stt(hf[:, pr_i * 128:(pr_i + 1) * 128], h0[:, pr_i * 128:(pr_i + 1) * 128],
                                CD[:, pr_i * 16 + b:pr_i * 16 + b + 1], pf[:, pr_i * 128:(pr_i + 1) * 128],
                                ALU.mult, ALU.add, [rh0, rCD, prf], [rhf])
                        if b + 1 < NSEQ:
                            h0_load(b + 1)
                        if b > 0:
                            hf_store(b - 1)
                    hf_store(NSEQ - 1)
                    cp("act", YOFF, po_[:], [pro_], [rYOFF])
                    release(po_)
                for j2l in range(2):
                    cc = 2 * hh + j2l
                    yv, ryv = YV[cc]
                    xs_, rxs = XS[par][cc]
                    for h2 in range(2):
                        lo, hi = 64 * h2, 64 * h2 + 64
                        hsel = 2 * j2l + h2
                        stt(yv[lo:hi, :], xs_[lo:hi, cs_], vcol("dsk%d" % i, 4 * g + cc, p0=lo, p1=hi),
                            py[lo:hi, hsel * 128:(hsel + 1) * 128], ALU.mult, ALU.add, [rxs, rVEC, pry], [ryv])
                        if sample:
                            tt_("dve", yv[lo:hi, :], yv[lo:hi, :], YOFF[lo:hi, hsel * 128:(hsel + 1) * 128], ALU.add,
                                [ryv, rYOFF], [ryv])
                release(py)
                if not sample:
                    pst, prst = bank()
                    mm(pst[:, 0:256], btok, XDD, True, True, [rbtok, rXDD], [prst])
                    hcol = slice(256 * hh, 256 * hh + 256)
                    if first:
                        cp("dve", HT[:, hcol], pst[:, 0:256], [prst], [rHT])
                    else:
                        tt_("pool", TMPH.rearrange("p (h d) -> p h d", h=4), HT[:, hcol].rearrange("p (h d) -> p h d", h=4),
                            cdec.unsqueeze(2).broadcast_to([128, 4, 64]), ALU.mult, [rHT, rcdec], [rTMPH])
                        tt_("dve", HT[:, hcol], TMPH, pst[:, 0:256], ALU.add, [rTMPH, prst], [rHT])
                    cp("act", HTB[:, hcol], HT[:, hcol], [rHT], [rHTB])

        def stage_c(g, ch):
            ti, q = (ch // 2, ch % 2) if ch < 16 else (8, 0)
            par, cp_ = ti % 2, ch % 2
            cs_ = slice(q * 128, q * 128 + 128)
            YV = YV2[cp_]
            for cc in range(4):
                yv, ryv = YV[cc]
                tt_("pool", yv, yv, ZS[par][cc][0][:, cs_], ALU.mult, [ryv, ZS[par][cc][1]], [ryv])
                act(SQY[:, cc * 128:(cc + 1) * 128], yv, AF.Square, [ryv], [rSQY])
            pb, pr = bank()
            for cc in range(4):
                mm(pb[:, 0:128], cb("o512"), SQY[:, cc * 128:(cc + 1) * 128], cc == 0, cc == 3, [rCB, rSQY], [pr])
            act(RSTD, pb[:, 0:128], AF.Ln, [pr, rMISC], [rRSTD], bias=MISC[:, 0:1], scale=1.0)
            act(RSTD, RSTD, AF.Exp, [rRSTD], [rRSTD], scale=-0.5)
            for cc in range(4):
                yv, ryv = YV[cc]
                stt(YN[cc][0][:, cs_], yv, vcol("gn%d" % i, 4 * g + cc), RSTD, ALU.mult, ALU.mult,
                    [ryv, rVEC, rRSTD], [YN[cc][1]])

        def outproj(g, ti):
            c0, w = TILES[ti]
            for n in range(8):
                pb, pr = bank()
                for k in range(4):
                    mm(pb[:, 0:w], wo3[:, k, n * 128:(n + 1) * 128], YN[k][0][:, 0:w], k == 0, k == 3,
                       [rWOo, YN[k][1]], [pr])
                resid_add(n, c0, w, pb, pr)
            if ti == 7:
                pb, pr = bank()
                for q in range(4):
                    tr(pb[:, q * 128:(q + 1) * 128], HT[:, q * 128:(q + 1) * 128], 128, [rHT], [pr])
                cp("act", SOUT, pb[:], [pr], [rSOUT])
                dma("sp", ss_p[i][8 * g:8 * g + 8].rearrange("(j h2) p n -> (h2 p) j n", h2=2),
                    SOUT.rearrange("p (j n) -> p j n", j=4), [rSOUT], [], "sout")

        def load_win(gg):
            cols = [2048 + 512 * gg + 128 * q for q in range(4)] + [4096 + 128 * gg, 4608 + 128 * gg] + \
                   [512 * gg + 128 * q for q in range(4)]
            for t_, col in enumerate(cols):
                dma("pool", win3[:, t_, :, :], w_in_o[i][:, col:col + 128].rearrange("(k p) n -> p k n", p=128),
                    [], [rWINt[t_]], "win%d" % t_)

        for g in range(4):
            S.barrier()
            if g == 0:
                load_win(0)
            dma("pool", wo3, w_out_o[i][512 * g:512 * (g + 1), :].rearrange("(k p) n -> p k n", p=128), [], [rWOo], "woo")
            cidx = [4 * g + q for q in range(4)] + [16 + g, 20 + g]
            for k in range(6):
                memset("pool", XPAD[k][0][:, 0:3], 0.0, [XPAD[k][1]])
            def stage_a(ch):
                prep(g, ch)

            def stage_c_full(ch):
                stage_c(g, ch)
                if ch == 16:
                    outproj(g, 8)
                elif ch % 2 == 1:
                    outproj(g, ch // 2)

            produce(g, 0, cidx)
            produce(g, 1, cidx)
            stage_a(0)
            stage_a(1)
            stage_b(g, 0)
            for s_ in range(0, 14):
                stage_c_full(s_)
                if s_ % 2 == 1:
                    produce(g, (s_ + 3) // 2, cidx)
                    if s_ == 13 and g < 3:
                        load_win(g + 1)
                stage_a(s_ + 2)
                stage_b(g, s_ + 1)
            stage_a(16)
            stage_b(g, 15)
            stage_c_full(14)
            stage_c_full(15)
            S.barrier()
            stage_b(g, 16)
            stage_c_full(16)
            ocols = [512 * g + 128 * q for q in range(4)] + [2048 + 128 * g, 2560 + 128 * g]
            for k in range(6):
                pb, pr = bank()
                tr(pb[0:3, 0:128], TLP[:, k * 3:(k + 1) * 3], 128, [rTLP], [pr])
                tr(pb[0:48, 128:256], TLS[:, k * 48:(k + 1) * 48], 128, [rTLS], [pr])
                cp("act", STT_[0:3, 256:384], pb[0:3, 0:128], [pr], [rSTT])
                cp("act", STT_[0:48, 512:640], pb[0:48, 128:256], [pr], [rSTT])
                dma("sp", cc_p[i][:, ocols[k]:ocols[k] + 128], STT_[0:3, 256:384], [rSTT], [], "stt")
                dma("sp", cc_s[i][:, :, ocols[k]:ocols[k] + 128].rearrange("s t d -> t s d"), STT_[0:48, 512:640],
                    [rSTT], [], "stt")

    if phases is None:
        phases = ["e0", "f0", "o0", "f1", "e1", "f2", "o1", "f3"]
    for ph in phases:
        if ph[0] == "e":
            even_phase(int(ph[1]), ph[3:] if len(ph) > 2 else None)
        elif ph[0] == "o":
            odd_phase(int(ph[1]))
        elif ph[0] == "f":
            ffn_phase(int(ph[1]))

    S.barrier()
    AR.reset()
    yo = [AR.f32(1024, "yo%d" % k) for k in range(2)]
    ys3 = y_s.rearrange("(s t) d -> t s d", t=8)
    for rt in range(17):
        buf, rb = yo[rt % 2]
        for half in range(2):
            pb, pr = bank()
            for q in range(4):
                c = half * 4 + q
                tr(pb[:, q * 128:(q + 1) * 128], Xt[:, c, rt * 128:(rt + 1) * 128], 128, rX[c], [pr])
            cp("act" if half == 0 else "dve", buf[:, half * 512:(half + 1) * 512], pb[:], [pr], [rb])
        if rt < 16:
            dma("sp", y_p[rt * 128:(rt + 1) * 128, :], buf, [rb], [], "yo%d" % (rt % 2))
        else:
            dma("sp", ys3, buf, [rb], [], "yo%d" % (rt % 2))

    S.emit(st)
    st.close()
    return nc


_CACHE = {}


def make_in_maps(inp):
    cfh, cbh = host_consts()
    vecs = host_vecs(inp)
    posrow = np.concatenate([np.arange(TP, dtype=np.float32),
                             np.repeat(8192.0 + np.arange(8, dtype=np.float32), NSEQ)])[None, :].astype(np.float32)
    qperm = np.concatenate([np.concatenate([np.arange(64 * c, 64 * c + 64), np.arange(64 * (4 + c), 64 * (4 + c) + 64)])
                            for c in range(4)])
    w_in_e = inp["w_in_e"].copy()
    w_in_e[:, :, 1024:1536] = inp["w_in_e"][:, :, 1024 + qperm]
    w_out_e = inp["w_out_e"].copy()
    w_out_e[:, 512:1024, :] = inp["w_out_e"][:, 512 + qperm, :]
    w_dt2 = np.ascontiguousarray(np.concatenate([inp["w_in_o"][:, :, 5120:5152]] * 2, axis=2))
    shared = dict(posrow=posrow, cf=cfh, cb=cbh, vecs=vecs, w_in_e=w_in_e, w_out_e=w_out_e, w_in_o=inp["w_in_o"],
                  w_dt2=w_dt2, w_out_o=inp["w_out_o"], w_gate=inp["w_gate"], w_up=inp["w_up"], w_down=inp["w_down"])
    in_maps = []
    for c in range(NCORES):
        sl = slice(NSEQ * c, NSEQ * (c + 1))
        m = dict(shared)
        m["xp"] = np.ascontiguousarray(inp["x_prompt"][c])
        m["xs"] = np.ascontiguousarray(inp["x_sample"][sl].reshape(TS, D))
        m["st_ca"] = np.ascontiguousarray(inp["state_conv_a"][:, sl].reshape(2, 480, 512))
        m["st_k"] = np.ascontiguousarray(inp["cache_win_k"][:, sl].reshape(2, NSEQ, 128, 128))
        m["st_v"] = np.ascontiguousarray(inp["cache_win_v"][:, sl].reshape(2, NSEQ, 128, 128))
        m["st_cc"] = np.ascontiguousarray(inp["state_conv_c"][:, sl].reshape(2, 48, 3072))
        m["st_ssm"] = np.ascontiguousarray(inp["state_ssm"][:, sl])
        m["st_ff"] = np.ascontiguousarray(inp["state_ffn_conv"][:, sl].reshape(4, 32, DFF))
        in_maps.append(m)
    return in_maps


def kernel(**inp):
    inp = {k: np.asarray(v) for k, v in inp.items()}
    if "nc" not in _CACHE:
        _CACHE["nc"] = build_program()
    nc = _CACHE["nc"]
    in_maps = make_in_maps(inp)
    res = run_bass_kernel_spmd(nc, in_maps, core_ids=list(range(NCORES)))
    R = res.results

    def cat(name, axis, shp=None):
        parts = [np.asarray(R[c][name]) if shp is None else np.asarray(R[c][name]).reshape(shp) for c in range(NCORES)]
        return np.concatenate(parts, axis=axis)

    y_prompt = np.stack([np.asarray(R[c]["y_p"]) for c in range(NCORES)], 0)
    y_sample = cat("y_s", 0, (NSEQ, 8, D))
    ca_p_ = np.stack([np.asarray(R[c]["ca_p"]) for c in range(NCORES)], 1)
    ca_s_ = cat("ca_s", 1)
    wk_p_ = np.stack([np.asarray(R[c]["wk_p"]).reshape(2, 128, 2, 64) for c in range(NCORES)], 1)
    wk_s_ = cat("wk_s", 1, (2, NSEQ, 128, 2, 64))
    wv_p_ = np.stack([np.asarray(R[c]["wv_p"]).reshape(2, 128, 2, 64) for c in range(NCORES)], 1)
    wv_s_ = cat("wv_s", 1, (2, NSEQ, 128, 2, 64))
    cc_p_ = np.stack([np.asarray(R[c]["cc_p"]) for c in range(NCORES)], 1)
    cc_s_ = cat("cc_s", 1)
    ss_p_ = np.stack([np.asarray(R[c]["ss_p"]) for c in range(NCORES)], 1)
    ss_s_ = cat("ss_s", 1)
    ff_p_ = np.stack([np.asarray(R[c]["ff_p"]) for c in range(NCORES)], 1)
    ff_s_ = cat("ff_s", 1)
    outs = (y_prompt, y_sample, ca_p_, ca_s_, wk_p_, wk_s_, wv_p_, wv_s_, cc_p_, cc_s_, ss_p_, ss_s_, ff_p_, ff_s_)
    return tuple(np.ascontiguousarray(o, dtype=np.float32) for o in outs)
```

```python
import contextlib
import math
import os
import numpy as np
import concourse.bass as bass
import concourse.mybir as mybir
from concourse.bass_utils import run_bass_kernel_spmd

F32 = mybir.dt.float32
BF16 = mybir.dt.bfloat16
I32 = mybir.dt.int32
AF = mybir.ActivationFunctionType
ALU = mybir.AluOpType

NCORES = 8
D = 1024
TP = 2048
TS = 128
T = TP + TS
NSEQ = 16
DFF = 2816
NJ = DFF // 128
EPS = 1e-6
NEG = -30000.0
ARENA_WORDS = 24300


class Res:
    __slots__ = ("name", "last_w", "readers", "dsem", "dcount", "excl")

    def __init__(self, name, excl=False):
        self.name = name
        self.excl = excl
        self.last_w = None
        self.readers = []
        self.dsem = None
        self.dcount = 0


class Op:
    __slots__ = ("eng", "fn", "waits", "signal", "sigval", "dma_res", "dma_val")

    def __init__(self, eng, fn):
        self.eng = eng
        self.fn = fn
        self.waits = []
        self.signal = False
        self.sigval = 0
        self.dma_res = None
        self.dma_val = 0


class Sched:
    ENGS = ("pe", "act", "dve", "pool", "sp")

    def __init__(self, nc):
        self.nc = nc
        self.ops = []
        self.dma_res = []
        self.last = {e: None for e in self.ENGS}
        self.dma_since = []
        self.pending = {e: [] for e in self.ENGS}

    def op(self, eng, fn, reads=(), writes=(), dma=None):
        o = Op(eng, fn)
        writes = list(writes) + [r for r in reads if r.excl and r not in writes]
        deps = [(d, True) for d in self.pending[eng]]
        self.pending[eng] = []
        for r in reads:
            if r.last_w is not None:
                deps.append((r.last_w, True))
        for w in writes:
            if w.last_w is not None:
                deps.append((w.last_w, False))
            deps.extend((x, False) for x in w.readers)
        seen = set()
        for d, raw in deps:
            if d.dma_res is None and d.eng == eng:
                if eng in ("pe", "sp") or not raw:
                    continue
            if id(d) in seen:
                continue
            seen.add(id(d))
            o.waits.append(d)
            d.signal = True
        if dma is not None:
            o.dma_res = dma
            dma.dcount += 1
            o.dma_val = 16 * dma.dcount
            o.signal = True
            if dma.dsem is None:
                dma.dsem = True
                self.dma_res.append(dma)
            self.dma_since.append(o)
        else:
            self.last[eng] = o
        for w in writes:
            w.last_w = o
            w.readers = []
        for r in reads:
            r.readers.append(o)
        self.ops.append(o)
        return o

    def barrier(self):
        tg = [o for o in self.last.values() if o is not None] + self.dma_since
        self.dma_since = []
        for e in self.ENGS:
            self.pending[e] = list(tg)

    def emit(self, stack):
        nc = self.nc
        esem = {e: stack.enter_context(nc.semaphore("s_" + e)) for e in self.ENGS}
        for i, r in enumerate(self.dma_res):
            r.dsem = stack.enter_context(nc.semaphore("d%d" % i))
        cnt = {e: 0 for e in self.ENGS}
        per = {e: [] for e in self.ENGS}
        for o in self.ops:
            if o.dma_res is None and o.signal:
                cnt[o.eng] += 1
                o.sigval = cnt[o.eng]
            per[o.eng].append(o)
        block = stack.enter_context(nc.Block())

        def run(eng_name, eng):
            known = {}
            for o in per[eng_name]:
                for d in o.waits:
                    if d.dma_res is not None:
                        key, val = d.dma_res.dsem, d.dma_val
                    else:
                        key, val = esem[d.eng], d.sigval
                    if known.get(id(key), 0) >= val:
                        continue
                    known[id(key)] = val
                    eng.wait_ge(key, val)
                ins = o.fn(eng)
                if o.dma_res is not None:
                    ins.then_inc(o.dma_res.dsem, 16)
                elif o.signal:
                    ins.then_inc(esem[eng_name], 1)
            if eng_name == "sp":
                for r in self.dma_res:
                    if known.get(id(r.dsem), 0) < 16 * r.dcount:
                        eng.wait_ge(r.dsem, 16 * r.dcount)
                for e2 in ("pe", "act", "dve", "pool"):
                    if cnt[e2] > 0:
                        eng.wait_ge(esem[e2], cnt[e2])

        @block.tensor
        def _(e):
            run("pe", e)

        @block.scalar
        def _(e):
            run("act", e)

        @block.vector
        def _(e):
            run("dve", e)

        @block.gpsimd
        def _(e):
            run("pool", e)

        @block.sync
        def _(e):
            run("sp", e)


class Arena:
    def __init__(self, t, words):
        self.t = t
        self.words = words
        self.off = 0

    def reset(self):
        self.off = 0

    def f32(self, n, name):
        a = self.off
        self.off += n
        assert self.off <= self.words, (name, self.off, self.words)
        return self.t[:, a:a + n], Res(name)

    def bf(self, n, name):
        w = (n + 1) // 2
        a = self.off
        self.off += w
        assert self.off <= self.words, (name, self.off, self.words)
        return self.t[:, a:a + w].bitcast(BF16)[:, 0:n], Res(name)


def vec_layout():
    ent = []
    for i in range(2):
        ent += [("nme%d" % i, 8), ("nmo%d" % i, 8), ("caw%d" % i, 124), ("cab%d" % i, 4), ("lng%d" % i, 4),
                ("lnb%d" % i, 4), ("qg%d" % i, 1), ("kg%d" % i, 1), ("snk%d" % i, 4), ("ccw%d" % i, 96),
                ("ccb%d" % i, 24), ("dtb%d" % i, 1), ("alog%d" % i, 1), ("dsk%d" % i, 16), ("gn%d" % i, 16)]
    for l in range(4):
        ent += [("nf%d" % l, 8), ("fcw%d" % l, 66), ("fcb%d" % l, 22)]
    ent += [("invf", 1)]
    off = {}
    o = 0
    for n, c in ent:
        off[n] = o
        o += c
    return off, o


CF = dict(ident=0, tri=128, ut=256, tri_s=384, ut_s=512, pmat=640, seqmask=768, ones=784)
NCF = 912
CB = dict(ident=0, o1024=128, o512=256, bd64=384, mcur=512, mprev=640, msnew=768, mshist=896, cbm_p=1024,
          cbm_s=1152, ones=1280)
NCB = 1408


def host_consts():
    cf = np.zeros((128, NCF), np.float32)
    cb = np.zeros((128, NCB), np.float32)
    idx = np.arange(128)
    eye = np.eye(128, dtype=np.float32)
    tri = (idx[:, None] <= idx[None, :]).astype(np.float32)
    ut = (idx[:, None] > idx[None, :]).astype(np.float32)
    tt_, ss_ = idx // 16, idx % 16
    same = ss_[:, None] == ss_[None, :]
    tri_s = (same & (tt_[:, None] <= tt_[None, :])).astype(np.float32)
    ut_s = (same & (tt_[:, None] > tt_[None, :])).astype(np.float32)
    pm = np.zeros((128, 128), np.float32)
    for hb in (0, 64):
        for d in range(8):
            pm[hb + d + 8, hb + d] = -1.0
            pm[hb + d, hb + d + 8] = 1.0
    cf[:, 0:128] = eye
    cf[:, 128:256] = tri
    cf[:, 256:384] = ut
    cf[:, 384:512] = tri_s
    cf[:, 512:640] = ut_s
    cf[:, 640:768] = pm
    cf[:, 768:784] = (ss_[:, None] == np.arange(16)[None, :]).astype(np.float32)
    cf[:, 784:912] = 1.0
    cb[:, 0:128] = eye
    cb[:, 128:256] = 1.0 / 1024
    cb[:, 256:384] = 1.0 / 512
    bd = np.zeros((128, 128), np.float32)
    bd[0:64, 0:64] = 1.0 / 64
    bd[64:128, 64:128] = 1.0 / 64
    cb[:, 384:512] = bd
    cb[:, 512:640] = np.where(idx[None, :] >= idx[:, None], 0.0, NEG)
    cb[:, 640:768] = np.where(idx[:, None] > idx[None, :], 0.0, NEG)
    cb[:, 768:896] = np.where(same & (tt_[:, None] <= tt_[None, :]), 0.0, NEG)
    cb[:, 896:1024] = np.where(idx[:, None] > tt_[None, :], 0.0, NEG)
    cb[:, 1024:1152] = tri
    cb[:, 1152:1280] = tri_s
    cb[:, 1280:1408] = 1.0
    return cf, cb


def host_vecs(inp):
    off, nv = vec_layout()
    v = np.zeros((128, nv), np.float32)

    def put(name, arr):
        arr = np.asarray(arr, np.float32)
        v[:, off[name]:off[name] + arr.shape[1]] = arr

    def chunks(x):
        return np.asarray(x).reshape(-1, 128).T

    def taps(w):
        K, C = w.shape
        return np.asarray(w).reshape(K, C // 128, 128).transpose(2, 1, 0).reshape(128, -1)

    def pair(x, n):
        x = np.asarray(x)
        return np.concatenate([np.repeat(x[0::2][None, :], 64, 0), np.repeat(x[1::2][None, :], 64, 0)], 0)

    for i in range(2):
        put("nme%d" % i, chunks(inp["norm_mix_e"][i]))
        put("nmo%d" % i, chunks(inp["norm_mix_o"][i]))
        put("caw%d" % i, taps(inp["conv_a_w"][i]))
        put("cab%d" % i, chunks(inp["conv_a_b"][i]))
        put("lng%d" % i, chunks(inp["ln_a_g"][i]))
        put("lnb%d" % i, chunks(inp["ln_a_b"][i]))
        put("qg%d" % i, np.tile(np.asarray(inp["q_norm_g"][i]), 2)[:, None])
        put("kg%d" % i, np.tile(np.asarray(inp["k_norm_g"][i]), 2)[:, None])
        sk = np.asarray(inp["sinks"][i])
        put("snk%d" % i, np.concatenate([np.repeat(sk[None, 0:4], 64, 0), np.repeat(sk[None, 4:8], 64, 0)], 0))
        put("ccw%d" % i, taps(inp["conv_c_w"][i]))
        put("ccb%d" % i, chunks(inp["conv_c_b"][i]))
        dtb = np.zeros((128, 1), np.float32)
        dtb[0:32, 0] = inp["dt_bias"][i]
        dtb[32:64, 0] = inp["dt_bias"][i]
        put("dtb%d" % i, dtb)
        al = np.zeros((128, 1), np.float32)
        al[32:64, 0] = inp["a_log"][i]
        put("alog%d" % i, al)
        put("dsk%d" % i, pair(inp["d_skip"][i], 16))
        put("gn%d" % i, chunks(inp["gnorm_c"][i]))
    for l in range(4):
        put("nf%d" % l, chunks(inp["norm_ffn"][l]))
        put("fcw%d" % l, taps(inp["ffn_conv_w"][l]))
        put("fcb%d" % l, chunks(inp["ffn_conv_b"][l]))
    invf = np.zeros((128, 1), np.float32)
    f = (500000.0 ** (-np.arange(0, 16, 2, dtype=np.float32) / 16.0)).astype(np.float32)
    for hb in (0, 64):
        invf[hb:hb + 8, 0] = f
        invf[hb + 8:hb + 16, 0] = f
    put("invf", invf)
    return v


def build_program(phases=None):
    nc = bass.Bass("TRN2", target_bir_lowering=False)
    VO, NV = vec_layout()

    def din(name, shape):
        return nc.dram_tensor(name, list(shape), F32, kind="ExternalInput").ap()

    def dout(name, shape):
        return nc.dram_tensor(name, list(shape), F32, kind="ExternalOutput").ap()

    xp_d = din("xp", [TP, D])
    xs_d = din("xs", [TS, D])
    pos_d = din("posrow", [1, T])
    cf_d = din("cf", [128, NCF])
    cb_d = din("cb", [128, NCB])
    vec_d = din("vecs", [128, NV])
    st_ca = din("st_ca", [2, 480, 512])
    st_k = din("st_k", [2, NSEQ, 128, 128])
    st_v = din("st_v", [2, NSEQ, 128, 128])
    st_cc = din("st_cc", [2, 48, 3072])
    st_ssm = din("st_ssm", [2, NSEQ, 32, 64, 128])
    st_ff = din("st_ff", [4, 32, DFF])
    w_in_e = din("w_in_e", [2, D, 1792])
    w_out_e = din("w_out_e", [2, D, D])
    w_in_o = din("w_in_o", [2, D, 5152])
    w_dt2 = din("w_dt2", [2, D, 64])
    w_out_o = din("w_out_o", [2, 2048, D])
    w_gate = din("w_gate", [4, D, DFF])
    w_up = din("w_up", [4, D, DFF])
    w_down = din("w_down", [4, DFF, D])

    y_p = dout("y_p", [TP, D])
    y_s = dout("y_s", [TS, D])
    ca_p = dout("ca_p", [2, 30, 512])
    ca_s = dout("ca_s", [2, NSEQ, 30, 512])
    wk_p = dout("wk_p", [2, 128, 128])
    wk_s = dout("wk_s", [2, NSEQ, 128, 128])
    wv_p = dout("wv_p", [2, 128, 128])
    wv_s = dout("wv_s", [2, NSEQ, 128, 128])
    cc_p = dout("cc_p", [2, 3, 3072])
    cc_s = dout("cc_s", [2, NSEQ, 3, 3072])
    ss_p = dout("ss_p", [2, 32, 64, 128])
    ss_s = dout("ss_s", [2, NSEQ, 32, 64, 128])
    ff_p = dout("ff_p", [4, 2, DFF])
    ff_s = dout("ff_s", [4, NSEQ, 2, DFF])

    st = contextlib.ExitStack()
    S = Sched(nc)

    def sbt(name, shape, dt=F32):
        return st.enter_context(nc.sbuf_tensor(name, shape, dt))

    Xt = sbt("X", [128, 8, T])
    XNt = sbt("XN", [128, 8, T], BF16)
    CFt = sbt("CFt", [128, NCF])
    CBt = sbt("CBt", [128, NCB], BF16)
    VEC = sbt("VEC", [128, NV])
    MISC = sbt("MISC", [128, 16])
    ARt = sbt("ARENA", [128, ARENA_WORDS])
    AR = Arena(ARt, ARENA_WORDS)
    PB = [st.enter_context(nc.psum_tensor("pb%d" % i, [128, 512], F32)) for i in range(8)]
    PR = [Res("pb%d" % i, excl=True) for i in range(8)]
    pstate = [0]
    held = set()

    def bank(hold=False):
        while True:
            i = pstate[0] % 8
            pstate[0] += 1
            if i not in held:
                break
        if hold:
            held.add(i)
        return PB[i], PR[i]

    def release(pb):
        for i in range(8):
            if PB[i] is pb:
                held.discard(i)

    SEM = {}

    def sh(name):
        if name not in SEM:
            SEM[name] = Res("sem_" + name)
        return SEM[name]

    def scol(ap2d, b):
        return ap2d.rearrange("p (t s) -> p t s", s=NSEQ)[:, :, b]

    rX = [[Res("X%d_%d" % (c, j)) for j in range(5)] for c in range(8)]
    rXN = [Res("XN%d" % c) for c in range(8)]
    rCF, rCB, rVEC, rMISC = Res("cf"), Res("cb"), Res("vec"), Res("misc")
    TT = [(0, 512), (512, 512), (1024, 512), (1536, 512), (2048, 128)]

    def xres(c, c0, w):
        return [rX[c][j] for j in range(5) if not (TT[j][0] >= c0 + w or TT[j][0] + TT[j][1] <= c0)]

    def cf(name, n=128, rows=128):
        return CFt[0:rows, CF[name]:CF[name] + n]

    def cb(name, n=128, rows=128):
        return CBt[0:rows, CB[name]:CB[name] + n]

    def vcol(name, j=0, n=1, p0=0, p1=128):
        return VEC[p0:p1, VO[name] + j:VO[name] + j + n]

    def mm(out, lhsT, rhs, start, stop, R, W):
        S.op("pe", lambda e: e.matmul(out, lhsT=lhsT, rhs=rhs, start=start, stop=stop), reads=R, writes=W)

    def tr(out, in_, k, R, W):
        S.op("pe", lambda e: e.transpose(out, in_, CFt[0:k, 0:k]), reads=list(R) + [rCF], writes=W)

    def act(out, in_, func, R, W, bias=None, scale=None):
        kw = {}
        if bias is not None:
            kw["bias"] = bias
        if scale is not None:
            kw["scale"] = scale
        S.op("act", lambda e: e.activation(out=out, in_=in_, func=func, **kw), reads=R, writes=W)

    def cp(eng, out, in_, R, W):
        if eng == "act":
            S.op("act", lambda e: e.copy(out=out, in_=in_), reads=R, writes=W)
        else:
            S.op(eng, lambda e: e.tensor_copy(out=out, in_=in_), reads=R, writes=W)

    def tt_(eng, out, in0, in1, op, R, W):
        S.op(eng, lambda e: e.tensor_tensor(out=out, in0=in0, in1=in1, op=op), reads=R, writes=W)

    def ts_(eng, out, in0, s1, op0, R, W, s2=None, op1=None):
        if op1 is None:
            S.op(eng, lambda e: e.tensor_scalar(out=out, in0=in0, scalar1=s1, scalar2=None, op0=op0), reads=R, writes=W)
        else:
            S.op(eng, lambda e: e.tensor_scalar(out=out, in0=in0, scalar1=s1, scalar2=s2, op0=op0, op1=op1),
                 reads=R, writes=W)

    def stt(out, in0, scalar, in1, op0, op1, R, W):
        S.op("dve", lambda e: e.scalar_tensor_tensor(out=out, in0=in0, scalar=scalar, in1=in1, op0=op0, op1=op1),
             reads=R, writes=W)

    def dma(eng, out, in_, R, W, sem):
        S.op(eng, lambda e: e.dma_start(out=out, in_=in_), reads=R, writes=W, dma=sh(sem))

    def memset(eng, ap, val, W):
        S.op(eng, lambda e: e.memset(ap, val), writes=W)

    def rstd_from(out, psum_ap, R, W):
        act(out, psum_ap, AF.Sqrt, R + [rMISC], W, bias=MISC[:, 0:1], scale=1.0)
        S.op("dve", lambda e: e.reciprocal(out=out, in_=out), reads=W, writes=W)

    def wload(dst3, src2, res, sem):
        dma("pool", dst3, src2.rearrange("(k p) n -> p k n", p=128), [], [res], sem)

    dma("sp", CFt[:], cf_d, [], [rCF], "cf")
    dma("pool", CBt[:], cb_d, [], [rCB], "cb")
    dma("sp", VEC[:], vec_d, [], [rVEC], "vec")
    memset("pool", MISC[:], 0.0, [rMISC])
    memset("pool", MISC[:, 0:1], EPS, [rMISC])
    for i in range(2):
        act(MISC[32:64, 10 + i:11 + i], vcol("alog%d" % i, p0=32, p1=64), AF.Exp, [rVEC, rMISC], [rMISC])
        ts_("pool", MISC[32:64, 10 + i:11 + i], MISC[32:64, 10 + i:11 + i], -1.0, ALU.mult, [rMISC], [rMISC])
        act(MISC[:, 2 + 4 * i:6 + 4 * i], vcol("snk%d" % i, n=4), AF.Exp, [rVEC, rMISC], [rMISC])

    AR.reset()
    xin = [AR.f32(1024, "xin%d" % k) for k in range(2)]
    for rt in range(17):
        buf, rb = xin[rt % 2]
        src = xp_d[rt * 128:(rt + 1) * 128, :] if rt < 16 else xs_d
        dma("sp", buf, src, [], [rb], "xin%d" % (rt % 2))
        for half in range(2):
            pb, pr = bank()
            for q in range(4):
                c = half * 4 + q
                tr(pb[:, q * 128:(q + 1) * 128], buf[:, c * 128:(c + 1) * 128], 128, [rb], [pr])
            if rt < 16:
                dst = Xt[:, half * 4:half * 4 + 4, rt * 128:(rt + 1) * 128]
                src_ps = pb[:].rearrange("p (c t) -> p c t", c=4)
            else:
                dst = Xt[:, half * 4:half * 4 + 4, TP:T].rearrange("p c (t s) -> p c t s", s=NSEQ)
                src_ps = pb[:].rearrange("p (c s t) -> p c t s", c=4, s=NSEQ)
            W = [rX[half * 4 + q][min(rt // 4, 4)] for q in range(4)]
            cp("act" if half == 0 else "dve", dst, src_ps, [pr], W)

    NORM_OFF = ARENA_WORDS - 4352
    nsq = [(ARt[:, NORM_OFF + 1088 * k:NORM_OFF + 1088 * (k + 1)].bitcast(BF16), Res("nsq%d" % k)) for k in range(2)]
    nrs = (ARt[:, NORM_OFF + 2176:NORM_OFF + 4352], Res("nrs"))

    def norm_body(gname):
        sq = nsq
        rs, rrs = nrs
        banks = [bank(hold=True) for _ in range(5)]
        for c in range(8):
            b, rb = sq[c % 2]
            if c % 2 == 0:
                act(b, Xt[:, c, :], AF.Square, rX[c], [rb])
            else:
                tt_("pool", b, Xt[:, c, :], Xt[:, c, :], ALU.mult, rX[c], [rb])
            for j, (c0, w) in enumerate(TT):
                pb, pr = banks[j]
                mm(pb[:, 0:w], cb("o1024"), b[:, c0:c0 + w], c == 0, c == 7, [rCB, rb], [pr])
        for j, (c0, w) in enumerate(TT):
            pb, pr = banks[j]
            rstd_from(rs[:, c0:c0 + w], pb[:, 0:w], [pr], [rrs])
            release(pb)
        for c in range(8):
            stt(XNt[:, c, :], Xt[:, c, :], vcol(gname, c), rs, ALU.mult, ALU.mult, rX[c] + [rVEC, rrs], [rXN[c]])

    def resid_add(n, c0, w, pb, pr):
        tt_("dve", Xt[:, n, c0:c0 + w], Xt[:, n, c0:c0 + w], pb[:, 0:w], ALU.add, xres(n, c0, w) + [pr], xres(n, c0, w))

    def ffn_phase(l):
        S.barrier()
        AR.reset()
        GW = 2210
        WG = [AR.bf(8 * 256, "wg%d" % k) for k in range(2)]
        WU = [AR.bf(8 * 256, "wu%d" % k) for k in range(2)]
        WD = [AR.bf(2 * 1024, "wd%d" % k) for k in range(2)]
        HS = [AR.f32(256, "hs%d" % k) for k in range(2)]
        G = [AR.f32(GW, "g%d" % k) for k in range(2)]
        U = [AR.f32(T, "u%d" % k) for k in range(2)]
        ACC, rACC = AR.f32(T, "acc")
        AT = [AR.bf(2 * T, "at%d" % k) for k in range(2)]
        SP_, rSP = AR.f32(256, "stgp")
        SS_, rSS = AR.f32(256, "stgs")
        for k in range(2):
            memset("pool", G[k][0][:, 0:2], 0.0, [G[k][1]])

        def loadw(g):
            s = g % 2
            wload(WG[s][0].rearrange("p (k n) -> p k n", k=8), w_gate[l][:, g * 256:(g + 1) * 256], WG[s][1], "wg%d" % s)
            wload(WU[s][0].rearrange("p (k n) -> p k n", k=8), w_up[l][:, g * 256:(g + 1) * 256], WU[s][1], "wu%d" % s)
            dma("sp", HS[s][0][0:32, :], st_ff[l][:, g * 256:(g + 1) * 256], [], [HS[s][1]], "hs%d" % s)

        def loadwd(g):
            s = g % 2
            wload(WD[s][0].rearrange("p (k n) -> p k n", k=2), w_down[l][g * 256:(g + 1) * 256, :], WD[s][1], "wd%d" % s)

        def down(g):
            s = g % 2
            wd3 = WD[s][0].rearrange("p (k n) -> p k n", k=2)
            at3 = AT[s][0].rearrange("p (j t) -> p j t", j=2)
            for n in range(8):
                for (c0, w) in TT:
                    pb, pr = bank()
                    for jj in range(2):
                        mm(pb[:, 0:w], wd3[:, jj, n * 128:(n + 1) * 128], at3[:, jj, c0:c0 + w], jj == 0, jj == 1,
                           [WD[s][1], AT[s][1]], [pr])
                    resid_add(n, c0, w, pb, pr)

        loadw(0)
        loadwd(0)
        norm_body("nf%d" % l)
        for g in range(NJ // 2):
            s = g % 2
            if g + 1 < NJ // 2:
                loadw(g + 1)
            wg3 = WG[s][0].rearrange("p (k n) -> p k n", k=8)
            wu3 = WU[s][0].rearrange("p (k n) -> p k n", k=8)
            wd3 = WD[s][0].rearrange("p (k n) -> p k n", k=2)
            at3 = AT[s][0].rearrange("p (j t) -> p j t", j=2)
            for jj in range(2):
                j = 2 * g + jj
                gb, rg = G[j % 2]
                ub, ru = U[j % 2]
                pb, pr = bank()
                tr(pb[:, 0:32], HS[s][0][0:32, jj * 128:(jj + 1) * 128], 32, [HS[s][1]], [pr])
                cp("dve", gb[:, 2050:2082].rearrange("p (k s) -> p k s", s=NSEQ),
                   pb[:, 0:32].rearrange("p (s k) -> p k s", k=2), [pr], [rg])
                for ti, (c0, w) in enumerate(TT):
                    pb, pr = bank()
                    for k in range(8):
                        mm(pb[:, 0:w], wg3[:, k, jj * 128:(jj + 1) * 128], XNt[:, k, c0:c0 + w], k == 0, k == 7,
                           [WG[s][1], rXN[k]], [pr])
                    dst = gb[:, 2 + c0:2 + c0 + w] if ti < 4 else gb[:, 2082:2210]
                    cp("act", dst, pb[:, 0:w], [pr], [rg])
                for ti, (c0, w) in enumerate(TT):
                    pb, pr = bank()
                    for k in range(8):
                        mm(pb[:, 0:w], wu3[:, k, jj * 128:(jj + 1) * 128], XNt[:, k, c0:c0 + w], k == 0, k == 7,
                           [WU[s][1], rXN[k]], [pr])
                    cp("dve", ub[:, c0:c0 + w], pb[:, 0:w], [pr], [ru])
                w0, w1, w2 = (vcol("fcw%d" % l, j * 3 + k) for k in range(3))
                bcol = vcol("fcb%d" % l, j)
                act(ACC[:, 0:TP], gb[:, 0:TP], AF.Identity, [rg, rVEC], [rACC], bias=bcol, scale=w0)
                act(ACC[:, TP:T], gb[:, 2050:2178], AF.Identity, [rg, rVEC], [rACC], bias=bcol, scale=w0)
                stt(ACC[:, 0:TP], gb[:, 1:1 + TP], w1, ACC[:, 0:TP], ALU.mult, ALU.add, [rg, rVEC, rACC], [rACC])
                stt(ACC[:, TP:T], gb[:, 2066:2194], w1, ACC[:, TP:T], ALU.mult, ALU.add, [rg, rVEC, rACC], [rACC])
                stt(ACC[:, 0:TP], gb[:, 2:2 + TP], w2, ACC[:, 0:TP], ALU.mult, ALU.add, [rg, rVEC, rACC], [rACC])
                stt(ACC[:, TP:T], gb[:, 2082:2210], w2, ACC[:, TP:T], ALU.mult, ALU.add, [rg, rVEC, rACC], [rACC])
                act(ACC, ACC, AF.Silu, [rACC], [rACC])
                tt_("pool", at3[:, jj, :], ACC, ub, ALU.mult, [rACC, ru], [AT[s][1]])
                pb, pr = bank()
                tr(pb[0:2, 0:128], gb[:, 2048:2050], 128, [rg], [pr])
                cp("act", SP_[0:2, jj * 128:(jj + 1) * 128], pb[0:2, 0:128], [pr], [rSP])
                pb, pr = bank()
                tr(pb[0:32, 0:128], gb[:, 2178:2210], 128, [rg], [pr])
                cp("act", SS_[0:32, jj * 128:(jj + 1) * 128], pb[0:32, 0:128], [pr], [rSS])
            dma("sp", ff_p[l][:, g * 256:(g + 1) * 256], SP_[0:2, :], [rSP], [], "sp")
            dma("sp", ff_s[l][:, :, g * 256:(g + 1) * 256].rearrange("s t d -> t s d"), SS_[0:32, :], [rSS], [], "ss")
            if g > 0:
                down(g - 1)
            if g + 1 < NJ // 2:
                loadwd(g + 1)
        down(NJ // 2 - 1)

    def even_phase(i, stop=None):
        S.barrier()
        AR.reset()
        UW = 30 + TP + 480 + TS
        UWW = (UW + 1) // 2
        WA = [AR.bf(8 * 256, "wa%d" % k) for k in range(2)]
        WO, rWO = AR.bf(4 * 1024, "woA")
        RG, rUP = AR.f32(UWW + 2048 + 1024, "up_hst_diag_co")
        UP = RG[:, 0:UWW].bitcast(BF16)[:, 0:UW]
        HST = [RG[:, UWW + 512 * k:UWW + 512 * (k + 1)] for k in range(4)]
        DIAG = RG[:, UWW + 2048:UWW + 3072].bitcast(BF16).rearrange("p (k n) -> p k n", k=16)
        CO = [RG[:, 1088 * c:1088 * (c + 1)].bitcast(BF16) for c in range(4)]
        UF, rUF = AR.f32(160, "uf")
        SIG = [AR.f32(512, "sig%d" % k) for k in range(1)] * 2
        CV = [AR.f32(T, "cv%d" % c) for c in range(4)]
        MU, rMU = AR.f32(T, "mu")
        CVB = [AR.bf(T, "cvb%d" % k) for k in range(2)]
        STG, rSTG = AR.f32(512, "stgA")
        STS, rSTS = AR.f32(512, "stsA")
        memset("pool", UP[:, 0:30], 0.0, [rUP])
        for k in range(4):
            rows = 128 if k < 3 else 96
            dma("sp", HST[k][0:rows, :], st_ca[i][k * 128:k * 128 + rows, :], [], [rUP], "up")
        dma("pool", WO.rearrange("p (k n) -> p k n", k=4), w_out_e[i][0:512, :].rearrange("(k p) n -> p k n", p=128),
            [], [rWO], "woA")
        dma("sp", ca_s[i][:, 0:22, :], st_ca[i].rearrange("(s k) c -> s k c", k=30)[:, 8:30, :], [], [], "dd")

        def loadA(c):
            s = c % 2
            w3 = WA[s][0].rearrange("p (k n) -> p k n", k=8)
            dma("pool", w3[:, :, 0:128], w_in_e[i][:, c * 128:(c + 1) * 128].rearrange("(k p) n -> p k n", p=128),
                [], [WA[s][1]], "wa%d" % s)
            dma("pool", w3[:, :, 128:256],
                w_in_e[i][:, 512 + c * 128:512 + (c + 1) * 128].rearrange("(k p) n -> p k n", p=128),
                [], [WA[s][1]], "wa%d" % s)

        loadA(0)
        norm_body("nme%d" % i)
        SB = 30 + TP
        for c in range(4):
            s = c % 2
            if c < 3:
                loadA(c + 1)
            w3 = WA[s][0].rearrange("p (k n) -> p k n", k=8)
            hdst = UP[:, SB:SB + 480].rearrange("p (k s) -> p s k", s=NSEQ)
            for k in range(4):
                rows = 128 if k < 3 else 96
                pb, pr = bank()
                tr(pb[:, 0:rows], HST[k][0:rows, c * 128:(c + 1) * 128], rows, [rUP], [pr])
                r0 = k * 128
                r = r0
                while r < r0 + rows:
                    sq_, kk = divmod(r, 30)
                    n = min(30 - kk, r0 + rows - r)
                    cp("dve", hdst[:, sq_, kk:kk + n], pb[:, r - r0:r - r0 + n], [pr], [rUP])
                    r += n
            for ti, (c0, w) in enumerate(TT):
                pv, prv = bank()
                pg, prg = bank()
                for k in range(8):
                    mm(pv[:, 0:w], w3[:, k, 0:128], XNt[:, k, c0:c0 + w], k == 0, k == 7, [WA[s][1], rXN[k]], [prv])
                for k in range(8):
                    mm(pg[:, 0:w], w3[:, k, 128:256], XNt[:, k, c0:c0 + w], k == 0, k == 7, [WA[s][1], rXN[k]], [prg])
                sg, rsg = SIG[ti % 2]
                act(sg[:, 0:w], pg[:, 0:w], AF.Sigmoid, [prg], [rsg])
                dst = UP[:, 30 + c0:30 + c0 + w] if ti < 4 else UP[:, SB + 480:UW]
                tt_("dve", dst, pv[:, 0:w], sg[:, 0:w], ALU.mult, [prv, rsg], [rUP])
                if ti == 3:
                    tt_("dve", UF[:, 0:30], pv[:, 482:512], sg[:, 482:512], ALU.mult, [prv, rsg], [rUF])
                if ti == 4:
                    tt_("dve", UF[:, 32:160], pv[:, 0:128], sg[:, 0:128], ALU.mult, [prv, rsg], [rUF])
            pb, pr = bank()
            tr(pb[0:30, 0:128], UF[:, 0:30], 128, [rUF], [pr])
            cp("act", STG[0:30, c * 128:(c + 1) * 128], pb[0:30, 0:128], [pr], [rSTG])
            pb, pr = bank()
            tr(pb[:, 0:128], UF[:, 32:160], 128, [rUF], [pr])
            cp("act", STS[:, c * 128:(c + 1) * 128], pb[:, 0:128], [pr], [rSTS])
            cv, rcv = CV[c]
            bcol = vcol("cab%d" % i, c)
            banks = [bank(hold=True) for _ in range(5)]
            for half in range(2):
                taps = list(range(16 * half, min(16 * half + 16, 31)))
                for j, k in enumerate(taps):
                    ts_("dve", DIAG[:, j, :], cb("ident"), vcol("caw%d" % i, c * 31 + k), ALU.mult, [rCB, rVEC], [rUP])
                for ti, (c0, w) in enumerate(TT):
                    pb, pr = banks[ti]
                    for j, k in enumerate(taps):
                        src = UP[:, k + c0:k + c0 + w] if ti < 4 else UP[:, SB + 16 * k:SB + 16 * k + 128]
                        mm(pb[:, 0:w], DIAG[:, j, :], src, k == 0, k == 30, [rUP], [pr])
            for ti, (c0, w) in enumerate(TT):
                pb, pr = banks[ti]
                act(cv[:, c0:c0 + w], pb[:, 0:w], AF.Identity, [pr, rVEC], [rcv], bias=bcol, scale=1.0)
                release(pb)
        if stop == "A0":
            return
        dma("sp", ca_p[i], STG[0:30, :], [rSTG], [], "stg")
        dma("sp", ca_s[i][:, 22:30, :].rearrange("s t d -> t s d"), STS[:, :], [rSTS], [], "sts")
        if stop == "A1":
            return
        banks = [bank(hold=True) for _ in range(5)]
        for c in range(4):
            b, rb = CVB[c % 2]
            cp("pool", b, CV[c][0], [CV[c][1]], [rb])
            for j, (c0, w) in enumerate(TT):
                mm(banks[j][0][:, 0:w], cb("o512"), b[:, c0:c0 + w], c == 0, c == 3, [rCB, rb], [banks[j][1]])
        for j, (c0, w) in enumerate(TT):
            cp("act", MU[:, c0:c0 + w], banks[j][0][:, 0:w], [banks[j][1]], [rMU])
            release(banks[j][0])
        for c in range(4):
            tt_("pool", CV[c][0], CV[c][0], MU, ALU.subtract, [CV[c][1], rMU], [CV[c][1]])
        banks = [bank(hold=True) for _ in range(5)]
        for c in range(4):
            b, rb = CVB[c % 2]
            act(b, CV[c][0], AF.Square, [CV[c][1]], [rb])
            for j, (c0, w) in enumerate(TT):
                mm(banks[j][0][:, 0:w], cb("o512"), b[:, c0:c0 + w], c == 0, c == 3, [rCB, rb], [banks[j][1]])
        for j, (c0, w) in enumerate(TT):
            rstd_from(MU[:, c0:c0 + w], banks[j][0][:, 0:w], [banks[j][1]], [rMU])
            release(banks[j][0])
        for c in range(4):
            tt_("dve", CV[c][0], CV[c][0], MU, ALU.mult, [CV[c][1], rMU], [CV[c][1]])
            act(CO[c], CV[c][0], AF.Silu, [CV[c][1], rVEC], [rUP], bias=vcol("lnb%d" % i, c),
                scale=vcol("lng%d" % i, c))
        wo3 = WO.rearrange("p (k n) -> p k n", k=4)
        for n in range(8):
            for (c0, w) in TT:
                pb, pr = bank()
                for k in range(4):
                    mm(pb[:, 0:w], wo3[:, k, n * 128:(n + 1) * 128], CO[k][:, c0:c0 + w], k == 0, k == 3,
                       [rWO, rUP], [pr])
                resid_add(n, c0, w, pb, pr)

        if stop == "A":
            return
        S.barrier()
        AR.reset()
        WQ = [AR.bf(8 * 128, "wq%d" % k) for k in range(2)]
        WO2, rWO2 = AR.bf(4 * 1024, "woB")
        COS, rCOS = AR.f32(T, "cos")
        SIN, rSIN = AR.f32(T, "sin")
        OT = [COS[:, 0:1088].bitcast(BF16), COS[:, 1088:2176].bitcast(BF16),
              SIN[:, 0:1088].bitcast(BF16), SIN[:, 1088:2176].bitcast(BF16)]
        rOT = [rCOS, rCOS, rSIN, rSIN]
        QR = [AR.bf(T, "qr%d" % c) for c in range(4)]
        KR, rKR = AR.bf(T, "kr")
        KF, rKF = AR.f32(256, "kf")
        VB, rVB = AR.bf(17 * 128, "vb")
        VF, rVF = AR.f32(256, "vf")
        TMP = [AR.f32(512, "tmpb%d" % k) for k in range(6)]
        TI, rTI = AR.f32(512, "ti")
        SQB, rSQB = AR.bf(512, "sqb")
        QNB, rQNB = AR.f32(512, "qnb")
        EB = [AR.bf(256, "eb%d" % k) for k in range(2)]
        REC = [AR.f32(128, "rec%d" % k) for k in range(2)]
        KHS = [AR.f32(128, "khs%d" % k) for k in range(2)]
        KHT, rKHT = AR.bf(NSEQ * 128, "kht")
        VH, rVH = AR.bf(NSEQ * 128, "vh")
        OA, rOA = AR.f32(128, "oa")
        KO, rKO = AR.f32(256, "ko")
        dma("pool", WO2.rearrange("p (k n) -> p k n", k=4), w_out_e[i][512:1024, :].rearrange("(k p) n -> p k n", p=128),
            [], [rWO2], "woB")
        vh3 = VH.rearrange("p (s d) -> p s d", s=NSEQ)
        dma("pool", vh3, st_v[i].rearrange("s k d -> k s d"), [], [rVH], "vh")
        dma("sp", wk_s[i][:, 0:120, :], st_k[i][:, 8:128, :], [], [], "ddk")
        dma("sp", wv_s[i][:, 0:120, :], st_v[i][:, 8:128, :], [], [], "ddk")
        ti_i = TI.bitcast(I32)
        for (c0, w) in TT:
            pz, rpz = TMP[1]
            dma("sp", pz[:, 0:w], pos_d[:, c0:c0 + w].partition_broadcast(128), [], [rpz], "pos")
            for (dst, rdst, ph) in ((SIN, rSIN, 0.0), (COS, rCOS, math.pi / 2)):
                a, ra = TMP[0]
                c_, rc_ = TMP[2]
                ts_("dve", a[:, 0:w], pz[:, 0:w], vcol("invf"), ALU.mult, [rpz, rVEC], [ra], s2=ph, op1=ALU.add)
                ts_("dve", c_[:, 0:w], a[:, 0:w], 1.0 / (2 * math.pi), ALU.mult, [ra], [rc_])
                cp("dve", ti_i[:, 0:w], c_[:, 0:w], [rc_], [rTI])
                cp("dve", c_[:, 0:w], ti_i[:, 0:w], [rTI], [rc_])
                stt(a[:, 0:w], c_[:, 0:w], -2 * math.pi, a[:, 0:w], ALU.mult, ALU.add, [rc_, ra], [ra])
                ts_("dve", c_[:, 0:w], a[:, 0:w], math.pi, ALU.is_gt, [ra], [rc_], s2=-2 * math.pi, op1=ALU.mult)
                tt_("dve", a[:, 0:w], a[:, 0:w], c_[:, 0:w], ALU.add, [ra, rc_], [ra])
                act(dst[:, c0:c0 + w], a[:, 0:w], AF.Sin, [ra], [rdst])
        if stop == "B0":
            return
        kht3 = KHT.rearrange("p (s k) -> p s k", s=NSEQ)
        for sq_ in range(NSEQ):
            hb_, rhb = KHS[sq_ % 2]
            dma("sp", hb_, st_k[i][sq_], [], [rhb], "khs%d" % (sq_ % 2))
            pb, pr = bank()
            tr(pb[:, 0:128], hb_, 128, [rhb], [pr])
            cp("act", kht3[:, sq_, :], pb[:, 0:128], [pr], [rKHT])

        def loadQ(idx):
            s = idx % 2
            col = 1024 + idx * 128
            wload(WQ[s][0].rearrange("p (k n) -> p k n", k=8), w_in_e[i][:, col:col + 128], WQ[s][1], "wq%d" % s)

        loadQ(0)
        for idx in range(5):
            s = idx % 2
            loadQ(idx + 1)
            w3 = WQ[s][0].rearrange("p (k n) -> p k n", k=8)
            gcol = vcol("qg%d" % i) if idx < 4 else vcol("kg%d" % i)
            for ti, (c0, w) in enumerate(TT):
                pb, pr = bank()
                for k in range(8):
                    mm(pb[:, 0:w], w3[:, k, :], XNt[:, k, c0:c0 + w], k == 0, k == 7, [WQ[s][1], rXN[k]], [pr])
                act(SQB[:, 0:w], pb[:, 0:w], AF.Square, [pr], [rSQB])
                p2, pr2 = bank()
                mm(p2[:, 0:w], cb("bd64"), SQB[:, 0:w], True, True, [rCB, rSQB], [pr2])
                r_, rr_ = TMP[3]
                rstd_from(r_[:, 0:w], p2[:, 0:w], [pr2], [rr_])
                stt(QNB[:, 0:w], pb[:, 0:w], gcol, r_[:, 0:w], ALU.mult, ALU.mult, [pr, rVEC, rr_], [rQNB])
                p3, pr3 = bank()
                mm(p3[:, 0:w], cf("pmat"), QNB[:, 0:w], True, True, [rCF, rQNB], [pr3])
                t1, rt1 = TMP[4]
                t2, rt2 = TMP[5]
                tt_("dve", t1[:, 0:w], p3[:, 0:w], SIN[:, c0:c0 + w], ALU.mult, [pr3, rSIN], [rt1])
                tt_("pool", t2[:, 0:w], QNB[:, 0:w], COS[:, c0:c0 + w], ALU.mult, [rQNB, rCOS], [rt2])
                if idx < 4:
                    tt_("pool", QR[idx][0][:, c0:c0 + w], t1[:, 0:w], t2[:, 0:w], ALU.add, [rt1, rt2], [QR[idx][1]])
                else:
                    tt_("pool", KR[:, c0:c0 + w], t1[:, 0:w], t2[:, 0:w], ALU.add, [rt1, rt2], [rKR])
                    if ti == 3:
                        tt_("pool", KF[:, 0:128], t1[:, 384:512], t2[:, 384:512], ALU.add, [rt1, rt2], [rKF])
                    if ti == 4:
                        tt_("pool", KF[:, 128:256], t1[:, 0:128], t2[:, 0:128], ALU.add, [rt1, rt2], [rKF])
        if stop == "B1":
            return
        pb, pr = bank()
        tr(pb[:, 0:128], KF[:, 0:128], 128, [rKF], [pr])
        tr(pb[:, 128:256], KF[:, 128:256], 128, [rKF], [pr])
        cp("act", KO, pb[:, 0:256], [pr], [rKO])
        dma("sp", wk_p[i], KO[:, 0:128], [rKO], [], "ko")
        dma("sp", wk_s[i][:, 120:128, :].rearrange("s t d -> t s d"), KO[:, 128:256], [rKO], [], "ko")
        if stop == "B15":
            return
        s = 5 % 2
        w3 = WQ[s][0].rearrange("p (k n) -> p k n", k=8)
        vb3 = VB.rearrange("p (b d) -> p b d", b=17)
        for blk in range(17):
            pb, pr = bank()
            for k in range(8):
                mm(pb[:, 0:128], XNt[:, k, blk * 128:(blk + 1) * 128], w3[:, k, :], k == 0, k == 7, [WQ[s][1], rXN[k]], [pr])
            cp("act", vb3[:, blk, :], pb[:, 0:128], [pr], [rVB])
            if blk == 15:
                cp("dve", VF[:, 0:128], pb[:, 0:128], [pr], [rVF])
            if blk == 16:
                cp("dve", VF[:, 128:256], pb[:, 0:128], [pr], [rVF])
        dma("sp", wv_p[i], VF[:, 0:128], [rVF], [], "vf")
        dma("sp", wv_s[i][:, 120:128, :].rearrange("s t d -> t s d"), VF[:, 128:256], [rVF], [], "vf")

        if stop == "B2":
            return
        def attend(blk):
            c0 = blk * 128
            for c in range(4):
                for h2 in range(2):
                    ps_, prs = bank()
                    lo, hi = 64 * h2, 64 * h2 + 64
                    qv = QR[c][0][lo:hi, c0:c0 + 128]
                    mm(ps_[:, 128:256], KR[lo:hi, c0:c0 + 128], qv, True, False, [rKR, QR[c][1]], [prs])
                    mm(ps_[:, 128:256], cb("ident"), cb("mcur") if blk < 16 else cb("msnew"), False, True, [rCB], [prs])
                    if 0 < blk < 16:
                        mm(ps_[:, 0:128], KR[lo:hi, c0 - 128:c0], qv, True, False, [rKR, QR[c][1]], [prs])
                        mm(ps_[:, 0:128], cb("ident"), cb("mprev"), False, True, [rCB], [prs])
                    if blk == 16:
                        for sq_ in range(NSEQ):
                            mm(scol(ps_[:, 0:128], sq_), kht3[lo:hi, sq_, :], scol(QR[c][0][lo:hi, c0:c0 + 128], sq_),
                               True, True, [rKHT, QR[c][1]], [prs])
                    e, re = EB[(c * 2 + h2) % 2]
                    if blk < 16:
                        first = 128 if blk == 0 else 0
                        act(e[:, first:256], ps_[:, first:256], AF.Exp, [prs], [re], scale=0.125)
                    else:
                        act(e[:, 128:256], ps_[:, 128:256], AF.Exp, [prs], [re], scale=0.125)
                        t1, rt1 = TMP[4]
                        tt_("dve", t1[:, 0:128], ps_[:, 0:128], cb("mshist"), ALU.add, [prs, rCB], [rt1])
                        act(e[:, 0:128], t1[:, 0:128], AF.Exp, [rt1], [re], scale=0.125)
                    po, pro = bank()
                    rc, rrc = REC[(c * 2 + h2) % 2]
                    esk = MISC[lo:hi, 2 + 4 * i + c:3 + 4 * i + c]
                    if blk < 16:
                        if blk > 0:
                            mm(po[:, 0:128], vb3[:, blk - 1, :], e[:, 0:128], True, False, [rVB, re], [pro])
                        mm(po[:, 0:128], vb3[:, blk, :], e[:, 128:256], blk == 0, True, [rVB, re], [pro])
                        if blk > 0:
                            mm(po[:, 128:256], cb("ones"), e[:, 0:128], True, False, [rCB, re], [pro])
                        mm(po[:, 128:256], cb("ones"), e[:, 128:256], blk == 0, True, [rCB, re], [pro])
                        ts_("dve", rc[lo:hi, :], po[lo:hi, 128:256], esk, ALU.add, [pro, rMISC], [rrc])
                        S.op("dve", lambda e_, rc=rc, lo=lo, hi=hi: e_.reciprocal(out=rc[lo:hi, :], in_=rc[lo:hi, :]),
                             reads=[rrc], writes=[rrc])
                        tt_("dve", OT[c][lo:hi, c0:c0 + 128], po[lo:hi, 0:128], rc[lo:hi, :], ALU.mult,
                            [pro, rrc], [rOT[c]])
                    else:
                        mm(po[:, 0:128], vb3[:, 16, :], e[:, 128:256], True, True, [rVB, re], [pro])
                        mm(po[:, 128:256], cb("ones"), e[:, 128:256], True, True, [rCB, re], [pro])
                        for sq_ in range(NSEQ):
                            mm(scol(po[:, 256:384], sq_), vh3[:, sq_, :], scol(e[:, 0:128], sq_), True, True, [rVH, re], [pro])
                        mm(po[:, 384:512], cb("ones"), e[:, 0:128], True, True, [rCB, re], [pro])
                        cp("act", OA[lo:hi, :], po[lo:hi, 256:384], [pro], [rOA])
                        t2, rt2 = TMP[5]
                        cp("act", t2[lo:hi, 0:128], po[lo:hi, 384:512], [pro], [rt2])
                        stt(rc[lo:hi, :], po[lo:hi, 128:256], esk, t2[lo:hi, 0:128], ALU.add, ALU.add,
                            [pro, rMISC, rt2], [rrc])
                        S.op("dve", lambda e_, rc=rc, lo=lo, hi=hi: e_.reciprocal(out=rc[lo:hi, :], in_=rc[lo:hi, :]),
                             reads=[rrc], writes=[rrc])
                        tt_("dve", OA[lo:hi, :], OA[lo:hi, :], po[lo:hi, 0:128], ALU.add, [rOA, pro], [rOA])
                        tt_("dve", OT[c][lo:hi, c0:c0 + 128], OA[lo:hi, :], rc[lo:hi, :], ALU.mult,
                            [rOA, rrc], [rOT[c]])

        for blk in range(17):
            if stop == "B3" and blk == 16:
                return
            attend(blk)
        wo3 = WO2.rearrange("p (k n) -> p k n", k=4)
        for n in range(8):
            for (c0, w) in TT:
                pb, pr = bank()
                for k in range(4):
                    mm(pb[:, 0:w], wo3[:, k, n * 128:(n + 1) * 128], OT[k][:, c0:c0 + w], k == 0, k == 3,
                       [rWO2, rOT[k]], [pr])
                resid_add(n, c0, w, pb, pr)

    def odd_phase(i):
        S.barrier()
        AR.reset()
        TW = 256
        TILES = [(k * TW, TW) for k in range(8)] + [(TP, TS)]
        WIN, rWIN = AR.bf(10 * 1024, "win")
        rWINt = [Res("win%d" % t) for t in range(10)]
        WOo, rWOo = AR.bf(4 * 1024, "woo")
        WDT, rWDT = AR.bf(8 * 64, "wdt")
        DTT, rDTT = AR.f32(17 * 64, "dtt")
        D2, rD2 = AR.f32(TW, "d2")
        XPAD = [AR.f32(3 + TW, "xpad%d" % k) for k in range(6)]
        ACC = [AR.f32(TW, "acco%d" % k) for k in range(1)] * 2
        XS = [[AR.bf(TW, "xs%d_%d" % (p, k)) for k in range(4)] for p in range(2)]
        BT = [AR.bf(TW, "bt%d" % p) for p in range(2)]
        CT = [AR.bf(TW, "ct%d" % p) for p in range(2)]
        ZS = [[AR.bf(TW, "zs%d_%d" % (p, k)) for k in range(4)] for p in range(2)]
        YN = [AR.bf(TW, "yn%d" % k) for k in range(4)]
        TLP, rTLP = AR.f32(6 * 3, "tlp")
        TLS, rTLS = AR.f32(6 * 48, "tls")
        STT_, rSTT = AR.f32(768, "sttail")
        XDT = [AR.bf(512, "xdt%d" % p) for p in range(2)]
        BTOK = [AR.bf(128, "btok%d" % p) for p in range(2)]
        DEND = [AR.f32(8, "dend%d" % p) for p in range(2)]
        CBM = [AR.f32(128, "cbm%d" % p) for p in range(2)]
        WT_ = [[AR.bf(512, "wt%d_%d" % (p, h)) for h in range(2)] for p in range(2)]
        CS_ = [[AR.bf(512, "cs%d_%d" % (p, h)) for h in range(2)] for p in range(2)]
        CDEC = [[AR.f32(4, "cdec%d_%d" % (p, h)) for h in range(2)] for p in range(2)]
        RR = [AR.f32(512, "rr%d" % h) for h in range(2)]
        DEC = [AR.bf(512, "dec%d" % h) for h in range(2)]
        EXPA = [AR.f32(512, "expa%d" % h) for h in range(2)]
        XDD, rXDD = AR.bf(256, "xdd")
        YV2 = [[AR.f32(128, "yv%d_%d" % (p, k)) for k in range(4)] for p in range(2)]
        SQY, rSQY = AR.bf(512, "sqy")
        RSTD, rRSTD = AR.f32(128, "rstd")
        u0 = AR.off
        HT, rHT = AR.f32(512, "ht")
        HTB, rHTB = AR.bf(512, "htb")
        TMPH, rTMPH = AR.f32(256, "tmph")
        SOUT, rSOUT = AR.f32(512, "sout")
        u1 = AR.off
        AR.off = u0
        H0 = [AR.f32(256, "h0%d" % k) for k in range(2)]
        H0T = [AR.bf(256, "h0t%d" % k) for k in range(2)]
        HFIN = [AR.f32(256, "hfin%d" % k) for k in range(3)]
        BM = [AR.bf(128, "bm%d" % k) for k in range(2)]
        CD, rCD = AR.f32(32, "cd")
        DTAX, rDTAX = AR.f32(256, "dtax")
        YOFF, rYOFF = AR.f32(512, "yoff")
        AR.off = max(AR.off, u1)
        dtt3 = DTT.rearrange("p (c h) -> p c h", c=17)

        win3 = WIN.rearrange("p (t k n) -> p t k n", t=10, k=8)
        wo3 = WOo.rearrange("p (k n) -> p k n", k=4)

        def produce(g, ti, cidx):
            c0, w = TILES[ti]
            par = ti % 2
            sample = ti == 8
            for k in range(6):
                xp_, rxp = XPAD[k]
                if sample:
                    ccol = cidx[k] * 128
                    dma("sp", STT_[0:48, 0:128], st_cc[i][:, ccol:ccol + 128], [], [rSTT], "stt")
                    pb, pr = bank()
                    tr(pb[:, 0:48], STT_[0:48, 0:128], 48, [rSTT], [pr])
                    cp("dve", xp_[:, 0:48].rearrange("p (k s) -> p k s", s=NSEQ),
                       pb[:, 0:48].rearrange("p (s k) -> p k s", k=3), [pr], [rxp])
                pb, pr = bank()
                for kk in range(8):
                    mm(pb[:, 0:w], win3[:, k, kk, :], XNt[:, kk, c0:c0 + w], kk == 0, kk == 7, [rWINt[k], rXN[kk]], [pr])
                doff = 48 if sample else 3
                cp("act", xp_[:, doff:doff + w], pb[:, 0:w], [pr], [rxp])
                step = 16 if sample else 1
                acc, racc = ACC[k % 2]
                act(acc[:, 0:w], xp_[:, 0:w], AF.Identity, [rxp, rVEC], [racc], bias=vcol("ccb%d" % i, cidx[k]),
                    scale=vcol("ccw%d" % i, cidx[k] * 4))
                for kq in range(1, 4):
                    stt(acc[:, 0:w], xp_[:, kq * step:kq * step + w], vcol("ccw%d" % i, cidx[k] * 4 + kq), acc[:, 0:w],
                        ALU.mult, ALU.add, [rxp, rVEC, racc], [racc])
                dst, rdst = (XS[par][k] if k < 4 else (BT[par] if k == 4 else CT[par]))
                act(dst[:, 0:w], acc[:, 0:w], AF.Silu, [racc], [rdst])
                if ti == 7:
                    cp("pool", TLP[:, k * 3:(k + 1) * 3], xp_[:, TW:TW + 3], [rxp], [rTLP])
                if sample:
                    cp("pool", TLS[:, k * 48:(k + 1) * 48], xp_[:, 48 + 80:48 + 128], [rxp], [rTLS])
                elif ti < 7:
                    cp("pool", xp_[:, 0:3], xp_[:, TW:TW + 3], [rxp], [rxp])
            for q in range(4):
                pb, pr = bank()
                for kk in range(8):
                    mm(pb[:, 0:w], win3[:, 6 + q, kk, :], XNt[:, kk, c0:c0 + w], kk == 0, kk == 7, [rWINt[6 + q], rXN[kk]], [pr])
                act(ZS[par][q][0][:, 0:w], pb[:, 0:w], AF.Silu, [pr], [ZS[par][q][1]])

        def prep(g, ch):
            ti, q = (ch // 2, ch % 2) if ch < 16 else (8, 0)
            par, cp_ = ti % 2, ch % 2
            cs_ = slice(q * 128, q * 128 + 128)
            sample = ch == 16
            tri = cf("tri_s") if sample else cf("tri")
            ut = cf("ut_s") if sample else cf("ut")
            cbm = cb("cbm_s") if sample else cb("cbm_p")
            dt_g = dtt3[:, ch, 8 * g:8 * g + 8]
            dta_g = dtt3[:, ch, 32 + 8 * g:32 + 8 * g + 8]
            bt, rbt = BT[par]
            ct, rct = CT[par]
            xdt, rxdt = XDT[cp_]
            btok, rbtok = BTOK[cp_]
            dend, rdend = DEND[cp_]
            cbmb, rcbm = CBM[cp_]
            pb, pr = bank()
            mm(pb[:, 0:128], bt[:, cs_], ct[:, cs_], True, True, [rbt, rct], [pr])
            tt_("dve", cbmb, pb[:, 0:128], cbm, ALU.mult, [pr, rCB], [rcbm])
            pb, pr = bank()
            for cc in range(4):
                mm(pb[:, cc * 128:(cc + 1) * 128], XS[par][cc][0][:, cs_], cb("ident"), True, True, [XS[par][cc][1], rCB], [pr])
            tt_("dve", xdt.rearrange("p (h d) -> p h d", h=8), pb[:].rearrange("p (h d) -> p h d", h=8),
                dt_g.unsqueeze(2).broadcast_to([128, 8, 64]), ALU.mult, [pr, rDTT], [rxdt])
            pb, pr = bank()
            mm(pb[:, 0:128], bt[:, cs_], cb("ident"), True, True, [rbt, rCB], [pr])
            cp("act", btok, pb[:, 0:128], [pr], [rbtok])
            pb, pr = bank()
            mm(pb[:, 0:8], ut, dta_g, True, True, [rCF, rDTT], [pr])
            act(dend, pb[:, 0:8], AF.Exp, [pr], [rdend])
            for hh in range(2):
                hs = slice(4 * hh, 4 * hh + 4)
                rr, rrr = RR[hh]
                dec, rdec = DEC[hh]
                expa, rexpa = EXPA[hh]
                wt, rwt = WT_[cp_][hh]
                cs2, rcs = CS_[cp_][hh]
                cdec, rcdec = CDEC[cp_][hh]
                tt_("pool", rr.rearrange("p (h l) -> p h l", h=4), tri.unsqueeze(1).broadcast_to([128, 4, 128]),
                    dta_g[:, hs].unsqueeze(2).broadcast_to([128, 4, 128]), ALU.mult, [rCF, rDTT], [rrr])
                pd, prd = bank()
                pa, pra = bank()
                mm(pd[:], ut, rr, True, True, [rCF, rrr], [prd])
                mm(pa[:], cf("ones"), rr, True, True, [rCF, rrr], [pra])
                act(dec, pd[:], AF.Exp, [prd], [rdec])
                act(expa, pa[:], AF.Exp, [pra], [rexpa])
                tt_("dve", wt.rearrange("p (h l) -> p h l", h=4), dec.rearrange("p (h l) -> p h l", h=4),
                    cbmb.unsqueeze(1).broadcast_to([128, 4, 128]), ALU.mult, [rdec, rcbm], [rwt])
                tt_("pool", cs2.rearrange("p (h l) -> p h l", h=4), expa.rearrange("p (h l) -> p h l", h=4),
                    ct[:, cs_].unsqueeze(1).broadcast_to([128, 4, 128]), ALU.mult, [rexpa, rct], [rcs])
                cp("pool", cdec, expa.rearrange("p (h l) -> p h l", h=4)[:, :, 127], [rexpa], [rcdec])

        def stage_b(g, ch):
            ti, q = (ch // 2, ch % 2) if ch < 16 else (8, 0)
            par, cp_ = ti % 2, ch % 2
            cs_ = slice(q * 128, q * 128 + 128)
            sample = ch == 16
            first = ch == 0
            YV = YV2[cp_]
            dta_g = dtt3[:, ch, 32 + 8 * g:32 + 8 * g + 8]
            xdt, rxdt = XDT[cp_]
            btok, rbtok = BTOK[cp_]
            dend, rdend = DEND[cp_]
            for hh in range(2):
                hs = slice(4 * hh, 4 * hh + 4)
                wt, rwt = WT_[cp_][hh]
                cs2, rcs = CS_[cp_][hh]
                cdec, rcdec = CDEC[cp_][hh]
                py, pry = bank(hold=True)
                use_off = (not sample) and (not first)
                for h in range(4):
                    j2 = (4 * hh + h) // 2
                    mm(py[:, h * 128:(h + 1) * 128], xdt[:, j2 * 128:(j2 + 1) * 128], wt[:, h * 128:(h + 1) * 128],
                       True, not use_off, [rxdt, rwt], [pry])
                    if use_off:
                        mm(py[:, h * 128:(h + 1) * 128], HTB[:, j2 * 128:(j2 + 1) * 128], cs2[:, h * 128:(h + 1) * 128],
                           False, True, [rHTB, rcs], [pry])
                tt_("pool", XDD.rearrange("p (h d) -> p h d", h=4), xdt.rearrange("p (h d) -> p h d", h=8)[:, hs, :],
                    dend[:, hs].unsqueeze(2).broadcast_to([128, 4, 64]), ALU.mult, [rxdt, rdend], [rXDD])
                if sample:
                    tt_("pool", DTAX.rearrange("p (h d) -> p h d", h=4),
                        dta_g[:, hs].unsqueeze(2).broadcast_to([128, 4, 64]),
                        cf("ones", 64).unsqueeze(1).broadcast_to([128, 4, 64]), ALU.mult, [rDTT, rCF], [rDTAX])
                    pc, prc = bank()
                    for pr_i in range(2):
                        mm(pc[:, pr_i * 16:(pr_i + 1) * 16], DTAX[:, pr_i * 128:(pr_i + 1) * 128], cf("seqmask", 16),
                           True, True, [rDTAX, rCF], [prc])
                    act(CD, pc[:, 0:32], AF.Exp, [prc], [rCD])
                    po_, pro_ = bank(hold=True)
                    hd0 = 8 * g + 4 * hh
                    def h0_load(b):
                        src = st_ssm[i][b, hd0:hd0 + 4].rearrange("(j h2) p n -> (h2 p) j n", h2=2)
                        dma("sp", H0[b % 2][0].rearrange("p (j n) -> p j n", j=2), src, [], [H0[b % 2][1]], "h0%d" % (b % 2))

                    def hf_store(b):
                        hf, rhf = HFIN[b % 3]
                        dst = ss_s[i][b, hd0:hd0 + 4].rearrange("(j h2) p n -> (h2 p) j n", h2=2)
                        dma("sp", dst, hf.rearrange("p (j n) -> p j n", j=2), [rhf], [], "hf%d" % (b % 3))

                    h0_load(0)
                    for b in range(NSEQ):
                        h0, rh0 = H0[b % 2]
                        h0t, rh0t = H0T[b % 2]
                        hf, rhf = HFIN[b % 3]
                        bm, rbm = BM[b % 2]
                        pt, prt = bank()
                        for pr_i in range(2):
                            tr(pt[:, pr_i * 128:(pr_i + 1) * 128], h0[:, pr_i * 128:(pr_i + 1) * 128], 128, [rh0], [prt])
                        cp("act", h0t, pt[:, 0:256], [prt], [rh0t])
                        for h in range(4):
                            pr_i = h // 2
                            mm(scol(po_[:, h * 128:(h + 1) * 128], b), h0t[:, pr_i * 128:(pr_i + 1) * 128],
                               scol(cs2[:, h * 128:(h + 1) * 128], b), True, True, [rh0t, rcs], [pro_])
                        ts_("pool", bm, btok, cf("seqmask", 16)[:, b:b + 1], ALU.mult, [rbtok, rCF], [rbm])
                        pf, prf = bank()
                        for pr_i in range(2):
                            mm(pf[:, pr_i * 128:(pr_i + 1) * 128], XDD[:, pr_i * 128:(pr_i + 1) * 128], bm, True, True,
                               [rXDD, rbm], [prf])
                        for pr_i in range(2):
                            stt(hf[:, pr_i * 128:(pr_i + 1) * 128], h0[:, pr_i * 128:(pr_i + 1) * 128],
                                CD[:, pr_i * 16 + b:pr_i * 16 + b + 1], pf[:, pr_i * 128:(pr_i + 1) * 128],
                                ALU.mult, ALU.add, [rh0, rCD, prf], [rhf])
                        if b + 1 < NSEQ:
                            h0_load(b + 1)
                        if b > 0:
                            hf_store(b - 1)
                    hf_store(NSEQ - 1)
                    cp("act", YOFF, po_[:], [pro_], [rYOFF])
                    release(po_)
                for j2l in range(2):
                    cc = 2 * hh + j2l
                    yv, ryv = YV[cc]
                    xs_, rxs = XS[par][cc]
                    for h2 in range(2):
                        lo, hi = 64 * h2, 64 * h2 + 64
                        hsel = 2 * j2l + h2
                        stt(yv[lo:hi, :], xs_[lo:hi, cs_], vcol("dsk%d" % i, 4 * g + cc, p0=lo, p1=hi),
                            py[lo:hi, hsel * 128:(hsel + 1) * 128], ALU.mult, ALU.add, [rxs, rVEC, pry], [ryv])
                        if sample:
                            tt_("dve", yv[lo:hi, :], yv[lo:hi, :], YOFF[lo:hi, hsel * 128:(hsel + 1) * 128], ALU.add,
                                [ryv, rYOFF], [ryv])
                release(py)
                if not sample:
                    pst, prst = bank()
                    mm(pst[:, 0:256], btok, XDD, True, True, [rbtok, rXDD], [prst])
                    hcol = slice(256 * hh, 256 * hh + 256)
                    if first:
                        cp("dve", HT[:, hcol], pst[:, 0:256], [prst], [rHT])
                    else:
                        tt_("pool", TMPH.rearrange("p (h d) -> p h d", h=4), HT[:, hcol].rearrange("p (h d) -> p h d", h=4),
                            cdec.unsqueeze(2).broadcast_to([128, 4, 64]), ALU.mult, [rHT, rcdec], [rTMPH])
                        tt_("dve", HT[:, hcol], TMPH, pst[:, 0:256], ALU.add, [rTMPH, prst], [rHT])
                    cp("act", HTB[:, hcol], HT[:, hcol], [rHT], [rHTB])

        def stage_c(g, ch):
            ti, q = (ch // 2, ch % 2) if ch < 16 else (8, 0)
            par, cp_ = ti % 2, ch % 2
            cs_ = slice(q * 128, q * 128 + 128)
            YV = YV2[cp_]
            for cc in range(4):
                yv, ryv = YV[cc]
                tt_("pool", yv, yv, ZS[par][cc][0][:, cs_], ALU.mult, [ryv, ZS[par][cc][1]], [ryv])
                act(SQY[:, cc * 128:(cc + 1) * 128], yv, AF.Square, [ryv], [rSQY])
            pb, pr = bank()
            for cc in range(4):
                mm(pb[:, 0:128], cb("o512"), SQY[:, cc * 128:(cc + 1) * 128], cc == 0, cc == 3, [rCB, rSQY], [pr])
            act(RSTD, pb[:, 0:128], AF.Ln, [pr, rMISC], [rRSTD], bias=MISC[:, 0:1], scale=1.0)
            act(RSTD, RSTD, AF.Exp, [rRSTD], [rRSTD], scale=-0.5)
            for cc in range(4):
                yv, ryv = YV[cc]
                stt(YN[cc][0][:, cs_], yv, vcol("gn%d" % i, 4 * g + cc), RSTD, ALU.mult, ALU.mult,
                    [ryv, rVEC, rRSTD], [YN[cc][1]])

        def outproj(g, ti):
            c0, w = TILES[ti]
            for n in range(8):
                pb, pr = bank()
                for k in range(4):
                    mm(pb[:, 0:w], wo3[:, k, n * 128:(n + 1) * 128], YN[k][0][:, 0:w], k == 0, k == 3,
                       [rWOo, YN[k][1]], [pr])
                resid_add(n, c0, w, pb, pr)
            if ti == 7:
                pb, pr = bank()
                for q in range(4):
                    tr(pb[:, q * 128:(q + 1) * 128], HT[:, q * 128:(q + 1) * 128], 128, [rHT], [pr])
                cp("act", SOUT, pb[:], [pr], [rSOUT])
                dma("sp", ss_p[i][8 * g:8 * g + 8].rearrange("(j h2) p n -> (h2 p) j n", h2=2),
                    SOUT.rearrange("p (j n) -> p j n", j=4), [rSOUT], [], "sout")

        def load_win(gg):
            cols = [2048 + 512 * gg + 128 * q for q in range(4)] + [4096 + 128 * gg, 4608 + 128 * gg] + \
                   [512 * gg + 128 * q for q in range(4)]
            for t_, col in enumerate(cols):
                dma("pool", win3[:, t_, :, :], w_in_o[i][:, col:col + 128].rearrange("(k p) n -> p k n", p=128),
                    [], [rWINt[t_]], "win%d" % t_)

        load_win(0)
        dma("pool", wo3, w_out_o[i][0:512, :].rearrange("(k p) n -> p k n", p=128), [], [rWOo], "woo")
        wload(WDT.rearrange("p (k n) -> p k n", k=8), w_dt2[i], rWDT, "wdt")
        wdt3 = WDT.rearrange("p (k n) -> p k n", k=8)
        norm_body("nmo%d" % i)
        for (c0, w) in TILES:
            pb, pr = bank()
            for k in range(8):
                mm(pb[0:64, 0:w], wdt3[:, k, :], XNt[:, k, c0:c0 + w], k == 0, k == 7, [rWDT, rXN[k]], [pr])
            act(D2[0:64, 0:w], pb[0:64, 0:w], AF.Exp, [pr, rVEC], [rD2], bias=vcol("dtb%d" % i, p1=64))
            act(D2[0:64, 0:w], D2[0:64, 0:w], AF.Ln, [rD2], [rD2], bias=1.0)
            ts_("pool", D2[32:64, 0:w], D2[32:64, 0:w], MISC[32:64, 10 + i:11 + i], ALU.mult, [rD2, rMISC], [rD2])
            for q in range(w // 128):
                ch = (c0 + q * 128) // 128
                pb2, pr2 = bank()
                tr(pb2[:, 0:64], D2[0:64, q * 128:(q + 1) * 128], 64, [rD2], [pr2])
                cp("act", dtt3[:, ch, :], pb2[:, 0:64], [pr2], [rDTT])

        for g in range(4):
            if g > 0:
                S.barrier()
            if g > 0:
                dma("pool", wo3, w_out_o[i][512 * g:512 * (g + 1), :].rearrange("(k p) n -> p k n", p=128), [], [rWOo], "woo")
            cidx = [4 * g + q for q in range(4)] + [16 + g, 20 + g]
            for k in range(6):
                memset("pool", XPAD[k][0][:, 0:3], 0.0, [XPAD[k][1]])
            def stage_a(ch):
                prep(g, ch)

            def stage_c_full(ch):
                stage_c(g, ch)
                if ch == 16:
                    outproj(g, 8)
                elif ch % 2 == 1:
                    outproj(g, ch // 2)

            produce(g, 0, cidx)
            produce(g, 1, cidx)
            stage_a(0)
            stage_a(1)
            stage_b(g, 0)
            for s_ in range(0, 14):
                stage_c_full(s_)
                if s_ % 2 == 1:
                    produce(g, (s_ + 3) // 2, cidx)
                    if s_ == 13 and g < 3:
                        load_win(g + 1)
                stage_a(s_ + 2)
                stage_b(g, s_ + 1)
            stage_a(16)
            stage_b(g, 15)
            stage_c_full(14)
            stage_c_full(15)
            S.barrier()
            stage_b(g, 16)
            stage_c_full(16)
            ocols = [512 * g + 128 * q for q in range(4)] + [2048 + 128 * g, 2560 + 128 * g]
            for k in range(6):
                pb, pr = bank()
                tr(pb[0:3, 0:128], TLP[:, k * 3:(k + 1) * 3], 128, [rTLP], [pr])
                tr(pb[0:48, 128:256], TLS[:, k * 48:(k + 1) * 48], 128, [rTLS], [pr])
                cp("act", STT_[0:3, 256:384], pb[0:3, 0:128], [pr], [rSTT])
                cp("act", STT_[0:48, 512:640], pb[0:48, 128:256], [pr], [rSTT])
                dma("sp", cc_p[i][:, ocols[k]:ocols[k] + 128], STT_[0:3, 256:384], [rSTT], [], "stt")
                dma("sp", cc_s[i][:, :, ocols[k]:ocols[k] + 128].rearrange("s t d -> t s d"), STT_[0:48, 512:640],
                    [rSTT], [], "stt")

    if phases is None:
        phases = ["e0", "f0", "o0", "f1", "e1", "f2", "o1", "f3"]
    for ph in phases:
        if ph[0] == "e":
            even_phase(int(ph[1]), ph[3:] if len(ph) > 2 else None)
        elif ph[0] == "o":
            odd_phase(int(ph[1]))
        elif ph[0] == "f":
            ffn_phase(int(ph[1]))

    S.barrier()
    AR.reset()
    yo = [AR.f32(1024, "yo%d" % k) for k in range(2)]
    ys3 = y_s.rearrange("(s t) d -> t s d", t=8)
    for rt in range(17):
        buf, rb = yo[rt % 2]
        for half in range(2):
            pb, pr = bank()
            for q in range(4):
                c = half * 4 + q
                tr(pb[:, q * 128:(q + 1) * 128], Xt[:, c, rt * 128:(rt + 1) * 128], 128, rX[c], [pr])
            cp("act" if half == 0 else "dve", buf[:, half * 512:(half + 1) * 512], pb[:], [pr], [rb])
        if rt < 16:
            dma("sp", y_p[rt * 128:(rt + 1) * 128, :], buf, [rb], [], "yo%d" % (rt % 2))
        else:
            dma("sp", ys3, buf, [rb], [], "yo%d" % (rt % 2))

    S.emit(st)
    st.close()
    return nc


_CACHE = {}


def make_in_maps(inp):
    cfh, cbh = host_consts()
    vecs = host_vecs(inp)
    posrow = np.concatenate([np.arange(TP, dtype=np.float32),
                             np.repeat(8192.0 + np.arange(8, dtype=np.float32), NSEQ)])[None, :].astype(np.float32)
    qperm = np.concatenate([np.concatenate([np.arange(64 * c, 64 * c + 64), np.arange(64 * (4 + c), 64 * (4 + c) + 64)])
                            for c in range(4)])
    w_in_e = inp["w_in_e"].copy()
    w_in_e[:, :, 1024:1536] = inp["w_in_e"][:, :, 1024 + qperm]
    w_out_e = inp["w_out_e"].copy()
    w_out_e[:, 512:1024, :] = inp["w_out_e"][:, 512 + qperm, :]
    w_dt2 = np.ascontiguousarray(np.concatenate([inp["w_in_o"][:, :, 5120:5152]] * 2, axis=2))
    shared = dict(posrow=posrow, cf=cfh, cb=cbh, vecs=vecs, w_in_e=w_in_e, w_out_e=w_out_e, w_in_o=inp["w_in_o"],
                  w_dt2=w_dt2, w_out_o=inp["w_out_o"], w_gate=inp["w_gate"], w_up=inp["w_up"], w_down=inp["w_down"])
    in_maps = []
    for c in range(NCORES):
        sl = slice(NSEQ * c, NSEQ * (c + 1))
        m = dict(shared)
        m["xp"] = np.ascontiguousarray(inp["x_prompt"][c])
        m["xs"] = np.ascontiguousarray(inp["x_sample"][sl].reshape(TS, D))
        m["st_ca"] = np.ascontiguousarray(inp["state_conv_a"][:, sl].reshape(2, 480, 512))
        m["st_k"] = np.ascontiguousarray(inp["cache_win_k"][:, sl].reshape(2, NSEQ, 128, 128))
        m["st_v"] = np.ascontiguousarray(inp["cache_win_v"][:, sl].reshape(2, NSEQ, 128, 128))
        m["st_cc"] = np.ascontiguousarray(inp["state_conv_c"][:, sl].reshape(2, 48, 3072))
        m["st_ssm"] = np.ascontiguousarray(inp["state_ssm"][:, sl])
        m["st_ff"] = np.ascontiguousarray(inp["state_ffn_conv"][:, sl].reshape(4, 32, DFF))
        in_maps.append(m)
    return in_maps


def kernel(**inp):
    inp = {k: np.asarray(v) for k, v in inp.items()}
    if "nc" not in _CACHE:
        _CACHE["nc"] = build_program()
    nc = _CACHE["nc"]
    in_maps = make_in_maps(inp)
    res = run_bass_kernel_spmd(nc, in_maps, core_ids=list(range(NCORES)))
    R = res.results

    def cat(name, axis, shp=None):
        parts = [np.asarray(R[c][name]) if shp is None else np.asarray(R[c][name]).reshape(shp) for c in range(NCORES)]
        return np.concatenate(parts, axis=axis)

    y_prompt = np.stack([np.asarray(R[c]["y_p"]) for c in range(NCORES)], 0)
    y_sample = cat("y_s", 0, (NSEQ, 8, D))
    ca_p_ = np.stack([np.asarray(R[c]["ca_p"]) for c in range(NCORES)], 1)
    ca_s_ = cat("ca_s", 1)
    wk_p_ = np.stack([np.asarray(R[c]["wk_p"]).reshape(2, 128, 2, 64) for c in range(NCORES)], 1)
    wk_s_ = cat("wk_s", 1, (2, NSEQ, 128, 2, 64))
    wv_p_ = np.stack([np.asarray(R[c]["wv_p"]).reshape(2, 128, 2, 64) for c in range(NCORES)], 1)
    wv_s_ = cat("wv_s", 1, (2, NSEQ, 128, 2, 64))
    cc_p_ = np.stack([np.asarray(R[c]["cc_p"]) for c in range(NCORES)], 1)
    cc_s_ = cat("cc_s", 1)
    ss_p_ = np.stack([np.asarray(R[c]["ss_p"]) for c in range(NCORES)], 1)
    ss_s_ = cat("ss_s", 1)
    ff_p_ = np.stack([np.asarray(R[c]["ff_p"]) for c in range(NCORES)], 1)
    ff_s_ = cat("ff_s", 1)
    outs = (y_prompt, y_sample, ca_p_, ca_s_, wk_p_, wk_s_, wv_p_, wv_s_, cc_p_, cc_s_, ss_p_, ss_s_, ff_p_, ff_s_)
    return tuple(np.ascontiguousarray(o, dtype=np.float32) for o in outs)
```

```python
import contextlib
import math
import os
import numpy as np
import concourse.bass as bass
import concourse.mybir as mybir
from concourse.bass_utils import run_bass_kernel_spmd

F32 = mybir.dt.float32
BF16 = mybir.dt.bfloat16
I32 = mybir.dt.int32
AF = mybir.ActivationFunctionType
ALU = mybir.AluOpType

NCORES = 8
D = 1024
TP = 2048
TS = 128
T = TP + TS
NSEQ = 16
DFF = 2816
NJ = DFF // 128
EPS = 1e-6
NEG = -30000.0
ARENA_WORDS = 24300


class Res:
    __slots__ = ("name", "last_w", "readers", "dsem", "dcount", "excl")

    def __init__(self, name, excl=False):
        self.name = name
        self.excl = excl
        self.last_w = None
        self.readers = []
        self.dsem = None
        self.dcount = 0


class Op:
    __slots__ = ("eng", "fn", "waits", "signal", "sigval", "dma_res", "dma_val")

    def __init__(self, eng, fn):
        self.eng = eng
        self.fn = fn
        self.waits = []
        self.signal = False
        self.sigval = 0
        self.dma_res = None
        self.dma_val = 0


class Sched:
    ENGS = ("pe", "act", "dve", "pool", "sp")

    def __init__(self, nc):
        self.nc = nc
        self.ops = []
        self.dma_res = []
        self.last = {e: None for e in self.ENGS}
        self.dma_since = []
        self.pending = {e: [] for e in self.ENGS}

    def op(self, eng, fn, reads=(), writes=(), dma=None):
        o = Op(eng, fn)
        writes = list(writes) + [r for r in reads if r.excl and r not in writes]
        deps = [(d, True) for d in self.pending[eng]]
        self.pending[eng] = []
        for r in reads:
            if r.last_w is not None:
                deps.append((r.last_w, True))
        for w in writes:
            if w.last_w is not None:
                deps.append((w.last_w, False))
            deps.extend((x, False) for x in w.readers)
        seen = set()
        for d, raw in deps:
            if d.dma_res is None and d.eng == eng:
                if eng in ("pe", "sp") or not raw:
                    continue
            if id(d) in seen:
                continue
            seen.add(id(d))
            o.waits.append(d)
            d.signal = True
        if dma is not None:
            o.dma_res = dma
            dma.dcount += 1
            o.dma_val = 16 * dma.dcount
            o.signal = True
            if dma.dsem is None:
                dma.dsem = True
                self.dma_res.append(dma)
            self.dma_since.append(o)
        else:
            self.last[eng] = o
        for w in writes:
            w.last_w = o
            w.readers = []
        for r in reads:
            r.readers.append(o)
        self.ops.append(o)
        return o

    def barrier(self):
        tg = [o for o in self.last.values() if o is not None] + self.dma_since
        self.dma_since = []
        for e in self.ENGS:
            self.pending[e] = list(tg)

    def emit(self, stack):
        nc = self.nc
        esem = {e: stack.enter_context(nc.semaphore("s_" + e)) for e in self.ENGS}
        for i, r in enumerate(self.dma_res):
            r.dsem = stack.enter_context(nc.semaphore("d%d" % i))
        cnt = {e: 0 for e in self.ENGS}
        per = {e: [] for e in self.ENGS}
        for o in self.ops:
            if o.dma_res is None and o.signal:
                cnt[o.eng] += 1
                o.sigval = cnt[o.eng]
            per[o.eng].append(o)
        block = stack.enter_context(nc.Block())

        def run(eng_name, eng):
            known = {}
            for o in per[eng_name]:
                for d in o.waits:
                    if d.dma_res is not None:
                        key, val = d.dma_res.dsem, d.dma_val
                    else:
                        key, val = esem[d.eng], d.sigval
                    if known.get(id(key), 0) >= val:
                        continue
                    known[id(key)] = val
                    eng.wait_ge(key, val)
                ins = o.fn(eng)
                if o.dma_res is not None:
                    ins.then_inc(o.dma_res.dsem, 16)
                elif o.signal:
                    ins.then_inc(esem[eng_name], 1)
            if eng_name == "sp":
                for r in self.dma_res:
                    if known.get(id(r.dsem), 0) < 16 * r.dcount:
                        eng.wait_ge(r.dsem, 16 * r.dcount)
                for e2 in ("pe", "act", "dve", "pool"):
                    if cnt[e2] > 0:
                        eng.wait_ge(esem[e2], cnt[e2])

        @block.tensor
        def _(e):
            run("pe", e)

        @block.scalar
        def _(e):
            run("act", e)

        @block.vector
        def _(e):
            run("dve", e)

        @block.gpsimd
        def _(e):
            run("pool", e)

        @block.sync
        def _(e):
            run("sp", e)


class Arena:
    def __init__(self, t, words):
        self.t = t
        self.words = words
        self.off = 0

    def reset(self):
        self.off = 0

    def f32(self, n, name):
        a = self.off
        self.off += n
        assert self.off <= self.words, (name, self.off, self.words)
        return self.t[:, a:a + n], Res(name)

    def bf(self, n, name):
        w = (n + 1) // 2
        a = self.off
        self.off += w
        assert self.off <= self.words, (name, self.off, self.words)
        return self.t[:, a:a + w].bitcast(BF16)[:, 0:n], Res(name)


def vec_layout():
    ent = []
    for i in range(2):
        ent += [("nme%d" % i, 8), ("nmo%d" % i, 8), ("caw%d" % i, 124), ("cab%d" % i, 4), ("lng%d" % i, 4),
                ("lnb%d" % i, 4), ("qg%d" % i, 1), ("kg%d" % i, 1), ("snk%d" % i, 4), ("ccw%d" % i, 96),
                ("ccb%d" % i, 24), ("dtb%d" % i, 1), ("alog%d" % i, 1), ("dsk%d" % i, 16), ("gn%d" % i, 16)]
    for l in range(4):
        ent += [("nf%d" % l, 8), ("fcw%d" % l, 66), ("fcb%d" % l, 22)]
    ent += [("invf", 1)]
    off = {}
    o = 0
    for n, c in ent:
        off[n] = o
        o += c
    return off, o


CF = dict(ident=0, tri=128, ut=256, tri_s=384, ut_s=512, pmat=640, seqmask=768, ones=784)
NCF = 912
CB = dict(ident=0, o1024=128, o512=256, bd64=384, mcur=512, mprev=640, msnew=768, mshist=896, cbm_p=1024,
          cbm_s=1152, ones=1280)
NCB = 1408


def host_consts():
    cf = np.zeros((128, NCF), np.float32)
    cb = np.zeros((128, NCB), np.float32)
    idx = np.arange(128)
    eye = np.eye(128, dtype=np.float32)
    tri = (idx[:, None] <= idx[None, :]).astype(np.float32)
    ut = (idx[:, None] > idx[None, :]).astype(np.float32)
    tt_, ss_ = idx // 16, idx % 16
    same = ss_[:, None] == ss_[None, :]
    tri_s = (same & (tt_[:, None] <= tt_[None, :])).astype(np.float32)
    ut_s = (same & (tt_[:, None] > tt_[None, :])).astype(np.float32)
    pm = np.zeros((128, 128), np.float32)
    for hb in (0, 64):
        for d in range(8):
            pm[hb + d + 8, hb + d] = -1.0
            pm[hb + d, hb + d + 8] = 1.0
    cf[:, 0:128] = eye
    cf[:, 128:256] = tri
    cf[:, 256:384] = ut
    cf[:, 384:512] = tri_s
    cf[:, 512:640] = ut_s
    cf[:, 640:768] = pm
    cf[:, 768:784] = (ss_[:, None] == np.arange(16)[None, :]).astype(np.float32)
    cf[:, 784:912] = 1.0
    cb[:, 0:128] = eye
    cb[:, 128:256] = 1.0 / 1024
    cb[:, 256:384] = 1.0 / 512
    bd = np.zeros((128, 128), np.float32)
    bd[0:64, 0:64] = 1.0 / 64
    bd[64:128, 64:128] = 1.0 / 64
    cb[:, 384:512] = bd
    cb[:, 512:640] = np.where(idx[None, :] >= idx[:, None], 0.0, NEG)
    cb[:, 640:768] = np.where(idx[:, None] > idx[None, :], 0.0, NEG)
    cb[:, 768:896] = np.where(same & (tt_[:, None] <= tt_[None, :]), 0.0, NEG)
    cb[:, 896:1024] = np.where(idx[:, None] > tt_[None, :], 0.0, NEG)
    cb[:, 1024:1152] = tri
    cb[:, 1152:1280] = tri_s
    cb[:, 1280:1408] = 1.0
    return cf, cb


def host_vecs(inp):
    off, nv = vec_layout()
    v = np.zeros((128, nv), np.float32)

    def put(name, arr):
        arr = np.asarray(arr, np.float32)
        v[:, off[name]:off[name] + arr.shape[1]] = arr

    def chunks(x):
        return np.asarray(x).reshape(-1, 128).T

    def taps(w):
        K, C = w.shape
        return np.asarray(w).reshape(K, C // 128, 128).transpose(2, 1, 0).reshape(128, -1)

    def pair(x, n):
        x = np.asarray(x)
        return np.concatenate([np.repeat(x[0::2][None, :], 64, 0), np.repeat(x[1::2][None, :], 64, 0)], 0)

    for i in range(2):
        put("nme%d" % i, chunks(inp["norm_mix_e"][i]))
        put("nmo%d" % i, chunks(inp["norm_mix_o"][i]))
        put("caw%d" % i, taps(inp["conv_a_w"][i]))
        put("cab%d" % i, chunks(inp["conv_a_b"][i]))
        put("lng%d" % i, chunks(inp["ln_a_g"][i]))
        put("lnb%d" % i, chunks(inp["ln_a_b"][i]))
        put("qg%d" % i, np.tile(np.asarray(inp["q_norm_g"][i]), 2)[:, None])
        put("kg%d" % i, np.tile(np.asarray(inp["k_norm_g"][i]), 2)[:, None])
        sk = np.asarray(inp["sinks"][i])
        put("snk%d" % i, np.concatenate([np.repeat(sk[None, 0:4], 64, 0), np.repeat(sk[None, 4:8], 64, 0)], 0))
        put("ccw%d" % i, taps(inp["conv_c_w"][i]))
        put("ccb%d" % i, chunks(inp["conv_c_b"][i]))
        dtb = np.zeros((128, 1), np.float32)
        dtb[0:32, 0] = inp["dt_bias"][i]
        dtb[32:64, 0] = inp["dt_bias"][i]
        put("dtb%d" % i, dtb)
        al = np.zeros((128, 1), np.float32)
        al[32:64, 0] = inp["a_log"][i]
        put("alog%d" % i, al)
        put("dsk%d" % i, pair(inp["d_skip"][i], 16))
        put("gn%d" % i, chunks(inp["gnorm_c"][i]))
    for l in range(4):
        put("nf%d" % l, chunks(inp["norm_ffn"][l]))
        put("fcw%d" % l, taps(inp["ffn_conv_w"][l]))
        put("fcb%d" % l, chunks(inp["ffn_conv_b"][l]))
    invf = np.zeros((128, 1), np.float32)
    f = (500000.0 ** (-np.arange(0, 16, 2, dtype=np.float32) / 16.0)).astype(np.float32)
    for hb in (0, 64):
        invf[hb:hb + 8, 0] = f
        invf[hb + 8:hb + 16, 0] = f
    put("invf", invf)
    return v


def build_program(phases=None):
    nc = bass.Bass("TRN2", target_bir_lowering=False)
    VO, NV = vec_layout()

    def din(name, shape):
        return nc.dram_tensor(name, list(shape), F32, kind="ExternalInput").ap()

    def dout(name, shape):
        return nc.dram_tensor(name, list(shape), F32, kind="ExternalOutput").ap()

    xp_d = din("xp", [TP, D])
    xs_d = din("xs", [TS, D])
    pos_d = din("posrow", [1, T])
    cf_d = din("cf", [128, NCF])
    cb_d = din("cb", [128, NCB])
    vec_d = din("vecs", [128, NV])
    st_ca = din("st_ca", [2, 480, 512])
    st_k = din("st_k", [2, NSEQ, 128, 128])
    st_v = din("st_v", [2, NSEQ, 128, 128])
    st_cc = din("st_cc", [2, 48, 3072])
    st_ssm = din("st_ssm", [2, NSEQ, 32, 64, 128])
    st_ff = din("st_ff", [4, 32, DFF])
    w_in_e = din("w_in_e", [2, D, 1792])
    w_out_e = din("w_out_e", [2, D, D])
    w_in_o = din("w_in_o", [2, D, 5152])
    w_dt2 = din("w_dt2", [2, D, 64])
    w_out_o = din("w_out_o", [2, 2048, D])
    w_gate = din("w_gate", [4, D, DFF])
    w_up = din("w_up", [4, D, DFF])
    w_down = din("w_down", [4, DFF, D])

    y_p = dout("y_p", [TP, D])
    y_s = dout("y_s", [TS, D])
    ca_p = dout("ca_p", [2, 30, 512])
    ca_s = dout("ca_s", [2, NSEQ, 30, 512])
    wk_p = dout("wk_p", [2, 128, 128])
    wk_s = dout("wk_s", [2, NSEQ, 128, 128])
    wv_p = dout("wv_p", [2, 128, 128])
    wv_s = dout("wv_s", [2, NSEQ, 128, 128])
    cc_p = dout("cc_p", [2, 3, 3072])
    cc_s = dout("cc_s", [2, NSEQ, 3, 3072])
    ss_p = dout("ss_p", [2, 32, 64, 128])
    ss_s = dout("ss_s", [2, NSEQ, 32, 64, 128])
    ff_p = dout("ff_p", [4, 2, DFF])
    ff_s = dout("ff_s", [4, NSEQ, 2, DFF])

    st = contextlib.ExitStack()
    S = Sched(nc)

    def sbt(name, shape, dt=F32):
        return st.enter_context(nc.sbuf_tensor(name, shape, dt))

    Xt = sbt("X", [128, 8, T])
    XNt = sbt("XN", [128, 8, T], BF16)
    CFt = sbt("CFt", [128, NCF])
    CBt = sbt("CBt", [128, NCB], BF16)
    VEC = sbt("VEC", [128, NV])
    MISC = sbt("MISC", [128, 16])
    ARt = sbt("ARENA", [128, ARENA_WORDS])
    AR = Arena(ARt, ARENA_WORDS)
    PB = [st.enter_context(nc.psum_tensor("pb%d" % i, [128, 512], F32)) for i in range(8)]
    PR = [Res("pb%d" % i, excl=True) for i in range(8)]
    pstate = [0]
    held = set()

    def bank(hold=False):
        while True:
            i = pstate[0] % 8
            pstate[0] += 1
            if i not in held:
                break
        if hold:
            held.add(i)
        return PB[i], PR[i]

    def release(pb):
        for i in range(8):
            if PB[i] is pb:
                held.discard(i)

    SEM = {}

    def sh(name):
        if name not in SEM:
            SEM[name] = Res("sem_" + name)
        return SEM[name]

    def scol(ap2d, b):
        return ap2d.rearrange("p (t s) -> p t s", s=NSEQ)[:, :, b]

    rX = [[Res("X%d_%d" % (c, j)) for j in range(5)] for c in range(8)]
    rXN = [Res("XN%d" % c) for c in range(8)]
    rCF, rCB, rVEC, rMISC = Res("cf"), Res("cb"), Res("vec"), Res("misc")
    TT = [(0, 512), (512, 512), (1024, 512), (1536, 512), (2048, 128)]

    def xres(c, c0, w):
        return [rX[c][j] for j in range(5) if not (TT[j][0] >= c0 + w or TT[j][0] + TT[j][1] <= c0)]

    def cf(name, n=128, rows=128):
        return CFt[0:rows, CF[name]:CF[name] + n]

    def cb(name, n=128, rows=128):
        return CBt[0:rows, CB[name]:CB[name] + n]

    def vcol(name, j=0, n=1, p0=0, p1=128):
        return VEC[p0:p1, VO[name] + j:VO[name] + j + n]

    def mm(out, lhsT, rhs, start, stop, R, W):
        S.op("pe", lambda e: e.matmul(out, lhsT=lhsT, rhs=rhs, start=start, stop=stop), reads=R, writes=W)

    def tr(out, in_, k, R, W):
        S.op("pe", lambda e: e.transpose(out, in_, CFt[0:k, 0:k]), reads=list(R) + [rCF], writes=W)

    def act(out, in_, func, R, W, bias=None, scale=None):
        kw = {}
        if bias is not None:
            kw["bias"] = bias
        if scale is not None:
            kw["scale"] = scale
        S.op("act", lambda e: e.activation(out=out, in_=in_, func=func, **kw), reads=R, writes=W)

    def cp(eng, out, in_, R, W):
        if eng == "act":
            S.op("act", lambda e: e.copy(out=out, in_=in_), reads=R, writes=W)
        else:
            S.op(eng, lambda e: e.tensor_copy(out=out, in_=in_), reads=R, writes=W)

    def tt_(eng, out, in0, in1, op, R, W):
        S.op(eng, lambda e: e.tensor_tensor(out=out, in0=in0, in1=in1, op=op), reads=R, writes=W)

    def ts_(eng, out, in0, s1, op0, R, W, s2=None, op1=None):
        if op1 is None:
            S.op(eng, lambda e: e.tensor_scalar(out=out, in0=in0, scalar1=s1, scalar2=None, op0=op0), reads=R, writes=W)
        else:
            S.op(eng, lambda e: e.tensor_scalar(out=out, in0=in0, scalar1=s1, scalar2=s2, op0=op0, op1=op1),
                 reads=R, writes=W)

    def stt(out, in0, scalar, in1, op0, op1, R, W):
        S.op("dve", lambda e: e.scalar_tensor_tensor(out=out, in0=in0, scalar=scalar, in1=in1, op0=op0, op1=op1),
             reads=R, writes=W)

    def dma(eng, out, in_, R, W, sem):
        S.op(eng, lambda e: e.dma_start(out=out, in_=in_), reads=R, writes=W, dma=sh(sem))

    def memset(eng, ap, val, W):
        S.op(eng, lambda e: e.memset(ap, val), writes=W)

    def rstd_from(out, psum_ap, R, W):
        act(out, psum_ap, AF.Sqrt, R + [rMISC], W, bias=MISC[:, 0:1], scale=1.0)
        S.op("dve", lambda e: e.reciprocal(out=out, in_=out), reads=W, writes=W)

    def wload(dst3, src2, res, sem):
        dma("pool", dst3, src2.rearrange("(k p) n -> p k n", p=128), [], [res], sem)

    dma("sp", CFt[:], cf_d, [], [rCF], "cf")
    dma("pool", CBt[:], cb_d, [], [rCB], "cb")
    dma("sp", VEC[:], vec_d, [], [rVEC], "vec")
    memset("pool", MISC[:], 0.0, [rMISC])
    memset("pool", MISC[:, 0:1], EPS, [rMISC])
    for i in range(2):
        act(MISC[32:64, 10 + i:11 + i], vcol("alog%d" % i, p0=32, p1=64), AF.Exp, [rVEC, rMISC], [rMISC])
        ts_("pool", MISC[32:64, 10 + i:11 + i], MISC[32:64, 10 + i:11 + i], -1.0, ALU.mult, [rMISC], [rMISC])
        act(MISC[:, 2 + 4 * i:6 + 4 * i], vcol("snk%d" % i, n=4), AF.Exp, [rVEC, rMISC], [rMISC])

    AR.reset()
    xin = [AR.f32(1024, "xin%d" % k) for k in range(2)]
    for rt in range(17):
        buf, rb = xin[rt % 2]
        src = xp_d[rt * 128:(rt + 1) * 128, :] if rt < 16 else xs_d
        dma("sp", buf, src, [], [rb], "xin%d" % (rt % 2))
        for half in range(2):
            pb, pr = bank()
            for q in range(4):
                c = half * 4 + q
                tr(pb[:, q * 128:(q + 1) * 128], buf[:, c * 128:(c + 1) * 128], 128, [rb], [pr])
            if rt < 16:
                dst = Xt[:, half * 4:half * 4 + 4, rt * 128:(rt + 1) * 128]
                src_ps = pb[:].rearrange("p (c t) -> p c t", c=4)
            else:
                dst = Xt[:, half * 4:half * 4 + 4, TP:T].rearrange("p c (t s) -> p c t s", s=NSEQ)
                src_ps = pb[:].rearrange("p (c s t) -> p c t s", c=4, s=NSEQ)
            W = [rX[half * 4 + q][min(rt // 4, 4)] for q in range(4)]
            cp("act" if half == 0 else "dve", dst, src_ps, [pr], W)

    def norm_phase(gname):
        S.barrier()
        AR.reset()
        sq = [AR.bf(T, "sq%d" % k) for k in range(2)]
        rs, rrs = AR.f32(T, "rs")
        banks = [bank(hold=True) for _ in range(5)]
        for c in range(8):
            b, rb = sq[c % 2]
            if c % 2 == 0:
                act(b, Xt[:, c, :], AF.Square, rX[c], [rb])
            else:
                tt_("pool", b, Xt[:, c, :], Xt[:, c, :], ALU.mult, rX[c], [rb])
            for j, (c0, w) in enumerate(TT):
                pb, pr = banks[j]
                mm(pb[:, 0:w], cb("o1024"), b[:, c0:c0 + w], c == 0, c == 7, [rCB, rb], [pr])
        for j, (c0, w) in enumerate(TT):
            pb, pr = banks[j]
            rstd_from(rs[:, c0:c0 + w], pb[:, 0:w], [pr], [rrs])
            release(pb)
        for c in range(8):
            stt(XNt[:, c, :], Xt[:, c, :], vcol(gname, c), rs, ALU.mult, ALU.mult, rX[c] + [rVEC, rrs], [rXN[c]])

    def resid_add(n, c0, w, pb, pr):
        tt_("dve", Xt[:, n, c0:c0 + w], Xt[:, n, c0:c0 + w], pb[:, 0:w], ALU.add, xres(n, c0, w) + [pr], xres(n, c0, w))

    def ffn_phase(l):
        norm_phase("nf%d" % l)
        S.barrier()
        AR.reset()
        GW = 2210
        WG = [AR.bf(8 * 256, "wg%d" % k) for k in range(2)]
        WU = [AR.bf(8 * 256, "wu%d" % k) for k in range(2)]
        WD = [AR.bf(2 * 1024, "wd%d" % k) for k in range(2)]
        HS = [AR.f32(256, "hs%d" % k) for k in range(2)]
        G = [AR.f32(GW, "g%d" % k) for k in range(2)]
        U = [AR.f32(T, "u%d" % k) for k in range(2)]
        ACC, rACC = AR.f32(T, "acc")
        AT = [AR.bf(2 * T, "at%d" % k) for k in range(2)]
        SP_, rSP = AR.f32(256, "stgp")
        SS_, rSS = AR.f32(256, "stgs")
        for k in range(2):
            memset("pool", G[k][0][:, 0:2], 0.0, [G[k][1]])

        def loadw(g):
            s = g % 2
            wload(WG[s][0].rearrange("p (k n) -> p k n", k=8), w_gate[l][:, g * 256:(g + 1) * 256], WG[s][1], "wg%d" % s)
            wload(WU[s][0].rearrange("p (k n) -> p k n", k=8), w_up[l][:, g * 256:(g + 1) * 256], WU[s][1], "wu%d" % s)
            dma("sp", HS[s][0][0:32, :], st_ff[l][:, g * 256:(g + 1) * 256], [], [HS[s][1]], "hs%d" % s)

        def loadwd(g):
            s = g % 2
            wload(WD[s][0].rearrange("p (k n) -> p k n", k=2), w_down[l][g * 256:(g + 1) * 256, :], WD[s][1], "wd%d" % s)

        def down(g):
            s = g % 2
            wd3 = WD[s][0].rearrange("p (k n) -> p k n", k=2)
            at3 = AT[s][0].rearrange("p (j t) -> p j t", j=2)
            for n in range(8):
                for (c0, w) in TT:
                    pb, pr = bank()
                    for jj in range(2):
                        mm(pb[:, 0:w], wd3[:, jj, n * 128:(n + 1) * 128], at3[:, jj, c0:c0 + w], jj == 0, jj == 1,
                           [WD[s][1], AT[s][1]], [pr])
                    resid_add(n, c0, w, pb, pr)

        loadw(0)
        loadwd(0)
        for g in range(NJ // 2):
            s = g % 2
            if g + 1 < NJ // 2:
                loadw(g + 1)
            wg3 = WG[s][0].rearrange("p (k n) -> p k n", k=8)
            wu3 = WU[s][0].rearrange("p (k n) -> p k n", k=8)
            wd3 = WD[s][0].rearrange("p (k n) -> p k n", k=2)
            at3 = AT[s][0].rearrange("p (j t) -> p j t", j=2)
            for jj in range(2):
                j = 2 * g + jj
                gb, rg = G[j % 2]
                ub, ru = U[j % 2]
                pb, pr = bank()
                tr(pb[:, 0:32], HS[s][0][0:32, jj * 128:(jj + 1) * 128], 32, [HS[s][1]], [pr])
                cp("dve", gb[:, 2050:2082].rearrange("p (k s) -> p k s", s=NSEQ),
                   pb[:, 0:32].rearrange("p (s k) -> p k s", k=2), [pr], [rg])
                for ti, (c0, w) in enumerate(TT):
                    pb, pr = bank()
                    for k in range(8):
                        mm(pb[:, 0:w], wg3[:, k, jj * 128:(jj + 1) * 128], XNt[:, k, c0:c0 + w], k == 0, k == 7,
                           [WG[s][1], rXN[k]], [pr])
                    dst = gb[:, 2 + c0:2 + c0 + w] if ti < 4 else gb[:, 2082:2210]
                    cp("act", dst, pb[:, 0:w], [pr], [rg])
                for ti, (c0, w) in enumerate(TT):
                    pb, pr = bank()
                    for k in range(8):
                        mm(pb[:, 0:w], wu3[:, k, jj * 128:(jj + 1) * 128], XNt[:, k, c0:c0 + w], k == 0, k == 7,
                           [WU[s][1], rXN[k]], [pr])
                    cp("dve", ub[:, c0:c0 + w], pb[:, 0:w], [pr], [ru])
                w0, w1, w2 = (vcol("fcw%d" % l, j * 3 + k) for k in range(3))
                bcol = vcol("fcb%d" % l, j)
                act(ACC[:, 0:TP], gb[:, 0:TP], AF.Identity, [rg, rVEC], [rACC], bias=bcol, scale=w0)
                act(ACC[:, TP:T], gb[:, 2050:2178], AF.Identity, [rg, rVEC], [rACC], bias=bcol, scale=w0)
                stt(ACC[:, 0:TP], gb[:, 1:1 + TP], w1, ACC[:, 0:TP], ALU.mult, ALU.add, [rg, rVEC, rACC], [rACC])
                stt(ACC[:, TP:T], gb[:, 2066:2194], w1, ACC[:, TP:T], ALU.mult, ALU.add, [rg, rVEC, rACC], [rACC])
                stt(ACC[:, 0:TP], gb[:, 2:2 + TP], w2, ACC[:, 0:TP], ALU.mult, ALU.add, [rg, rVEC, rACC], [rACC])
                stt(ACC[:, TP:T], gb[:, 2082:2210], w2, ACC[:, TP:T], ALU.mult, ALU.add, [rg, rVEC, rACC], [rACC])
                act(ACC, ACC, AF.Silu, [rACC], [rACC])
                tt_("pool", at3[:, jj, :], ACC, ub, ALU.mult, [rACC, ru], [AT[s][1]])
                pb, pr = bank()
                tr(pb[0:2, 0:128], gb[:, 2048:2050], 128, [rg], [pr])
                cp("act", SP_[0:2, jj * 128:(jj + 1) * 128], pb[0:2, 0:128], [pr], [rSP])
                pb, pr = bank()
                tr(pb[0:32, 0:128], gb[:, 2178:2210], 128, [rg], [pr])
                cp("act", SS_[0:32, jj * 128:(jj + 1) * 128], pb[0:32, 0:128], [pr], [rSS])
            dma("sp", ff_p[l][:, g * 256:(g + 1) * 256], SP_[0:2, :], [rSP], [], "sp")
            dma("sp", ff_s[l][:, :, g * 256:(g + 1) * 256].rearrange("s t d -> t s d"), SS_[0:32, :], [rSS], [], "ss")
            if g > 0:
                down(g - 1)
            if g + 1 < NJ // 2:
                loadwd(g + 1)
        down(NJ // 2 - 1)

    def even_phase(i, stop=None):
        norm_phase("nme%d" % i)
        S.barrier()
        AR.reset()
        UW = 30 + TP + 480 + TS
        UWW = (UW + 1) // 2
        WA = [AR.bf(8 * 256, "wa%d" % k) for k in range(2)]
        WO, rWO = AR.bf(4 * 1024, "woA")
        RG, rUP = AR.f32(UWW + 2048 + 1024, "up_hst_diag_co")
        UP = RG[:, 0:UWW].bitcast(BF16)[:, 0:UW]
        HST = [RG[:, UWW + 512 * k:UWW + 512 * (k + 1)] for k in range(4)]
        DIAG = RG[:, UWW + 2048:UWW + 3072].bitcast(BF16).rearrange("p (k n) -> p k n", k=16)
        CO = [RG[:, 1088 * c:1088 * (c + 1)].bitcast(BF16) for c in range(4)]
        UF, rUF = AR.f32(160, "uf")
        SIG = [AR.f32(512, "sig%d" % k) for k in range(1)] * 2
        CV = [AR.f32(T, "cv%d" % c) for c in range(4)]
        MU, rMU = AR.f32(T, "mu")
        CVB = [AR.bf(T, "cvb%d" % k) for k in range(2)]
        STG, rSTG = AR.f32(512, "stgA")
        STS, rSTS = AR.f32(512, "stsA")
        memset("pool", UP[:, 0:30], 0.0, [rUP])
        for k in range(4):
            rows = 128 if k < 3 else 96
            dma("sp", HST[k][0:rows, :], st_ca[i][k * 128:k * 128 + rows, :], [], [rUP], "up")
        dma("pool", WO.rearrange("p (k n) -> p k n", k=4), w_out_e[i][0:512, :].rearrange("(k p) n -> p k n", p=128),
            [], [rWO], "woA")
        dma("sp", ca_s[i][:, 0:22, :], st_ca[i].rearrange("(s k) c -> s k c", k=30)[:, 8:30, :], [], [], "dd")

        def loadA(c):
            s = c % 2
            w3 = WA[s][0].rearrange("p (k n) -> p k n", k=8)
            dma("pool", w3[:, :, 0:128], w_in_e[i][:, c * 128:(c + 1) * 128].rearrange("(k p) n -> p k n", p=128),
                [], [WA[s][1]], "wa%d" % s)
            dma("pool", w3[:, :, 128:256],
                w_in_e[i][:, 512 + c * 128:512 + (c + 1) * 128].rearrange("(k p) n -> p k n", p=128),
                [], [WA[s][1]], "wa%d" % s)

        loadA(0)
        SB = 30 + TP
        for c in range(4):
            s = c % 2
            if c < 3:
                loadA(c + 1)
            w3 = WA[s][0].rearrange("p (k n) -> p k n", k=8)
            hdst = UP[:, SB:SB + 480].rearrange("p (k s) -> p s k", s=NSEQ)
            for k in range(4):
                rows = 128 if k < 3 else 96
                pb, pr = bank()
                tr(pb[:, 0:rows], HST[k][0:rows, c * 128:(c + 1) * 128], rows, [rUP], [pr])
                r0 = k * 128
                r = r0
                while r < r0 + rows:
                    sq_, kk = divmod(r, 30)
                    n = min(30 - kk, r0 + rows - r)
                    cp("dve", hdst[:, sq_, kk:kk + n], pb[:, r - r0:r - r0 + n], [pr], [rUP])
                    r += n
            for ti, (c0, w) in enumerate(TT):
                pv, prv = bank()
                pg, prg = bank()
                for k in range(8):
                    mm(pv[:, 0:w], w3[:, k, 0:128], XNt[:, k, c0:c0 + w], k == 0, k == 7, [WA[s][1], rXN[k]], [prv])
                for k in range(8):
                    mm(pg[:, 0:w], w3[:, k, 128:256], XNt[:, k, c0:c0 + w], k == 0, k == 7, [WA[s][1], rXN[k]], [prg])
                sg, rsg = SIG[ti % 2]
                act(sg[:, 0:w], pg[:, 0:w], AF.Sigmoid, [prg], [rsg])
                dst = UP[:, 30 + c0:30 + c0 + w] if ti < 4 else UP[:, SB + 480:UW]
                tt_("dve", dst, pv[:, 0:w], sg[:, 0:w], ALU.mult, [prv, rsg], [rUP])
                if ti == 3:
                    tt_("dve", UF[:, 0:30], pv[:, 482:512], sg[:, 482:512], ALU.mult, [prv, rsg], [rUF])
                if ti == 4:
                    tt_("dve", UF[:, 32:160], pv[:, 0:128], sg[:, 0:128], ALU.mult, [prv, rsg], [rUF])
            pb, pr = bank()
            tr(pb[0:30, 0:128], UF[:, 0:30], 128, [rUF], [pr])
            cp("act", STG[0:30, c * 128:(c + 1) * 128], pb[0:30, 0:128], [pr], [rSTG])
            pb, pr = bank()
            tr(pb[:, 0:128], UF[:, 32:160], 128, [rUF], [pr])
            cp("act", STS[:, c * 128:(c + 1) * 128], pb[:, 0:128], [pr], [rSTS])
            cv, rcv = CV[c]
            bcol = vcol("cab%d" % i, c)
            banks = [bank(hold=True) for _ in range(5)]
            for half in range(2):
                taps = list(range(16 * half, min(16 * half + 16, 31)))
                for j, k in enumerate(taps):
                    ts_("dve", DIAG[:, j, :], cb("ident"), vcol("caw%d" % i, c * 31 + k), ALU.mult, [rCB, rVEC], [rUP])
                for ti, (c0, w) in enumerate(TT):
                    pb, pr = banks[ti]
                    for j, k in enumerate(taps):
                        src = UP[:, k + c0:k + c0 + w] if ti < 4 else UP[:, SB + 16 * k:SB + 16 * k + 128]
                        mm(pb[:, 0:w], DIAG[:, j, :], src, k == 0, k == 30, [rUP], [pr])
            for ti, (c0, w) in enumerate(TT):
                pb, pr = banks[ti]
                act(cv[:, c0:c0 + w], pb[:, 0:w], AF.Identity, [pr, rVEC], [rcv], bias=bcol, scale=1.0)
                release(pb)
        if stop == "A0":
            return
        dma("sp", ca_p[i], STG[0:30, :], [rSTG], [], "stg")
        dma("sp", ca_s[i][:, 22:30, :].rearrange("s t d -> t s d"), STS[:, :], [rSTS], [], "sts")
        if stop == "A1":
            return
        banks = [bank(hold=True) for _ in range(5)]
        for c in range(4):
            b, rb = CVB[c % 2]
            cp("pool", b, CV[c][0], [CV[c][1]], [rb])
            for j, (c0, w) in enumerate(TT):
                mm(banks[j][0][:, 0:w], cb("o512"), b[:, c0:c0 + w], c == 0, c == 3, [rCB, rb], [banks[j][1]])
        for j, (c0, w) in enumerate(TT):
            cp("act", MU[:, c0:c0 + w], banks[j][0][:, 0:w], [banks[j][1]], [rMU])
            release(banks[j][0])
        for c in range(4):
            tt_("pool", CV[c][0], CV[c][0], MU, ALU.subtract, [CV[c][1], rMU], [CV[c][1]])
        banks = [bank(hold=True) for _ in range(5)]
        for c in range(4):
            b, rb = CVB[c % 2]
            act(b, CV[c][0], AF.Square, [CV[c][1]], [rb])
            for j, (c0, w) in enumerate(TT):
                mm(banks[j][0][:, 0:w], cb("o512"), b[:, c0:c0 + w], c == 0, c == 3, [rCB, rb], [banks[j][1]])
        for j, (c0, w) in enumerate(TT):
            rstd_from(MU[:, c0:c0 + w], banks[j][0][:, 0:w], [banks[j][1]], [rMU])
            release(banks[j][0])
        for c in range(4):
            tt_("dve", CV[c][0], CV[c][0], MU, ALU.mult, [CV[c][1], rMU], [CV[c][1]])
            act(CO[c], CV[c][0], AF.Silu, [CV[c][1], rVEC], [rUP], bias=vcol("lnb%d" % i, c),
                scale=vcol("lng%d" % i, c))
        wo3 = WO.rearrange("p (k n) -> p k n", k=4)
        for n in range(8):
            for (c0, w) in TT:
                pb, pr = bank()
                for k in range(4):
                    mm(pb[:, 0:w], wo3[:, k, n * 128:(n + 1) * 128], CO[k][:, c0:c0 + w], k == 0, k == 3,
                       [rWO, rUP], [pr])
                resid_add(n, c0, w, pb, pr)

        if stop == "A":
            return
        S.barrier()
        AR.reset()
        WQ = [AR.bf(8 * 128, "wq%d" % k) for k in range(2)]
        WO2, rWO2 = AR.bf(4 * 1024, "woB")
        COS, rCOS = AR.f32(T, "cos")
        SIN, rSIN = AR.f32(T, "sin")
        OT = [COS[:, 0:1088].bitcast(BF16), COS[:, 1088:2176].bitcast(BF16),
              SIN[:, 0:1088].bitcast(BF16), SIN[:, 1088:2176].bitcast(BF16)]
        rOT = [rCOS, rCOS, rSIN, rSIN]
        QR = [AR.bf(T, "qr%d" % c) for c in range(4)]
        KR, rKR = AR.bf(T, "kr")
        KF, rKF = AR.f32(256, "kf")
        VB, rVB = AR.bf(17 * 128, "vb")
        VF, rVF = AR.f32(256, "vf")
        TMP = [AR.f32(512, "tmpb%d" % k) for k in range(6)]
        TI, rTI = AR.f32(512, "ti")
        SQB, rSQB = AR.bf(512, "sqb")
        QNB, rQNB = AR.f32(512, "qnb")
        EB = [AR.bf(256, "eb%d" % k) for k in range(2)]
        REC = [AR.f32(128, "rec%d" % k) for k in range(2)]
        KHS = [AR.f32(128, "khs%d" % k) for k in range(2)]
        KHT, rKHT = AR.bf(NSEQ * 128, "kht")
        VH, rVH = AR.bf(NSEQ * 128, "vh")
        OA, rOA = AR.f32(128, "oa")
        KO, rKO = AR.f32(256, "ko")
        dma("pool", WO2.rearrange("p (k n) -> p k n", k=4), w_out_e[i][512:1024, :].rearrange("(k p) n -> p k n", p=128),
            [], [rWO2], "woB")
        vh3 = VH.rearrange("p (s d) -> p s d", s=NSEQ)
        dma("pool", vh3, st_v[i].rearrange("s k d -> k s d"), [], [rVH], "vh")
        dma("sp", wk_s[i][:, 0:120, :], st_k[i][:, 8:128, :], [], [], "ddk")
        dma("sp", wv_s[i][:, 0:120, :], st_v[i][:, 8:128, :], [], [], "ddk")
        ti_i = TI.bitcast(I32)
        for (c0, w) in TT:
            pz, rpz = TMP[1]
            dma("sp", pz[:, 0:w], pos_d[:, c0:c0 + w].partition_broadcast(128), [], [rpz], "pos")
            for (dst, rdst, ph) in ((SIN, rSIN, 0.0), (COS, rCOS, math.pi / 2)):
                a, ra = TMP[0]
                c_, rc_ = TMP[2]
                ts_("dve", a[:, 0:w], pz[:, 0:w], vcol("invf"), ALU.mult, [rpz, rVEC], [ra], s2=ph, op1=ALU.add)
                ts_("dve", c_[:, 0:w], a[:, 0:w], 1.0 / (2 * math.pi), ALU.mult, [ra], [rc_])
                cp("dve", ti_i[:, 0:w], c_[:, 0:w], [rc_], [rTI])
                cp("dve", c_[:, 0:w], ti_i[:, 0:w], [rTI], [rc_])
                stt(a[:, 0:w], c_[:, 0:w], -2 * math.pi, a[:, 0:w], ALU.mult, ALU.add, [rc_, ra], [ra])
                ts_("dve", c_[:, 0:w], a[:, 0:w], math.pi, ALU.is_gt, [ra], [rc_], s2=-2 * math.pi, op1=ALU.mult)
                tt_("dve", a[:, 0:w], a[:, 0:w], c_[:, 0:w], ALU.add, [ra, rc_], [ra])
                act(dst[:, c0:c0 + w], a[:, 0:w], AF.Sin, [ra], [rdst])
        if stop == "B0":
            return
        kht3 = KHT.rearrange("p (s k) -> p s k", s=NSEQ)
        for sq_ in range(NSEQ):
            hb_, rhb = KHS[sq_ % 2]
            dma("sp", hb_, st_k[i][sq_], [], [rhb], "khs%d" % (sq_ % 2))
            pb, pr = bank()
            tr(pb[:, 0:128], hb_, 128, [rhb], [pr])
            cp("act", kht3[:, sq_, :], pb[:, 0:128], [pr], [rKHT])

        def loadQ(idx):
            s = idx % 2
            col = 1024 + idx * 128
            wload(WQ[s][0].rearrange("p (k n) -> p k n", k=8), w_in_e[i][:, col:col + 128], WQ[s][1], "wq%d" % s)

        loadQ(0)
        for idx in range(5):
            s = idx % 2
            loadQ(idx + 1)
            w3 = WQ[s][0].rearrange("p (k n) -> p k n", k=8)
            gcol = vcol("qg%d" % i) if idx < 4 else vcol("kg%d" % i)
            for ti, (c0, w) in enumerate(TT):
                pb, pr = bank()
                for k in range(8):
                    mm(pb[:, 0:w], w3[:, k, :], XNt[:, k, c0:c0 + w], k == 0, k == 7, [WQ[s][1], rXN[k]], [pr])
                act(SQB[:, 0:w], pb[:, 0:w], AF.Square, [pr], [rSQB])
                p2, pr2 = bank()
                mm(p2[:, 0:w], cb("bd64"), SQB[:, 0:w], True, True, [rCB, rSQB], [pr2])
                r_, rr_ = TMP[3]
                rstd_from(r_[:, 0:w], p2[:, 0:w], [pr2], [rr_])
                stt(QNB[:, 0:w], pb[:, 0:w], gcol, r_[:, 0:w], ALU.mult, ALU.mult, [pr, rVEC, rr_], [rQNB])
                p3, pr3 = bank()
                mm(p3[:, 0:w], cf("pmat"), QNB[:, 0:w], True, True, [rCF, rQNB], [pr3])
                t1, rt1 = TMP[4]
                t2, rt2 = TMP[5]
                tt_("dve", t1[:, 0:w], p3[:, 0:w], SIN[:, c0:c0 + w], ALU.mult, [pr3, rSIN], [rt1])
                tt_("pool", t2[:, 0:w], QNB[:, 0:w], COS[:, c0:c0 + w], ALU.mult, [rQNB, rCOS], [rt2])
                if idx < 4:
                    tt_("pool", QR[idx][0][:, c0:c0 + w], t1[:, 0:w], t2[:, 0:w], ALU.add, [rt1, rt2], [QR[idx][1]])
                else:
                    tt_("pool", KR[:, c0:c0 + w], t1[:, 0:w], t2[:, 0:w], ALU.add, [rt1, rt2], [rKR])
                    if ti == 3:
                        tt_("pool", KF[:, 0:128], t1[:, 384:512], t2[:, 384:512], ALU.add, [rt1, rt2], [rKF])
                    if ti == 4:
                        tt_("pool", KF[:, 128:256], t1[:, 0:128], t2[:, 0:128], ALU.add, [rt1, rt2], [rKF])
        if stop == "B1":
            return
        pb, pr = bank()
        tr(pb[:, 0:128], KF[:, 0:128], 128, [rKF], [pr])
        tr(pb[:, 128:256], KF[:, 128:256], 128, [rKF], [pr])
        cp("act", KO, pb[:, 0:256], [pr], [rKO])
        dma("sp", wk_p[i], KO[:, 0:128], [rKO], [], "ko")
        dma("sp", wk_s[i][:, 120:128, :].rearrange("s t d -> t s d"), KO[:, 128:256], [rKO], [], "ko")
        if stop == "B15":
            return
        s = 5 % 2
        w3 = WQ[s][0].rearrange("p (k n) -> p k n", k=8)
        vb3 = VB.rearrange("p (b d) -> p b d", b=17)
        for blk in range(17):
            pb, pr = bank()
            for k in range(8):
                mm(pb[:, 0:128], XNt[:, k, blk * 128:(blk + 1) * 128], w3[:, k, :], k == 0, k == 7, [WQ[s][1], rXN[k]], [pr])
            cp("act", vb3[:, blk, :], pb[:, 0:128], [pr], [rVB])
            if blk == 15:
                cp("dve", VF[:, 0:128], pb[:, 0:128], [pr], [rVF])
            if blk == 16:
                cp("dve", VF[:, 128:256], pb[:, 0:128], [pr], [rVF])
        dma("sp", wv_p[i], VF[:, 0:128], [rVF], [], "vf")
        dma("sp", wv_s[i][:, 120:128, :].rearrange("s t d -> t s d"), VF[:, 128:256], [rVF], [], "vf")

        if stop == "B2":
            return
        def att_p1(blk, c, h2, u):
            c0 = blk * 128
            ps_, prs = bank()
            lo, hi = 64 * h2, 64 * h2 + 64
            qv = QR[c][0][lo:hi, c0:c0 + 128]
            mm(ps_[:, 128:256], KR[lo:hi, c0:c0 + 128], qv, True, False, [rKR, QR[c][1]], [prs])
            mm(ps_[:, 128:256], cb("ident"), cb("mcur") if blk < 16 else cb("msnew"), False, True, [rCB], [prs])
            if 0 < blk < 16:
                mm(ps_[:, 0:128], KR[lo:hi, c0 - 128:c0], qv, True, False, [rKR, QR[c][1]], [prs])
                mm(ps_[:, 0:128], cb("ident"), cb("mprev"), False, True, [rCB], [prs])
            if blk == 16:
                for sq_ in range(NSEQ):
                    mm(scol(ps_[:, 0:128], sq_), kht3[lo:hi, sq_, :], scol(QR[c][0][lo:hi, c0:c0 + 128], sq_),
                       True, True, [rKHT, QR[c][1]], [prs])
            e, re = EB[u % 2]
            if blk < 16:
                first = 128 if blk == 0 else 0
                act(e[:, first:256], ps_[:, first:256], AF.Exp, [prs], [re], scale=0.125)
            else:
                act(e[:, 128:256], ps_[:, 128:256], AF.Exp, [prs], [re], scale=0.125)
                t1, rt1 = TMP[4]
                tt_("dve", t1[:, 0:128], ps_[:, 0:128], cb("mshist"), ALU.add, [prs, rCB], [rt1])
                act(e[:, 0:128], t1[:, 0:128], AF.Exp, [rt1], [re], scale=0.125)

        def att_p2(blk, c, h2, u):
            c0 = blk * 128
            lo, hi = 64 * h2, 64 * h2 + 64
            e, re = EB[u % 2]
            po, pro = bank()
            rc, rrc = REC[u % 2]
            esk = MISC[lo:hi, 2 + 4 * i + c:3 + 4 * i + c]
            if blk < 16:
                if blk > 0:
                    mm(po[:, 0:128], vb3[:, blk - 1, :], e[:, 0:128], True, False, [rVB, re], [pro])
                mm(po[:, 0:128], vb3[:, blk, :], e[:, 128:256], blk == 0, True, [rVB, re], [pro])
                if blk > 0:
                    mm(po[:, 128:256], cb("ones"), e[:, 0:128], True, False, [rCB, re], [pro])
                mm(po[:, 128:256], cb("ones"), e[:, 128:256], blk == 0, True, [rCB, re], [pro])
                ts_("dve", rc[lo:hi, :], po[lo:hi, 128:256], esk, ALU.add, [pro, rMISC], [rrc])
                S.op("dve", lambda e_, rc=rc, lo=lo, hi=hi: e_.reciprocal(out=rc[lo:hi, :], in_=rc[lo:hi, :]),
                     reads=[rrc], writes=[rrc])
                tt_("dve", OT[c][lo:hi, c0:c0 + 128], po[lo:hi, 0:128], rc[lo:hi, :], ALU.mult,
                    [pro, rrc], [rOT[c]])
            else:
                mm(po[:, 0:128], vb3[:, 16, :], e[:, 128:256], True, True, [rVB, re], [pro])
                mm(po[:, 128:256], cb("ones"), e[:, 128:256], True, True, [rCB, re], [pro])
                for sq_ in range(NSEQ):
                    mm(scol(po[:, 256:384], sq_), vh3[:, sq_, :], scol(e[:, 0:128], sq_), True, True, [rVH, re], [pro])
                mm(po[:, 384:512], cb("ones"), e[:, 0:128], True, True, [rCB, re], [pro])
                cp("act", OA[lo:hi, :], po[lo:hi, 256:384], [pro], [rOA])
                t2, rt2 = TMP[5]
                cp("act", t2[lo:hi, 0:128], po[lo:hi, 384:512], [pro], [rt2])
                stt(rc[lo:hi, :], po[lo:hi, 128:256], esk, t2[lo:hi, 0:128], ALU.add, ALU.add,
                    [pro, rMISC, rt2], [rrc])
                S.op("dve", lambda e_, rc=rc, lo=lo, hi=hi: e_.reciprocal(out=rc[lo:hi, :], in_=rc[lo:hi, :]),
                     reads=[rrc], writes=[rrc])
                tt_("dve", OA[lo:hi, :], OA[lo:hi, :], po[lo:hi, 0:128], ALU.add, [rOA, pro], [rOA])
                tt_("dve", OT[c][lo:hi, c0:c0 + 128], OA[lo:hi, :], rc[lo:hi, :], ALU.mult,
                    [rOA, rrc], [rOT[c]])

        units = [(blk, c, h2) for blk in range(17 if stop != "B3" else 16) for c in range(4) for h2 in range(2)]
        att_p1(*units[0], 0)
        for u, un in enumerate(units):
            if u + 1 < len(units):
                att_p1(*units[u + 1], u + 1)
            att_p2(*un, u)
        wo3 = WO2.rearrange("p (k n) -> p k n", k=4)
        for n in range(8):
            for (c0, w) in TT:
                pb, pr = bank()
                for k in range(4):
                    mm(pb[:, 0:w], wo3[:, k, n * 128:(n + 1) * 128], OT[k][:, c0:c0 + w], k == 0, k == 3,
                       [rWO2, rOT[k]], [pr])
                resid_add(n, c0, w, pb, pr)

    def odd_phase(i):
        norm_phase("nmo%d" % i)
        S.barrier()
        AR.reset()
        TW = 256
        TILES = [(k * TW, TW) for k in range(8)] + [(TP, TS)]
        WIN, rWIN = AR.bf(10 * 1024, "win")
        rWINt = [Res("win%d" % t) for t in range(10)]
        WOo, rWOo = AR.bf(4 * 1024, "woo")
        WDT, rWDT = AR.bf(8 * 64, "wdt")
        DTT, rDTT = AR.f32(17 * 64, "dtt")
        D2, rD2 = AR.f32(TW, "d2")
        XPAD = [AR.f32(3 + TW, "xpad%d" % k) for k in range(6)]
        ACC = [AR.f32(TW, "acco%d" % k) for k in range(1)] * 2
        XS = [[AR.bf(TW, "xs%d_%d" % (p, k)) for k in range(4)] for p in range(2)]
        BT = [AR.bf(TW, "bt%d" % p) for p in range(2)]
        CT = [AR.bf(TW, "ct%d" % p) for p in range(2)]
        ZS = [[AR.bf(TW, "zs%d_%d" % (p, k)) for k in range(4)] for p in range(2)]
        YN = [AR.bf(TW, "yn%d" % k) for k in range(4)]
        TLP, rTLP = AR.f32(6 * 3, "tlp")
        TLS, rTLS = AR.f32(6 * 48, "tls")
        STT_, rSTT = AR.f32(768, "sttail")
        XDT = [AR.bf(512, "xdt%d" % p) for p in range(2)]
        BTOK = [AR.bf(128, "btok%d" % p) for p in range(2)]
        DEND = [AR.f32(8, "dend%d" % p) for p in range(2)]
        CBM = [AR.f32(128, "cbm%d" % p) for p in range(2)]
        WT_ = [[AR.bf(512, "wt%d_%d" % (p, h)) for h in range(2)] for p in range(2)]
        CS_ = [[AR.bf(512, "cs%d_%d" % (p, h)) for h in range(2)] for p in range(2)]
        CDEC = [[AR.f32(4, "cdec%d_%d" % (p, h)) for h in range(2)] for p in range(2)]
        RR = [AR.f32(512, "rr%d" % h) for h in range(2)]
        DEC = [AR.bf(512, "dec%d" % h) for h in range(2)]
        EXPA = [AR.f32(512, "expa%d" % h) for h in range(2)]
        XDD, rXDD = AR.bf(256, "xdd")
        YV2 = [[AR.f32(128, "yv%d_%d" % (p, k)) for k in range(4)] for p in range(2)]
        SQY, rSQY = AR.bf(512, "sqy")
        RSTD, rRSTD = AR.f32(128, "rstd")
        u0 = AR.off
        HT, rHT = AR.f32(512, "ht")
        HTB, rHTB = AR.bf(512, "htb")
        TMPH, rTMPH = AR.f32(256, "tmph")
        SOUT, rSOUT = AR.f32(512, "sout")
        u1 = AR.off
        AR.off = u0
        H0 = [AR.f32(256, "h0%d" % k) for k in range(2)]
        H0T = [AR.bf(256, "h0t%d" % k) for k in range(2)]
        HFIN = [AR.f32(256, "hfin%d" % k) for k in range(3)]
        BM = [AR.bf(128, "bm%d" % k) for k in range(2)]
        CD, rCD = AR.f32(32, "cd")
        DTAX, rDTAX = AR.f32(256, "dtax")
        YOFF, rYOFF = AR.f32(512, "yoff")
        AR.off = max(AR.off, u1)
        dtt3 = DTT.rearrange("p (c h) -> p c h", c=17)

        wload(WDT.rearrange("p (k n) -> p k n", k=8), w_dt2[i], rWDT, "wdt")
        wdt3 = WDT.rearrange("p (k n) -> p k n", k=8)
        for (c0, w) in TILES:
            pb, pr = bank()
            for k in range(8):
                mm(pb[0:64, 0:w], wdt3[:, k, :], XNt[:, k, c0:c0 + w], k == 0, k == 7, [rWDT, rXN[k]], [pr])
            act(D2[0:64, 0:w], pb[0:64, 0:w], AF.Exp, [pr, rVEC], [rD2], bias=vcol("dtb%d" % i, p1=64))
            act(D2[0:64, 0:w], D2[0:64, 0:w], AF.Ln, [rD2], [rD2], bias=1.0)
            ts_("pool", D2[32:64, 0:w], D2[32:64, 0:w], MISC[32:64, 10 + i:11 + i], ALU.mult, [rD2, rMISC], [rD2])
            for q in range(w // 128):
                ch = (c0 + q * 128) // 128
                pb2, pr2 = bank()
                tr(pb2[:, 0:64], D2[0:64, q * 128:(q + 1) * 128], 64, [rD2], [pr2])
                cp("act", dtt3[:, ch, :], pb2[:, 0:64], [pr2], [rDTT])

        win3 = WIN.rearrange("p (t k n) -> p t k n", t=10, k=8)
        wo3 = WOo.rearrange("p (k n) -> p k n", k=4)

        def produce(g, ti, cidx):
            c0, w = TILES[ti]
            par = ti % 2
            sample = ti == 8
            for k in range(6):
                xp_, rxp = XPAD[k]
                if sample:
                    ccol = cidx[k] * 128
                    dma("sp", STT_[0:48, 0:128], st_cc[i][:, ccol:ccol + 128], [], [rSTT], "stt")
                    pb, pr = bank()
                    tr(pb[:, 0:48], STT_[0:48, 0:128], 48, [rSTT], [pr])
                    cp("dve", xp_[:, 0:48].rearrange("p (k s) -> p k s", s=NSEQ),
                       pb[:, 0:48].rearrange("p (s k) -> p k s", k=3), [pr], [rxp])
                pb, pr = bank()
                for kk in range(8):
                    mm(pb[:, 0:w], win3[:, k, kk, :], XNt[:, kk, c0:c0 + w], kk == 0, kk == 7, [rWINt[k], rXN[kk]], [pr])
                doff = 48 if sample else 3
                cp("act", xp_[:, doff:doff + w], pb[:, 0:w], [pr], [rxp])
                step = 16 if sample else 1
                acc, racc = ACC[k % 2]
                act(acc[:, 0:w], xp_[:, 0:w], AF.Identity, [rxp, rVEC], [racc], bias=vcol("ccb%d" % i, cidx[k]),
                    scale=vcol("ccw%d" % i, cidx[k] * 4))
                for kq in range(1, 4):
                    stt(acc[:, 0:w], xp_[:, kq * step:kq * step + w], vcol("ccw%d" % i, cidx[k] * 4 + kq), acc[:, 0:w],
                        ALU.mult, ALU.add, [rxp, rVEC, racc], [racc])
                dst, rdst = (XS[par][k] if k < 4 else (BT[par] if k == 4 else CT[par]))
                act(dst[:, 0:w], acc[:, 0:w], AF.Silu, [racc], [rdst])
                if ti == 7:
                    cp("pool", TLP[:, k * 3:(k + 1) * 3], xp_[:, TW:TW + 3], [rxp], [rTLP])
                if sample:
                    cp("pool", TLS[:, k * 48:(k + 1) * 48], xp_[:, 48 + 80:48 + 128], [rxp], [rTLS])
                elif ti < 7:
                    cp("pool", xp_[:, 0:3], xp_[:, TW:TW + 3], [rxp], [rxp])
            for q in range(4):
                pb, pr = bank()
                for kk in range(8):
                    mm(pb[:, 0:w], win3[:, 6 + q, kk, :], XNt[:, kk, c0:c0 + w], kk == 0, kk == 7, [rWINt[6 + q], rXN[kk]], [pr])
                act(ZS[par][q][0][:, 0:w], pb[:, 0:w], AF.Silu, [pr], [ZS[par][q][1]])

        def prep(g, ch):
            ti, q = (ch // 2, ch % 2) if ch < 16 else (8, 0)
            par, cp_ = ti % 2, ch % 2
            cs_ = slice(q * 128, q * 128 + 128)
            sample = ch == 16
            tri = cf("tri_s") if sample else cf("tri")
            ut = cf("ut_s") if sample else cf("ut")
            cbm = cb("cbm_s") if sample else cb("cbm_p")
            dt_g = dtt3[:, ch, 8 * g:8 * g + 8]
            dta_g = dtt3[:, ch, 32 + 8 * g:32 + 8 * g + 8]
            bt, rbt = BT[par]
            ct, rct = CT[par]
            xdt, rxdt = XDT[cp_]
            btok, rbtok = BTOK[cp_]
            dend, rdend = DEND[cp_]
            cbmb, rcbm = CBM[cp_]
            pb, pr = bank()
            mm(pb[:, 0:128], bt[:, cs_], ct[:, cs_], True, True, [rbt, rct], [pr])
            tt_("dve", cbmb, pb[:, 0:128], cbm, ALU.mult, [pr, rCB], [rcbm])
            pb, pr = bank()
            for cc in range(4):
                mm(pb[:, cc * 128:(cc + 1) * 128], XS[par][cc][0][:, cs_], cb("ident"), True, True, [XS[par][cc][1], rCB], [pr])
            tt_("dve", xdt.rearrange("p (h d) -> p h d", h=8), pb[:].rearrange("p (h d) -> p h d", h=8),
                dt_g.unsqueeze(2).broadcast_to([128, 8, 64]), ALU.mult, [pr, rDTT], [rxdt])
            pb, pr = bank()
            mm(pb[:, 0:128], bt[:, cs_], cb("ident"), True, True, [rbt, rCB], [pr])
            cp("act", btok, pb[:, 0:128], [pr], [rbtok])
            pb, pr = bank()
            mm(pb[:, 0:8], ut, dta_g, True, True, [rCF, rDTT], [pr])
            act(dend, pb[:, 0:8], AF.Exp, [pr], [rdend])
            for hh in range(2):
                hs = slice(4 * hh, 4 * hh + 4)
                rr, rrr = RR[hh]
                dec, rdec = DEC[hh]
                expa, rexpa = EXPA[hh]
                wt, rwt = WT_[cp_][hh]
                cs2, rcs = CS_[cp_][hh]
                cdec, rcdec = CDEC[cp_][hh]
                tt_("pool", rr.rearrange("p (h l) -> p h l", h=4), tri.unsqueeze(1).broadcast_to([128, 4, 128]),
                    dta_g[:, hs].unsqueeze(2).broadcast_to([128, 4, 128]), ALU.mult, [rCF, rDTT], [rrr])
                pd, prd = bank()
                pa, pra = bank()
                mm(pd[:], ut, rr, True, True, [rCF, rrr], [prd])
                mm(pa[:], cf("ones"), rr, True, True, [rCF, rrr], [pra])
                act(dec, pd[:], AF.Exp, [prd], [rdec])
                act(expa, pa[:], AF.Exp, [pra], [rexpa])
                tt_("dve", wt.rearrange("p (h l) -> p h l", h=4), dec.rearrange("p (h l) -> p h l", h=4),
                    cbmb.unsqueeze(1).broadcast_to([128, 4, 128]), ALU.mult, [rdec, rcbm], [rwt])
                tt_("pool", cs2.rearrange("p (h l) -> p h l", h=4), expa.rearrange("p (h l) -> p h l", h=4),
                    ct[:, cs_].unsqueeze(1).broadcast_to([128, 4, 128]), ALU.mult, [rexpa, rct], [rcs])
                cp("pool", cdec, expa.rearrange("p (h l) -> p h l", h=4)[:, :, 127], [rexpa], [rcdec])

        def stage_b(g, ch):
            ti, q = (ch // 2, ch % 2) if ch < 16 else (8, 0)
            par, cp_ = ti % 2, ch % 2
            cs_ = slice(q * 128, q * 128 + 128)
            sample = ch == 16
            first = ch == 0
            YV = YV2[cp_]
            dta_g = dtt3[:, ch, 32 + 8 * g:32 + 8 * g + 8]
            xdt, rxdt = XDT[cp_]
            btok, rbtok = BTOK[cp_]
            dend, rdend = DEND[cp_]
            for hh in range(2):
                hs = slice(4 * hh, 4 * hh + 4)
                wt, rwt = WT_[cp_][hh]
                cs2, rcs = CS_[cp_][hh]
                cdec, rcdec = CDEC[cp_][hh]
                py, pry = bank(hold=True)
                use_off = (not sample) and (not first)
                for h in range(4):
                    j2 = (4 * hh + h) // 2
                    mm(py[:, h * 128:(h + 1) * 128], xdt[:, j2 * 128:(j2 + 1) * 128], wt[:, h * 128:(h + 1) * 128],
                       True, not use_off, [rxdt, rwt], [pry])
                    if use_off:
                        mm(py[:, h * 128:(h + 1) * 128], HTB[:, j2 * 128:(j2 + 1) * 128], cs2[:, h * 128:(h + 1) * 128],
                           False, True, [rHTB, rcs], [pry])
                tt_("pool", XDD.rearrange("p (h d) -> p h d", h=4), xdt.rearrange("p (h d) -> p h d", h=8)[:, hs, :],
                    dend[:, hs].unsqueeze(2).broadcast_to([128, 4, 64]), ALU.mult, [rxdt, rdend], [rXDD])
                if sample:
                    tt_("pool", DTAX.rearrange("p (h d) -> p h d", h=4),
                        dta_g[:, hs].unsqueeze(2).broadcast_to([128, 4, 64]),
                        cf("ones", 64).unsqueeze(1).broadcast_to([128, 4, 64]), ALU.mult, [rDTT, rCF], [rDTAX])
                    pc, prc = bank()
                    for pr_i in range(2):
                        mm(pc[:, pr_i * 16:(pr_i + 1) * 16], DTAX[:, pr_i * 128:(pr_i + 1) * 128], cf("seqmask", 16),
                           True, True, [rDTAX, rCF], [prc])
                    act(CD, pc[:, 0:32], AF.Exp, [prc], [rCD])
                    po_, pro_ = bank(hold=True)
                    hd0 = 8 * g + 4 * hh
                    def h0_load(b):
                        src = st_ssm[i][b, hd0:hd0 + 4].rearrange("(j h2) p n -> (h2 p) j n", h2=2)
                        dma("sp", H0[b % 2][0].rearrange("p (j n) -> p j n", j=2), src, [], [H0[b % 2][1]], "h0%d" % (b % 2))

                    def hf_store(b):
                        hf, rhf = HFIN[b % 3]
                        dst = ss_s[i][b, hd0:hd0 + 4].rearrange("(j h2) p n -> (h2 p) j n", h2=2)
                        dma("sp", dst, hf.rearrange("p (j n) -> p j n", j=2), [rhf], [], "hf%d" % (b % 3))

                    h0_load(0)
                    for b in range(NSEQ):
                        h0, rh0 = H0[b % 2]
                        h0t, rh0t = H0T[b % 2]
                        hf, rhf = HFIN[b % 3]
                        bm, rbm = BM[b % 2]
                        pt, prt = bank()
                        for pr_i in range(2):
                            tr(pt[:, pr_i * 128:(pr_i + 1) * 128], h0[:, pr_i * 128:(pr_i + 1) * 128], 128, [rh0], [prt])
                        cp("act", h0t, pt[:, 0:256], [prt], [rh0t])
                        for h in range(4):
                            pr_i = h // 2
                            mm(scol(po_[:, h * 128:(h + 1) * 128], b), h0t[:, pr_i * 128:(pr_i + 1) * 128],
                               scol(cs2[:, h * 128:(h + 1) * 128], b), True, True, [rh0t, rcs], [pro_])
                        ts_("pool", bm, btok, cf("seqmask", 16)[:, b:b + 1], ALU.mult, [rbtok, rCF], [rbm])
                        pf, prf = bank()
                        for pr_i in range(2):
                            mm(pf[:, pr_i * 128:(pr_i + 1) * 128], XDD[:, pr_i * 128:(pr_i + 1) * 128], bm, True, True,
                               [rXDD, rbm], [prf])
                        for pr_i in range(2):
                            stt(hf[:, pr_i * 128:(pr_i + 1) * 128], h0[:, pr_i * 128:(pr_i + 1) * 128],
                                CD[:, pr_i * 16 + b:pr_i * 16 + b + 1], pf[:, pr_i * 128:(pr_i + 1) * 128],
                                ALU.mult, ALU.add, [rh0, rCD, prf], [rhf])
                        if b + 1 < NSEQ:
                            h0_load(b + 1)
                        if b > 0:
                            hf_store(b - 1)
                    hf_store(NSEQ - 1)
                    cp("act", YOFF, po_[:], [pro_], [rYOFF])
                    release(po_)
                for j2l in range(2):
                    cc = 2 * hh + j2l
                    yv, ryv = YV[cc]
                    xs_, rxs = XS[par][cc]
                    for h2 in range(2):
                        lo, hi = 64 * h2, 64 * h2 + 64
                        hsel = 2 * j2l + h2
                        stt(yv[lo:hi, :], xs_[lo:hi, cs_], vcol("dsk%d" % i, 4 * g + cc, p0=lo, p1=hi),
                            py[lo:hi, hsel * 128:(hsel + 1) * 128], ALU.mult, ALU.add, [rxs, rVEC, pry], [ryv])
                        if sample:
                            tt_("dve", yv[lo:hi, :], yv[lo:hi, :], YOFF[lo:hi, hsel * 128:(hsel + 1) * 128], ALU.add,
                                [ryv, rYOFF], [ryv])
                release(py)
                if not sample:
                    pst, prst = bank()
                    mm(pst[:, 0:256], btok, XDD, True, True, [rbtok, rXDD], [prst])
                    hcol = slice(256 * hh, 256 * hh + 256)
                    if first:
                        cp("dve", HT[:, hcol], pst[:, 0:256], [prst], [rHT])
                    else:
                        tt_("pool", TMPH.rearrange("p (h d) -> p h d", h=4), HT[:, hcol].rearrange("p (h d) -> p h d", h=4),
                            cdec.unsqueeze(2).broadcast_to([128, 4, 64]), ALU.mult, [rHT, rcdec], [rTMPH])
                        tt_("dve", HT[:, hcol], TMPH, pst[:, 0:256], ALU.add, [rTMPH, prst], [rHT])
                    cp("act", HTB[:, hcol], HT[:, hcol], [rHT], [rHTB])

        def stage_c(g, ch):
            ti, q = (ch // 2, ch % 2) if ch < 16 else (8, 0)
            par, cp_ = ti % 2, ch % 2
            cs_ = slice(q * 128, q * 128 + 128)
            YV = YV2[cp_]
            for cc in range(4):
                yv, ryv = YV[cc]
                tt_("pool", yv, yv, ZS[par][cc][0][:, cs_], ALU.mult, [ryv, ZS[par][cc][1]], [ryv])
                act(SQY[:, cc * 128:(cc + 1) * 128], yv, AF.Square, [ryv], [rSQY])
            pb, pr = bank()
            for cc in range(4):
                mm(pb[:, 0:128], cb("o512"), SQY[:, cc * 128:(cc + 1) * 128], cc == 0, cc == 3, [rCB, rSQY], [pr])
            act(RSTD, pb[:, 0:128], AF.Ln, [pr, rMISC], [rRSTD], bias=MISC[:, 0:1], scale=1.0)
            act(RSTD, RSTD, AF.Exp, [rRSTD], [rRSTD], scale=-0.5)
            for cc in range(4):
                yv, ryv = YV[cc]
                stt(YN[cc][0][:, cs_], yv, vcol("gn%d" % i, 4 * g + cc), RSTD, ALU.mult, ALU.mult,
                    [ryv, rVEC, rRSTD], [YN[cc][1]])

        def outproj(g, ti):
            c0, w = TILES[ti]
            for n in range(8):
                pb, pr = bank()
                for k in range(4):
                    mm(pb[:, 0:w], wo3[:, k, n * 128:(n + 1) * 128], YN[k][0][:, 0:w], k == 0, k == 3,
                       [rWOo, YN[k][1]], [pr])
                resid_add(n, c0, w, pb, pr)
            if ti == 7:
                pb, pr = bank()
                for q in range(4):
                    tr(pb[:, q * 128:(q + 1) * 128], HT[:, q * 128:(q + 1) * 128], 128, [rHT], [pr])
                cp("act", SOUT, pb[:], [pr], [rSOUT])
                dma("sp", ss_p[i][8 * g:8 * g + 8].rearrange("(j h2) p n -> (h2 p) j n", h2=2),
                    SOUT.rearrange("p (j n) -> p j n", j=4), [rSOUT], [], "sout")

        def load_win(gg):
            cols = [2048 + 512 * gg + 128 * q for q in range(4)] + [4096 + 128 * gg, 4608 + 128 * gg] + \
                   [512 * gg + 128 * q for q in range(4)]
            for t_, col in enumerate(cols):
                dma("pool", win3[:, t_, :, :], w_in_o[i][:, col:col + 128].rearrange("(k p) n -> p k n", p=128),
                    [], [rWINt[t_]], "win%d" % t_)

        for g in range(4):
            S.barrier()
            if g == 0:
                load_win(0)
            dma("pool", wo3, w_out_o[i][512 * g:512 * (g + 1), :].rearrange("(k p) n -> p k n", p=128), [], [rWOo], "woo")
            cidx = [4 * g + q for q in range(4)] + [16 + g, 20 + g]
            for k in range(6):
                memset("pool", XPAD[k][0][:, 0:3], 0.0, [XPAD[k][1]])
            def stage_a(ch):
                prep(g, ch)

            def stage_c_full(ch):
                stage_c(g, ch)
                if ch == 16:
                    outproj(g, 8)
                elif ch % 2 == 1:
                    outproj(g, ch // 2)

            produce(g, 0, cidx)
            produce(g, 1, cidx)
            stage_a(0)
            stage_a(1)
            stage_b(g, 0)
            for s_ in range(0, 14):
                stage_c_full(s_)
                if s_ % 2 == 1:
                    produce(g, (s_ + 3) // 2, cidx)
                    if s_ == 13 and g < 3:
                        load_win(g + 1)
                stage_a(s_ + 2)
                stage_b(g, s_ + 1)
            stage_a(16)
            stage_b(g, 15)
            stage_c_full(14)
            stage_c_full(15)
            S.barrier()
            stage_b(g, 16)
            stage_c_full(16)
            ocols = [512 * g + 128 * q for q in range(4)] + [2048 + 128 * g, 2560 + 128 * g]
            for k in range(6):
                pb, pr = bank()
                tr(pb[0:3, 0:128], TLP[:, k * 3:(k + 1) * 3], 128, [rTLP], [pr])
                tr(pb[0:48, 128:256], TLS[:, k * 48:(k + 1) * 48], 128, [rTLS], [pr])
                cp("act", STT_[0:3, 256:384], pb[0:3, 0:128], [pr], [rSTT])
                cp("act", STT_[0:48, 512:640], pb[0:48, 128:256], [pr], [rSTT])
                dma("sp", cc_p[i][:, ocols[k]:ocols[k] + 128], STT_[0:3, 256:384], [rSTT], [], "stt")
                dma("sp", cc_s[i][:, :, ocols[k]:ocols[k] + 128].rearrange("s t d -> t s d"), STT_[0:48, 512:640],
                    [rSTT], [], "stt")

    if phases is None:
        phases = ["e0", "f0", "o0", "f1", "e1", "f2", "o1", "f3"]
    for ph in phases:
        if ph[0] == "e":
            even_phase(int(ph[1]), ph[3:] if len(ph) > 2 else None)
        elif ph[0] == "o":
            odd_phase(int(ph[1]))
        elif ph[0] == "f":
            ffn_phase(int(ph[1]))

    S.barrier()
    AR.reset()
    yo = [AR.f32(1024, "yo%d" % k) for k in range(2)]
    ys3 = y_s.rearrange("(s t) d -> t s d", t=8)
    for rt in range(17):
        buf, rb = yo[rt % 2]
        for half in range(2):
            pb, pr = bank()
            for q in range(4):
                c = half * 4 + q
                tr(pb[:, q * 128:(q + 1) * 128], Xt[:, c, rt * 128:(rt + 1) * 128], 128, rX[c], [pr])
            cp("act" if half == 0 else "dve", buf[:, half * 512:(half + 1) * 512], pb[:], [pr], [rb])
        if rt < 16:
            dma("sp", y_p[rt * 128:(rt + 1) * 128, :], buf, [rb], [], "yo%d" % (rt % 2))
        else:
            dma("sp", ys3, buf, [rb], [], "yo%d" % (rt % 2))

    S.emit(st)
    st.close()
    return nc


_CACHE = {}


def make_in_maps(inp):
    cfh, cbh = host_consts()
    vecs = host_vecs(inp)
    posrow = np.concatenate([np.arange(TP, dtype=np.float32),
                             np.repeat(8192.0 + np.arange(8, dtype=np.float32), NSEQ)])[None, :].astype(np.float32)
    qperm = np.concatenate([np.concatenate([np.arange(64 * c, 64 * c + 64), np.arange(64 * (4 + c), 64 * (4 + c) + 64)])
                            for c in range(4)])
    w_in_e = inp["w_in_e"].copy()
    w_in_e[:, :, 1024:1536] = inp["w_in_e"][:, :, 1024 + qperm]
    w_out_e = inp["w_out_e"].copy()
    w_out_e[:, 512:1024, :] = inp["w_out_e"][:, 512 + qperm, :]
    w_dt2 = np.ascontiguousarray(np.concatenate([inp["w_in_o"][:, :, 5120:5152]] * 2, axis=2))
    shared = dict(posrow=posrow, cf=cfh, cb=cbh, vecs=vecs, w_in_e=w_in_e, w_out_e=w_out_e, w_in_o=inp["w_in_o"],
                  w_dt2=w_dt2, w_out_o=inp["w_out_o"], w_gate=inp["w_gate"], w_up=inp["w_up"], w_down=inp["w_down"])
    in_maps = []
    for c in range(NCORES):
        sl = slice(NSEQ * c, NSEQ * (c + 1))
        m = dict(shared)
        m["xp"] = np.ascontiguousarray(inp["x_prompt"][c])
        m["xs"] = np.ascontiguousarray(inp["x_sample"][sl].reshape(TS, D))
        m["st_ca"] = np.ascontiguousarray(inp["state_conv_a"][:, sl].reshape(2, 480, 512))
        m["st_k"] = np.ascontiguousarray(inp["cache_win_k"][:, sl].reshape(2, NSEQ, 128, 128))
        m["st_v"] = np.ascontiguousarray(inp["cache_win_v"][:, sl].reshape(2, NSEQ, 128, 128))
        m["st_cc"] = np.ascontiguousarray(inp["state_conv_c"][:, sl].reshape(2, 48, 3072))
        m["st_ssm"] = np.ascontiguousarray(inp["state_ssm"][:, sl])
        m["st_ff"] = np.ascontiguousarray(inp["state_ffn_conv"][:, sl].reshape(4, 32, DFF))
        in_maps.append(m)
    return in_maps


def kernel(**inp):
    inp = {k: np.asarray(v) for k, v in inp.items()}
    if "nc" not in _CACHE:
        _CACHE["nc"] = build_program()
    nc = _CACHE["nc"]
    in_maps = make_in_maps(inp)
    res = run_bass_kernel_spmd(nc, in_maps, core_ids=list(range(NCORES)))
    R = res.results

    def cat(name, axis, shp=None):
        parts = [np.asarray(R[c][name]) if shp is None else np.asarray(R[c][name]).reshape(shp) for c in range(NCORES)]
        return np.concatenate(parts, axis=axis)

    y_prompt = np.stack([np.asarray(R[c]["y_p"]) for c in range(NCORES)], 0)
    y_sample = cat("y_s", 0, (NSEQ, 8, D))
    ca_p_ = np.stack([np.asarray(R[c]["ca_p"]) for c in range(NCORES)], 1)
    ca_s_ = cat("ca_s", 1)
    wk_p_ = np.stack([np.asarray(R[c]["wk_p"]).reshape(2, 128, 2, 64) for c in range(NCORES)], 1)
    wk_s_ = cat("wk_s", 1, (2, NSEQ, 128, 2, 64))
    wv_p_ = np.stack([np.asarray(R[c]["wv_p"]).reshape(2, 128, 2, 64) for c in range(NCORES)], 1)
    wv_s_ = cat("wv_s", 1, (2, NSEQ, 128, 2, 64))
    cc_p_ = np.stack([np.asarray(R[c]["cc_p"]) for c in range(NCORES)], 1)
    cc_s_ = cat("cc_s", 1)
    ss_p_ = np.stack([np.asarray(R[c]["ss_p"]) for c in range(NCORES)], 1)
    ss_s_ = cat("ss_s", 1)
    ff_p_ = np.stack([np.asarray(R[c]["ff_p"]) for c in range(NCORES)], 1)
    ff_s_ = cat("ff_s", 1)
    outs = (y_prompt, y_sample, ca_p_, ca_s_, wk_p_, wk_s_, wv_p_, wv_s_, cc_p_, cc_s_, ss_p_, ss_s_, ff_p_, ff_s_)
    return tuple(np.ascontiguousarray(o, dtype=np.float32) for o in outs)
```

```python
import contextlib
import math
import os
import numpy as np
import concourse.bass as bass
import concourse.mybir as mybir
from concourse.bass_utils import run_bass_kernel_spmd

F32 = mybir.dt.float32
BF16 = mybir.dt.bfloat16
I32 = mybir.dt.int32
AF = mybir.ActivationFunctionType
ALU = mybir.AluOpType

NCORES = 8
D = 1024
TP = 2048
TS = 128
T = TP + TS
NSEQ = 16
DFF = 2816
NJ = DFF // 128
EPS = 1e-6
NEG = -30000.0
ARENA_WORDS = 24300


class Res:
    __slots__ = ("name", "last_w", "readers", "dsem", "dcount", "excl")

    def __init__(self, name, excl=False):
        self.name = name
        self.excl = excl
        self.last_w = None
        self.readers = []
        self.dsem = None
        self.dcount = 0


class Op:
    __slots__ = ("eng", "fn", "waits", "signal", "sigval", "dma_res", "dma_val")

    def __init__(self, eng, fn):
        self.eng = eng
        self.fn = fn
        self.waits = []
        self.signal = False
        self.sigval = 0
        self.dma_res = None
        self.dma_val = 0


class Sched:
    ENGS = ("pe", "act", "dve", "pool", "sp")

    def __init__(self, nc):
        self.nc = nc
        self.ops = []
        self.dma_res = []
        self.last = {e: None for e in self.ENGS}
        self.dma_since = []
        self.pending = {e: [] for e in self.ENGS}

    def op(self, eng, fn, reads=(), writes=(), dma=None):
        o = Op(eng, fn)
        writes = list(writes) + [r for r in reads if r.excl and r not in writes]
        deps = [(d, True) for d in self.pending[eng]]
        self.pending[eng] = []
        for r in reads:
            if r.last_w is not None:
                deps.append((r.last_w, True))
        for w in writes:
            if w.last_w is not None:
                deps.append((w.last_w, False))
            deps.extend((x, False) for x in w.readers)
        seen = set()
        for d, raw in deps:
            if d.dma_res is None and d.eng == eng:
                if eng in ("pe", "sp") or not raw:
                    continue
            if id(d) in seen:
                continue
            seen.add(id(d))
            o.waits.append(d)
            d.signal = True
        if dma is not None:
            o.dma_res = dma
            dma.dcount += 1
            o.dma_val = 16 * dma.dcount
            o.signal = True
            if dma.dsem is None:
                dma.dsem = True
                self.dma_res.append(dma)
            self.dma_since.append(o)
        else:
            self.last[eng] = o
        for w in writes:
            w.last_w = o
            w.readers = []
        for r in reads:
            r.readers.append(o)
        self.ops.append(o)
        return o

    def barrier(self):
        tg = [o for o in self.last.values() if o is not None] + self.dma_since
        self.dma_since = []
        for e in self.ENGS:
            self.pending[e] = list(tg)

    def emit(self, stack):
        nc = self.nc
        esem = {e: stack.enter_context(nc.semaphore("s_" + e)) for e in self.ENGS}
        for i, r in enumerate(self.dma_res):
            r.dsem = stack.enter_context(nc.semaphore("d%d" % i))
        cnt = {e: 0 for e in self.ENGS}
        per = {e: [] for e in self.ENGS}
        for o in self.ops:
            if o.dma_res is None and o.signal:
                cnt[o.eng] += 1
                o.sigval = cnt[o.eng]
            per[o.eng].append(o)
        block = stack.enter_context(nc.Block())

        def run(eng_name, eng):
            known = {}
            for o in per[eng_name]:
                for d in o.waits:
                    if d.dma_res is not None:
                        key, val = d.dma_res.dsem, d.dma_val
                    else:
                        key, val = esem[d.eng], d.sigval
                    if known.get(id(key), 0) >= val:
                        continue
                    known[id(key)] = val
                    eng.wait_ge(key, val)
                ins = o.fn(eng)
                if o.dma_res is not None:
                    ins.then_inc(o.dma_res.dsem, 16)
                elif o.signal:
                    ins.then_inc(esem[eng_name], 1)
            if eng_name == "sp":
                for r in self.dma_res:
                    if known.get(id(r.dsem), 0) < 16 * r.dcount:
                        eng.wait_ge(r.dsem, 16 * r.dcount)
                for e2 in ("pe", "act", "dve", "pool"):
                    if cnt[e2] > 0:
                        eng.wait_ge(esem[e2], cnt[e2])

        @block.tensor
        def _(e):
            run("pe", e)

        @block.scalar
        def _(e):
            run("act", e)

        @block.vector
        def _(e):
            run("dve", e)

        @block.gpsimd
        def _(e):
            run("pool", e)

        @block.sync
        def _(e):
            run("sp", e)


class Arena:
    def __init__(self, t, words):
        self.t = t
        self.words = words
        self.off = 0

    def reset(self):
        self.off = 0

    def f32(self, n, name):
        a = self.off
        self.off += n
        assert self.off <= self.words, (name, self.off, self.words)
        return self.t[:, a:a + n], Res(name)

    def bf(self, n, name):
        w = (n + 1) // 2
        a = self.off
        self.off += w
        assert self.off <= self.words, (name, self.off, self.words)
        return self.t[:, a:a + w].bitcast(BF16)[:, 0:n], Res(name)


def vec_layout():
    ent = []
    for i in range(2):
        ent += [("nme%d" % i, 8), ("nmo%d" % i, 8), ("caw%d" % i, 124), ("cab%d" % i, 4), ("lng%d" % i, 4),
                ("lnb%d" % i, 4), ("qg%d" % i, 1), ("kg%d" % i, 1), ("snk%d" % i, 4), ("ccw%d" % i, 96),
                ("ccb%d" % i, 24), ("dtb%d" % i, 1), ("alog%d" % i, 1), ("dsk%d" % i, 16), ("gn%d" % i, 16)]
    for l in range(4):
        ent += [("nf%d" % l, 8), ("fcw%d" % l, 66), ("fcb%d" % l, 22)]
    ent += [("invf", 1)]
    off = {}
    o = 0
    for n, c in ent:
        off[n] = o
        o += c
    return off, o


CF = dict(ident=0, tri=128, ut=256, tri_s=384, ut_s=512, pmat=640, seqmask=768, ones=784)
NCF = 912
CB = dict(ident=0, o1024=128, o512=256, bd64=384, mcur=512, mprev=640, msnew=768, mshist=896, cbm_p=1024,
          cbm_s=1152, ones=1280)
NCB = 1408


def host_consts():
    cf = np.zeros((128, NCF), np.float32)
    cb = np.zeros((128, NCB), np.float32)
    idx = np.arange(128)
    eye = np.eye(128, dtype=np.float32)
    tri = (idx[:, None] <= idx[None, :]).astype(np.float32)
    ut = (idx[:, None] > idx[None, :]).astype(np.float32)
    tt_, ss_ = idx // 16, idx % 16
    same = ss_[:, None] == ss_[None, :]
    tri_s = (same & (tt_[:, None] <= tt_[None, :])).astype(np.float32)
    ut_s = (same & (tt_[:, None] > tt_[None, :])).astype(np.float32)
    pm = np.zeros((128, 128), np.float32)
    for hb in (0, 64):
        for d in range(8):
            pm[hb + d + 8, hb + d] = -1.0
            pm[hb + d, hb + d + 8] = 1.0
    cf[:, 0:128] = eye
    cf[:, 128:256] = tri
    cf[:, 256:384] = ut
    cf[:, 384:512] = tri_s
    cf[:, 512:640] = ut_s
    cf[:, 640:768] = pm
    cf[:, 768:784] = (ss_[:, None] == np.arange(16)[None, :]).astype(np.float32)
    cf[:, 784:912] = 1.0
    cb[:, 0:128] = eye
    cb[:, 128:256] = 1.0 / 1024
    cb[:, 256:384] = 1.0 / 512
    bd = np.zeros((128, 128), np.float32)
    bd[0:64, 0:64] = 1.0 / 64
    bd[64:128, 64:128] = 1.0 / 64
    cb[:, 384:512] = bd
    cb[:, 512:640] = np.where(idx[None, :] >= idx[:, None], 0.0, NEG)
    cb[:, 640:768] = np.where(idx[:, None] > idx[None, :], 0.0, NEG)
    cb[:, 768:896] = np.where(same & (tt_[:, None] <= tt_[None, :]), 0.0, NEG)
    cb[:, 896:1024] = np.where(idx[:, None] > tt_[None, :], 0.0, NEG)
    cb[:, 1024:1152] = tri
    cb[:, 1152:1280] = tri_s
    cb[:, 1280:1408] = 1.0
    return cf, cb


def host_vecs(inp):
    off, nv = vec_layout()
    v = np.zeros((128, nv), np.float32)

    def put(name, arr):
        arr = np.asarray(arr, np.float32)
        v[:, off[name]:off[name] + arr.shape[1]] = arr

    def chunks(x):
        return np.asarray(x).reshape(-1, 128).T

    def taps(w):
        K, C = w.shape
        return np.asarray(w).reshape(K, C // 128, 128).transpose(2, 1, 0).reshape(128, -1)

    def pair(x, n):
        x = np.asarray(x)
        return np.concatenate([np.repeat(x[0::2][None, :], 64, 0), np.repeat(x[1::2][None, :], 64, 0)], 0)

    for i in range(2):
        put("nme%d" % i, chunks(inp["norm_mix_e"][i]))
        put("nmo%d" % i, chunks(inp["norm_mix_o"][i]))
        put("caw%d" % i, taps(inp["conv_a_w"][i]))
        put("cab%d" % i, chunks(inp["conv_a_b"][i]))
        put("lng%d" % i, chunks(inp["ln_a_g"][i]))
        put("lnb%d" % i, chunks(inp["ln_a_b"][i]))
        put("qg%d" % i, np.tile(np.asarray(inp["q_norm_g"][i]), 2)[:, None])
        put("kg%d" % i, np.tile(np.asarray(inp["k_norm_g"][i]), 2)[:, None])
        sk = np.asarray(inp["sinks"][i])
        put("snk%d" % i, np.concatenate([np.repeat(sk[None, 0:4], 64, 0), np.repeat(sk[None, 4:8], 64, 0)], 0))
        put("ccw%d" % i, taps(inp["conv_c_w"][i]))
        put("ccb%d" % i, chunks(inp["conv_c_b"][i]))
        dtb = np.zeros((128, 1), np.float32)
        dtb[0:32, 0] = inp["dt_bias"][i]
        dtb[32:64, 0] = inp["dt_bias"][i]
        put("dtb%d" % i, dtb)
        al = np.zeros((128, 1), np.float32)
        al[32:64, 0] = inp["a_log"][i]
        put("alog%d" % i, al)
        put("dsk%d" % i, pair(inp["d_skip"][i], 16))
        put("gn%d" % i, chunks(inp["gnorm_c"][i]))
    for l in range(4):
        put("nf%d" % l, chunks(inp["norm_ffn"][l]))
        put("fcw%d" % l, taps(inp["ffn_conv_w"][l]))
        put("fcb%d" % l, chunks(inp["ffn_conv_b"][l]))
    invf = np.zeros((128, 1), np.float32)
    f = (500000.0 ** (-np.arange(0, 16, 2, dtype=np.float32) / 16.0)).astype(np.float32)
    for hb in (0, 64):
        invf[hb:hb + 8, 0] = f
        invf[hb + 8:hb + 16, 0] = f
    put("invf", invf)
    return v


def build_program(phases=None):
    nc = bass.Bass("TRN2", target_bir_lowering=False)
    VO, NV = vec_layout()

    def din(name, shape):
        return nc.dram_tensor(name, list(shape), F32, kind="ExternalInput").ap()

    def dout(name, shape):
        return nc.dram_tensor(name, list(shape), F32, kind="ExternalOutput").ap()

    xp_d = din("xp", [TP, D])
    xs_d = din("xs", [TS, D])
    pos_d = din("posrow", [1, T])
    cf_d = din("cf", [128, NCF])
    cb_d = din("cb", [128, NCB])
    vec_d = din("vecs", [128, NV])
    st_ca = din("st_ca", [2, 480, 512])
    st_k = din("st_k", [2, NSEQ, 128, 128])
    st_v = din("st_v", [2, NSEQ, 128, 128])
    st_cc = din("st_cc", [2, 48, 3072])
    st_ssm = din("st_ssm", [2, NSEQ, 32, 64, 128])
    st_ff = din("st_ff", [4, 32, DFF])
    w_in_e = din("w_in_e", [2, D, 1792])
    w_out_e = din("w_out_e", [2, D, D])
    w_in_o = din("w_in_o", [2, D, 5152])
    w_dt2 = din("w_dt2", [2, D, 64])
    w_out_o = din("w_out_o", [2, 2048, D])
    w_gate = din("w_gate", [4, D, DFF])
    w_up = din("w_up", [4, D, DFF])
    w_down = din("w_down", [4, DFF, D])

    y_p = dout("y_p", [TP, D])
    y_s = dout("y_s", [TS, D])
    ca_p = dout("ca_p", [2, 30, 512])
    ca_s = dout("ca_s", [2, NSEQ, 30, 512])
    wk_p = dout("wk_p", [2, 128, 128])
    wk_s = dout("wk_s", [2, NSEQ, 128, 128])
    wv_p = dout("wv_p", [2, 128, 128])
    wv_s = dout("wv_s", [2, NSEQ, 128, 128])
    cc_p = dout("cc_p", [2, 3, 3072])
    cc_s = dout("cc_s", [2, NSEQ, 3, 3072])
    ss_p = dout("ss_p", [2, 32, 64, 128])
    ss_s = dout("ss_s", [2, NSEQ, 32, 64, 128])
    ff_p = dout("ff_p", [4, 2, DFF])
    ff_s = dout("ff_s", [4, NSEQ, 2, DFF])

    st = contextlib.ExitStack()
    S = Sched(nc)

    def sbt(name, shape, dt=F32):
        return st.enter_context(nc.sbuf_tensor(name, shape, dt))

    Xt = sbt("X", [128, 8, T])
    XNt = sbt("XN", [128, 8, T], BF16)
    CFt = sbt("CFt", [128, NCF])
    CBt = sbt("CBt", [128, NCB], BF16)
    VEC = sbt("VEC", [128, NV])
    MISC = sbt("MISC", [128, 16])
    ARt = sbt("ARENA", [128, ARENA_WORDS])
    AR = Arena(ARt, ARENA_WORDS)
    PB = [st.enter_context(nc.psum_tensor("pb%d" % i, [128, 512], F32)) for i in range(8)]
    PR = [Res("pb%d" % i, excl=True) for i in range(8)]
    pstate = [0]
    held = set()

    def bank(hold=False):
        while True:
            i = pstate[0] % 8
            pstate[0] += 1
            if i not in held:
                break
        if hold:
            held.add(i)
        return PB[i], PR[i]

    def release(pb):
        for i in range(8):
            if PB[i] is pb:
                held.discard(i)

    SEM = {}

    def sh(name):
        if name not in SEM:
            SEM[name] = Res("sem_" + name)
        return SEM[name]

    def scol(ap2d, b):
        return ap2d.rearrange("p (t s) -> p t s", s=NSEQ)[:, :, b]

    rX = [[Res("X%d_%d" % (c, j)) for j in range(5)] for c in range(8)]
    rXN = [Res("XN%d" % c) for c in range(8)]
    rCF, rCB, rVEC, rMISC = Res("cf"), Res("cb"), Res("vec"), Res("misc")
    TT = [(0, 512), (512, 512), (1024, 512), (1536, 512), (2048, 128)]

    def xres(c, c0, w):
        return [rX[c][j] for j in range(5) if not (TT[j][0] >= c0 + w or TT[j][0] + TT[j][1] <= c0)]

    def cf(name, n=128, rows=128):
        return CFt[0:rows, CF[name]:CF[name] + n]

    def cb(name, n=128, rows=128):
        return CBt[0:rows, CB[name]:CB[name] + n]

    def vcol(name, j=0, n=1, p0=0, p1=128):
        return VEC[p0:p1, VO[name] + j:VO[name] + j + n]

    def mm(out, lhsT, rhs, start, stop, R, W):
        S.op("pe", lambda e: e.matmul(out, lhsT=lhsT, rhs=rhs, start=start, stop=stop), reads=R, writes=W)

    def tr(out, in_, k, R, W):
        S.op("pe", lambda e: e.transpose(out, in_, CFt[0:k, 0:k]), reads=list(R) + [rCF], writes=W)

    def act(out, in_, func, R, W, bias=None, scale=None):
        kw = {}
        if bias is not None:
            kw["bias"] = bias
        if scale is not None:
            kw["scale"] = scale
        S.op("act", lambda e: e.activation(out=out, in_=in_, func=func, **kw), reads=R, writes=W)

    def cp(eng, out, in_, R, W):
        if eng == "act":
            S.op("act", lambda e: e.copy(out=out, in_=in_), reads=R, writes=W)
        else:
            S.op(eng, lambda e: e.tensor_copy(out=out, in_=in_), reads=R, writes=W)

    def tt_(eng, out, in0, in1, op, R, W):
        S.op(eng, lambda e: e.tensor_tensor(out=out, in0=in0, in1=in1, op=op), reads=R, writes=W)

    def ts_(eng, out, in0, s1, op0, R, W, s2=None, op1=None):
        if op1 is None:
            S.op(eng, lambda e: e.tensor_scalar(out=out, in0=in0, scalar1=s1, scalar2=None, op0=op0), reads=R, writes=W)
        else:
            S.op(eng, lambda e: e.tensor_scalar(out=out, in0=in0, scalar1=s1, scalar2=s2, op0=op0, op1=op1),
                 reads=R, writes=W)

    def stt(out, in0, scalar, in1, op0, op1, R, W):
        S.op("dve", lambda e: e.scalar_tensor_tensor(out=out, in0=in0, scalar=scalar, in1=in1, op0=op0, op1=op1),
             reads=R, writes=W)

    def dma(eng, out, in_, R, W, sem):
        S.op(eng, lambda e: e.dma_start(out=out, in_=in_), reads=R, writes=W, dma=sh(sem))

    def memset(eng, ap, val, W):
        S.op(eng, lambda e: e.memset(ap, val), writes=W)

    def rstd_from(out, psum_ap, R, W):
        act(out, psum_ap, AF.Sqrt, R + [rMISC], W, bias=MISC[:, 0:1], scale=1.0)
        S.op("dve", lambda e: e.reciprocal(out=out, in_=out), reads=W, writes=W)

    def wload(dst3, src2, res, sem):
        dma("pool", dst3, src2.rearrange("(k p) n -> p k n", p=128), [], [res], sem)

    dma("sp", CFt[:], cf_d, [], [rCF], "cf")
    dma("pool", CBt[:], cb_d, [], [rCB], "cb")
    dma("sp", VEC[:], vec_d, [], [rVEC], "vec")
    memset("pool", MISC[:], 0.0, [rMISC])
    memset("pool", MISC[:, 0:1], EPS, [rMISC])
    for i in range(2):
        act(MISC[32:64, 10 + i:11 + i], vcol("alog%d" % i, p0=32, p1=64), AF.Exp, [rVEC, rMISC], [rMISC])
        ts_("pool", MISC[32:64, 10 + i:11 + i], MISC[32:64, 10 + i:11 + i], -1.0, ALU.mult, [rMISC], [rMISC])
        act(MISC[:, 2 + 4 * i:6 + 4 * i], vcol("snk%d" % i, n=4), AF.Exp, [rVEC, rMISC], [rMISC])

    AR.reset()
    xin = [AR.f32(1024, "xin%d" % k) for k in range(2)]
    for rt in range(17):
        buf, rb = xin[rt % 2]
        src = xp_d[rt * 128:(rt + 1) * 128, :] if rt < 16 else xs_d
        dma("sp", buf, src, [], [rb], "xin%d" % (rt % 2))
        for half in range(2):
            pb, pr = bank()
            for q in range(4):
                c = half * 4 + q
                tr(pb[:, q * 128:(q + 1) * 128], buf[:, c * 128:(c + 1) * 128], 128, [rb], [pr])
            if rt < 16:
                dst = Xt[:, half * 4:half * 4 + 4, rt * 128:(rt + 1) * 128]
                src_ps = pb[:].rearrange("p (c t) -> p c t", c=4)
            else:
                dst = Xt[:, half * 4:half * 4 + 4, TP:T].rearrange("p c (t s) -> p c t s", s=NSEQ)
                src_ps = pb[:].rearrange("p (c s t) -> p c t s", c=4, s=NSEQ)
            W = [rX[half * 4 + q][min(rt // 4, 4)] for q in range(4)]
            cp("act" if half == 0 else "dve", dst, src_ps, [pr], W)

    def norm_phase(gname):
        S.barrier()
        AR.reset()
        sq = [AR.bf(T, "sq%d" % k) for k in range(2)]
        rs, rrs = AR.f32(T, "rs")
        banks = [bank(hold=True) for _ in range(5)]
        for c in range(8):
            b, rb = sq[c % 2]
            if c % 2 == 0:
                act(b, Xt[:, c, :], AF.Square, rX[c], [rb])
            else:
                tt_("pool", b, Xt[:, c, :], Xt[:, c, :], ALU.mult, rX[c], [rb])
            for j, (c0, w) in enumerate(TT):
                pb, pr = banks[j]
                mm(pb[:, 0:w], cb("o1024"), b[:, c0:c0 + w], c == 0, c == 7, [rCB, rb], [pr])
        for j, (c0, w) in enumerate(TT):
            pb, pr = banks[j]
            rstd_from(rs[:, c0:c0 + w], pb[:, 0:w], [pr], [rrs])
            release(pb)
        for c in range(8):
            stt(XNt[:, c, :], Xt[:, c, :], vcol(gname, c), rs, ALU.mult, ALU.mult, rX[c] + [rVEC, rrs], [rXN[c]])

    def resid_add(n, c0, w, pb, pr):
        tt_("dve", Xt[:, n, c0:c0 + w], Xt[:, n, c0:c0 + w], pb[:, 0:w], ALU.add, xres(n, c0, w) + [pr], xres(n, c0, w))

    def ffn_phase(l):
        norm_phase("nf%d" % l)
        S.barrier()
        AR.reset()
        GW = 2210
        WG = [AR.bf(8 * 256, "wg%d" % k) for k in range(2)]
        WU = [AR.bf(8 * 256, "wu%d" % k) for k in range(2)]
        WD = [AR.bf(2 * 1024, "wd%d" % k) for k in range(2)]
        HS = [AR.f32(256, "hs%d" % k) for k in range(2)]
        G = [AR.f32(GW, "g%d" % k) for k in range(2)]
        U = [AR.f32(T, "u%d" % k) for k in range(2)]
        ACC, rACC = AR.f32(T, "acc")
        AT = [AR.bf(2 * T, "at%d" % k) for k in range(2)]
        SP_, rSP = AR.f32(256, "stgp")
        SS_, rSS = AR.f32(256, "stgs")
        for k in range(2):
            memset("pool", G[k][0][:, 0:2], 0.0, [G[k][1]])

        def loadw(g):
            s = g % 2
            wload(WG[s][0].rearrange("p (k n) -> p k n", k=8), w_gate[l][:, g * 256:(g + 1) * 256], WG[s][1], "wg%d" % s)
            wload(WU[s][0].rearrange("p (k n) -> p k n", k=8), w_up[l][:, g * 256:(g + 1) * 256], WU[s][1], "wu%d" % s)
            dma("sp", HS[s][0][0:32, :], st_ff[l][:, g * 256:(g + 1) * 256], [], [HS[s][1]], "hs%d" % s)

        def loadwd(g):
            s = g % 2
            wload(WD[s][0].rearrange("p (k n) -> p k n", k=2), w_down[l][g * 256:(g + 1) * 256, :], WD[s][1], "wd%d" % s)

        def down(g):
            s = g % 2
            wd3 = WD[s][0].rearrange("p (k n) -> p k n", k=2)
            at3 = AT[s][0].rearrange("p (j t) -> p j t", j=2)
            for n in range(8):
                for (c0, w) in TT:
                    pb, pr = bank()
                    for jj in range(2):
                        mm(pb[:, 0:w], wd3[:, jj, n * 128:(n + 1) * 128], at3[:, jj, c0:c0 + w], jj == 0, jj == 1,
                           [WD[s][1], AT[s][1]], [pr])
                    resid_add(n, c0, w, pb, pr)

        loadw(0)
        loadwd(0)
        for g in range(NJ // 2):
            s = g % 2
            if g + 1 < NJ // 2:
                loadw(g + 1)
            wg3 = WG[s][0].rearrange("p (k n) -> p k n", k=8)
            wu3 = WU[s][0].rearrange("p (k n) -> p k n", k=8)
            wd3 = WD[s][0].rearrange("p (k n) -> p k n", k=2)
            at3 = AT[s][0].rearrange("p (j t) -> p j t", j=2)
            for jj in range(2):
                j = 2 * g + jj
                gb, rg = G[j % 2]
                ub, ru = U[j % 2]
                pb, pr = bank()
                tr(pb[:, 0:32], HS[s][0][0:32, jj * 128:(jj + 1) * 128], 32, [HS[s][1]], [pr])
                cp("dve", gb[:, 2050:2082].rearrange("p (k s) -> p k s", s=NSEQ),
                   pb[:, 0:32].rearrange("p (s k) -> p k s", k=2), [pr], [rg])
                for ti, (c0, w) in enumerate(TT):
                    pb, pr = bank()
                    for k in range(8):
                        mm(pb[:, 0:w], wg3[:, k, jj * 128:(jj + 1) * 128], XNt[:, k, c0:c0 + w], k == 0, k == 7,
                           [WG[s][1], rXN[k]], [pr])
                    dst = gb[:, 2 + c0:2 + c0 + w] if ti < 4 else gb[:, 2082:2210]
                    cp("act", dst, pb[:, 0:w], [pr], [rg])
                for ti, (c0, w) in enumerate(TT):
                    pb, pr = bank()
                    for k in range(8):
                        mm(pb[:, 0:w], wu3[:, k, jj * 128:(jj + 1) * 128], XNt[:, k, c0:c0 + w], k == 0, k == 7,
                           [WU[s][1], rXN[k]], [pr])
                    cp("dve", ub[:, c0:c0 + w], pb[:, 0:w], [pr], [ru])
                w0, w1, w2 = (vcol("fcw%d" % l, j * 3 + k) for k in range(3))
                bcol = vcol("fcb%d" % l, j)
                act(ACC[:, 0:TP], gb[:, 0:TP], AF.Identity, [rg, rVEC], [rACC], bias=bcol, scale=w0)
                act(ACC[:, TP:T], gb[:, 2050:2178], AF.Identity, [rg, rVEC], [rACC], bias=bcol, scale=w0)
                stt(ACC[:, 0:TP], gb[:, 1:1 + TP], w1, ACC[:, 0:TP], ALU.mult, ALU.add, [rg, rVEC, rACC], [rACC])
                stt(ACC[:, TP:T], gb[:, 2066:2194], w1, ACC[:, TP:T], ALU.mult, ALU.add, [rg, rVEC, rACC], [rACC])
                stt(ACC[:, 0:TP], gb[:, 2:2 + TP], w2, ACC[:, 0:TP], ALU.mult, ALU.add, [rg, rVEC, rACC], [rACC])
                stt(ACC[:, TP:T], gb[:, 2082:2210], w2, ACC[:, TP:T], ALU.mult, ALU.add, [rg, rVEC, rACC], [rACC])
                act(ACC, ACC, AF.Silu, [rACC], [rACC])
                tt_("pool", at3[:, jj, :], ACC, ub, ALU.mult, [rACC, ru], [AT[s][1]])
                pb, pr = bank()
                tr(pb[0:2, 0:128], gb[:, 2048:2050], 128, [rg], [pr])
                cp("act", SP_[0:2, jj * 128:(jj + 1) * 128], pb[0:2, 0:128], [pr], [rSP])
                pb, pr = bank()
                tr(pb[0:32, 0:128], gb[:, 2178:2210], 128, [rg], [pr])
                cp("act", SS_[0:32, jj * 128:(jj + 1) * 128], pb[0:32, 0:128], [pr], [rSS])
            dma("sp", ff_p[l][:, g * 256:(g + 1) * 256], SP_[0:2, :], [rSP], [], "sp")
            dma("sp", ff_s[l][:, :, g * 256:(g + 1) * 256].rearrange("s t d -> t s d"), SS_[0:32, :], [rSS], [], "ss")
            if g > 0:
                down(g - 1)
            if g + 1 < NJ // 2:
                loadwd(g + 1)
        down(NJ // 2 - 1)

    def even_phase(i, stop=None):
        norm_phase("nme%d" % i)
        S.barrier()
        AR.reset()
        UW = 30 + TP + 480 + TS
        UWW = (UW + 1) // 2
        WA = [AR.bf(8 * 256, "wa%d" % k) for k in range(2)]
        WO, rWO = AR.bf(4 * 1024, "woA")
        RG, rUP = AR.f32(UWW + 2048 + 1024, "up_hst_diag_co")
        UP = RG[:, 0:UWW].bitcast(BF16)[:, 0:UW]
        HST = [RG[:, UWW + 512 * k:UWW + 512 * (k + 1)] for k in range(4)]
        DIAG = RG[:, UWW + 2048:UWW + 3072].bitcast(BF16).rearrange("p (k n) -> p k n", k=16)
        CO = [RG[:, 1088 * c:1088 * (c + 1)].bitcast(BF16) for c in range(4)]
        UF, rUF = AR.f32(160, "uf")
        SIG = [AR.f32(512, "sig%d" % k) for k in range(1)] * 2
        CV = [AR.f32(T, "cv%d" % c) for c in range(4)]
        MU, rMU = AR.f32(T, "mu")
        CVB = [AR.bf(T, "cvb%d" % k) for k in range(2)]
        STG, rSTG = AR.f32(512, "stgA")
        STS, rSTS = AR.f32(512, "stsA")
        memset("pool", UP[:, 0:30], 0.0, [rUP])
        for k in range(4):
            rows = 128 if k < 3 else 96
            dma("sp", HST[k][0:rows, :], st_ca[i][k * 128:k * 128 + rows, :], [], [rUP], "up")
        dma("pool", WO.rearrange("p (k n) -> p k n", k=4), w_out_e[i][0:512, :].rearrange("(k p) n -> p k n", p=128),
            [], [rWO], "woA")
        dma("sp", ca_s[i][:, 0:22, :], st_ca[i].rearrange("(s k) c -> s k c", k=30)[:, 8:30, :], [], [], "dd")

        def loadA(c):
            s = c % 2
            w3 = WA[s][0].rearrange("p (k n) -> p k n", k=8)
            dma("pool", w3[:, :, 0:128], w_in_e[i][:, c * 128:(c + 1) * 128].rearrange("(k p) n -> p k n", p=128),
                [], [WA[s][1]], "wa%d" % s)
            dma("pool", w3[:, :, 128:256],
                w_in_e[i][:, 512 + c * 128:512 + (c + 1) * 128].rearrange("(k p) n -> p k n", p=128),
                [], [WA[s][1]], "wa%d" % s)

        loadA(0)
        SB = 30 + TP
        for c in range(4):
            s = c % 2
            if c < 3:
                loadA(c + 1)
            w3 = WA[s][0].rearrange("p (k n) -> p k n", k=8)
            hdst = UP[:, SB:SB + 480].rearrange("p (k s) -> p s k", s=NSEQ)
            for k in range(4):
                rows = 128 if k < 3 else 96
                pb, pr = bank()
                tr(pb[:, 0:rows], HST[k][0:rows, c * 128:(c + 1) * 128], rows, [rUP], [pr])
                r0 = k * 128
                r = r0
                while r < r0 + rows:
                    sq_, kk = divmod(r, 30)
                    n = min(30 - kk, r0 + rows - r)
                    cp("dve", hdst[:, sq_, kk:kk + n], pb[:, r - r0:r - r0 + n], [pr], [rUP])
                    r += n
            for ti, (c0, w) in enumerate(TT):
                pv, prv = bank()
                pg, prg = bank()
                for k in range(8):
                    mm(pv[:, 0:w], w3[:, k, 0:128], XNt[:, k, c0:c0 + w], k == 0, k == 7, [WA[s][1], rXN[k]], [prv])
                for k in range(8):
                    mm(pg[:, 0:w], w3[:, k, 128:256], XNt[:, k, c0:c0 + w], k == 0, k == 7, [WA[s][1], rXN[k]], [prg])
                sg, rsg = SIG[ti % 2]
                act(sg[:, 0:w], pg[:, 0:w], AF.Sigmoid, [prg], [rsg])
                dst = UP[:, 30 + c0:30 + c0 + w] if ti < 4 else UP[:, SB + 480:UW]
                tt_("dve", dst, pv[:, 0:w], sg[:, 0:w], ALU.mult, [prv, rsg], [rUP])
                if ti == 3:
                    tt_("dve", UF[:, 0:30], pv[:, 482:512], sg[:, 482:512], ALU.mult, [prv, rsg], [rUF])
                if ti == 4:
                    tt_("dve", UF[:, 32:160], pv[:, 0:128], sg[:, 0:128], ALU.mult, [prv, rsg], [rUF])
            pb, pr = bank()
            tr(pb[0:30, 0:128], UF[:, 0:30], 128, [rUF], [pr])
            cp("act", STG[0:30, c * 128:(c + 1) * 128], pb[0:30, 0:128], [pr], [rSTG])
            pb, pr = bank()
            tr(pb[:, 0:128], UF[:, 32:160], 128, [rUF], [pr])
            cp("act", STS[:, c * 128:(c + 1) * 128], pb[:, 0:128], [pr], [rSTS])
            cv, rcv = CV[c]
            bcol = vcol("cab%d" % i, c)
            banks = [bank(hold=True) for _ in range(5)]
            for half in range(2):
                taps = list(range(16 * half, min(16 * half + 16, 31)))
                for j, k in enumerate(taps):
                    ts_("dve", DIAG[:, j, :], cb("ident"), vcol("caw%d" % i, c * 31 + k), ALU.mult, [rCB, rVEC], [rUP])
                for ti, (c0, w) in enumerate(TT):
                    pb, pr = banks[ti]
                    for j, k in enumerate(taps):
                        src = UP[:, k + c0:k + c0 + w] if ti < 4 else UP[:, SB + 16 * k:SB + 16 * k + 128]
                        mm(pb[:, 0:w], DIAG[:, j, :], src, k == 0, k == 30, [rUP], [pr])
            for ti, (c0, w) in enumerate(TT):
                pb, pr = banks[ti]
                act(cv[:, c0:c0 + w], pb[:, 0:w], AF.Identity, [pr, rVEC], [rcv], bias=bcol, scale=1.0)
                release(pb)
        if stop == "A0":
            return
        dma("sp", ca_p[i], STG[0:30, :], [rSTG], [], "stg")
        dma("sp", ca_s[i][:, 22:30, :].rearrange("s t d -> t s d"), STS[:, :], [rSTS], [], "sts")
        if stop == "A1":
            return
        banks = [bank(hold=True) for _ in range(5)]
        for c in range(4):
            b, rb = CVB[c % 2]
            cp("pool", b, CV[c][0], [CV[c][1]], [rb])
            for j, (c0, w) in enumerate(TT):
                mm(banks[j][0][:, 0:w], cb("o512"), b[:, c0:c0 + w], c == 0, c == 3, [rCB, rb], [banks[j][1]])
        for j, (c0, w) in enumerate(TT):
            cp("act", MU[:, c0:c0 + w], banks[j][0][:, 0:w], [banks[j][1]], [rMU])
            release(banks[j][0])
        for c in range(4):
            tt_("pool", CV[c][0], CV[c][0], MU, ALU.subtract, [CV[c][1], rMU], [CV[c][1]])
        banks = [bank(hold=True) for _ in range(5)]
        for c in range(4):
            b, rb = CVB[c % 2]
            act(b, CV[c][0], AF.Square, [CV[c][1]], [rb])
            for j, (c0, w) in enumerate(TT):
                mm(banks[j][0][:, 0:w], cb("o512"), b[:, c0:c0 + w], c == 0, c == 3, [rCB, rb], [banks[j][1]])
        for j, (c0, w) in enumerate(TT):
            rstd_from(MU[:, c0:c0 + w], banks[j][0][:, 0:w], [banks[j][1]], [rMU])
            release(banks[j][0])
        for c in range(4):
            tt_("dve", CV[c][0], CV[c][0], MU, ALU.mult, [CV[c][1], rMU], [CV[c][1]])
            act(CO[c], CV[c][0], AF.Silu, [CV[c][1], rVEC], [rUP], bias=vcol("lnb%d" % i, c),
                scale=vcol("lng%d" % i, c))
        wo3 = WO.rearrange("p (k n) -> p k n", k=4)
        for n in range(8):
            for (c0, w) in TT:
                pb, pr = bank()
                for k in range(4):
                    mm(pb[:, 0:w], wo3[:, k, n * 128:(n + 1) * 128], CO[k][:, c0:c0 + w], k == 0, k == 3,
                       [rWO, rUP], [pr])
                resid_add(n, c0, w, pb, pr)

        if stop == "A":
            return
        S.barrier()
        AR.reset()
        WQ = [AR.bf(8 * 128, "wq%d" % k) for k in range(2)]
        WO2, rWO2 = AR.bf(4 * 1024, "woB")
        COS, rCOS = AR.f32(T, "cos")
        SIN, rSIN = AR.f32(T, "sin")
        OT = [COS[:, 0:1088].bitcast(BF16), COS[:, 1088:2176].bitcast(BF16),
              SIN[:, 0:1088].bitcast(BF16), SIN[:, 1088:2176].bitcast(BF16)]
        rOT = [rCOS, rCOS, rSIN, rSIN]
        QR = [AR.bf(T, "qr%d" % c) for c in range(4)]
        KR, rKR = AR.bf(T, "kr")
        KF, rKF = AR.f32(256, "kf")
        VB, rVB = AR.bf(17 * 128, "vb")
        VF, rVF = AR.f32(256, "vf")
        TMP = [AR.f32(512, "tmpb%d" % k) for k in range(6)]
        TI, rTI = AR.f32(512, "ti")
        SQB, rSQB = AR.bf(512, "sqb")
        QNB, rQNB = AR.f32(512, "qnb")
        EB = [AR.bf(256, "eb%d" % k) for k in range(2)]
        REC = [AR.f32(128, "rec%d" % k) for k in range(2)]
        KHS = [AR.f32(128, "khs%d" % k) for k in range(2)]
        KHT, rKHT = AR.bf(NSEQ * 128, "kht")
        VH, rVH = AR.bf(NSEQ * 128, "vh")
        OA, rOA = AR.f32(128, "oa")
        KO, rKO = AR.f32(256, "ko")
        dma("pool", WO2.rearrange("p (k n) -> p k n", k=4), w_out_e[i][512:1024, :].rearrange("(k p) n -> p k n", p=128),
            [], [rWO2], "woB")
        vh3 = VH.rearrange("p (s d) -> p s d", s=NSEQ)
        dma("pool", vh3, st_v[i].rearrange("s k d -> k s d"), [], [rVH], "vh")
        dma("sp", wk_s[i][:, 0:120, :], st_k[i][:, 8:128, :], [], [], "ddk")
        dma("sp", wv_s[i][:, 0:120, :], st_v[i][:, 8:128, :], [], [], "ddk")
        ti_i = TI.bitcast(I32)
        for (c0, w) in TT:
            pz, rpz = TMP[1]
            dma("sp", pz[:, 0:w], pos_d[:, c0:c0 + w].partition_broadcast(128), [], [rpz], "pos")
            for (dst, rdst, ph) in ((SIN, rSIN, 0.0), (COS, rCOS, math.pi / 2)):
                a, ra = TMP[0]
                c_, rc_ = TMP[2]
                ts_("dve", a[:, 0:w], pz[:, 0:w], vcol("invf"), ALU.mult, [rpz, rVEC], [ra], s2=ph, op1=ALU.add)
                ts_("dve", c_[:, 0:w], a[:, 0:w], 1.0 / (2 * math.pi), ALU.mult, [ra], [rc_])
                cp("dve", ti_i[:, 0:w], c_[:, 0:w], [rc_], [rTI])
                cp("dve", c_[:, 0:w], ti_i[:, 0:w], [rTI], [rc_])
                stt(a[:, 0:w], c_[:, 0:w], -2 * math.pi, a[:, 0:w], ALU.mult, ALU.add, [rc_, ra], [ra])
                ts_("dve", c_[:, 0:w], a[:, 0:w], math.pi, ALU.is_gt, [ra], [rc_], s2=-2 * math.pi, op1=ALU.mult)
                tt_("dve", a[:, 0:w], a[:, 0:w], c_[:, 0:w], ALU.add, [ra, rc_], [ra])
                act(dst[:, c0:c0 + w], a[:, 0:w], AF.Sin, [ra], [rdst])
        if stop == "B0":
            return
        kht3 = KHT.rearrange("p (s k) -> p s k", s=NSEQ)
        for sq_ in range(NSEQ):
            hb_, rhb = KHS[sq_ % 2]
            dma("sp", hb_, st_k[i][sq_], [], [rhb], "khs%d" % (sq_ % 2))
            pb, pr = bank()
            tr(pb[:, 0:128], hb_, 128, [rhb], [pr])
            cp("act", kht3[:, sq_, :], pb[:, 0:128], [pr], [rKHT])

        def loadQ(idx):
            s = idx % 2
            col = 1024 + idx * 128
            wload(WQ[s][0].rearrange("p (k n) -> p k n", k=8), w_in_e[i][:, col:col + 128], WQ[s][1], "wq%d" % s)

        loadQ(0)
        for idx in range(5):
            s = idx % 2
            loadQ(idx + 1)
            w3 = WQ[s][0].rearrange("p (k n) -> p k n", k=8)
            gcol = vcol("qg%d" % i) if idx < 4 else vcol("kg%d" % i)
            for ti, (c0, w) in enumerate(TT):
                pb, pr = bank()
                for k in range(8):
                    mm(pb[:, 0:w], w3[:, k, :], XNt[:, k, c0:c0 + w], k == 0, k == 7, [WQ[s][1], rXN[k]], [pr])
                act(SQB[:, 0:w], pb[:, 0:w], AF.Square, [pr], [rSQB])
                p2, pr2 = bank()
                mm(p2[:, 0:w], cb("bd64"), SQB[:, 0:w], True, True, [rCB, rSQB], [pr2])
                r_, rr_ = TMP[3]
                rstd_from(r_[:, 0:w], p2[:, 0:w], [pr2], [rr_])
                stt(QNB[:, 0:w], pb[:, 0:w], gcol, r_[:, 0:w], ALU.mult, ALU.mult, [pr, rVEC, rr_], [rQNB])
                p3, pr3 = bank()
                mm(p3[:, 0:w], cf("pmat"), QNB[:, 0:w], True, True, [rCF, rQNB], [pr3])
                t1, rt1 = TMP[4]
                t2, rt2 = TMP[5]
                tt_("dve", t1[:, 0:w], p3[:, 0:w], SIN[:, c0:c0 + w], ALU.mult, [pr3, rSIN], [rt1])
                tt_("pool", t2[:, 0:w], QNB[:, 0:w], COS[:, c0:c0 + w], ALU.mult, [rQNB, rCOS], [rt2])
                if idx < 4:
                    tt_("pool", QR[idx][0][:, c0:c0 + w], t1[:, 0:w], t2[:, 0:w], ALU.add, [rt1, rt2], [QR[idx][1]])
                else:
                    tt_("pool", KR[:, c0:c0 + w], t1[:, 0:w], t2[:, 0:w], ALU.add, [rt1, rt2], [rKR])
                    if ti == 3:
                        tt_("pool", KF[:, 0:128], t1[:, 384:512], t2[:, 384:512], ALU.add, [rt1, rt2], [rKF])
                    if ti == 4:
                        tt_("pool", KF[:, 128:256], t1[:, 0:128], t2[:, 0:128], ALU.add, [rt1, rt2], [rKF])
        if stop == "B1":
            return
        pb, pr = bank()
        tr(pb[:, 0:128], KF[:, 0:128], 128, [rKF], [pr])
        tr(pb[:, 128:256], KF[:, 128:256], 128, [rKF], [pr])
        cp("act", KO, pb[:, 0:256], [pr], [rKO])
        dma("sp", wk_p[i], KO[:, 0:128], [rKO], [], "ko")
        dma("sp", wk_s[i][:, 120:128, :].rearrange("s t d -> t s d"), KO[:, 128:256], [rKO], [], "ko")
        if stop == "B15":
            return
        s = 5 % 2
        w3 = WQ[s][0].rearrange("p (k n) -> p k n", k=8)
        vb3 = VB.rearrange("p (b d) -> p b d", b=17)
        for blk in range(17):
            pb, pr = bank()
            for k in range(8):
                mm(pb[:, 0:128], XNt[:, k, blk * 128:(blk + 1) * 128], w3[:, k, :], k == 0, k == 7, [WQ[s][1], rXN[k]], [pr])
            cp("act", vb3[:, blk, :], pb[:, 0:128], [pr], [rVB])
            if blk == 15:
                cp("dve", VF[:, 0:128], pb[:, 0:128], [pr], [rVF])
            if blk == 16:
                cp("dve", VF[:, 128:256], pb[:, 0:128], [pr], [rVF])
        dma("sp", wv_p[i], VF[:, 0:128], [rVF], [], "vf")
        dma("sp", wv_s[i][:, 120:128, :].rearrange("s t d -> t s d"), VF[:, 128:256], [rVF], [], "vf")

        if stop == "B2":
            return
        def att_p1(blk, c, h2, u):
            c0 = blk * 128
            ps_, prs = bank()
            lo, hi = 64 * h2, 64 * h2 + 64
            qv = QR[c][0][lo:hi, c0:c0 + 128]
            mm(ps_[:, 128:256], KR[lo:hi, c0:c0 + 128], qv, True, False, [rKR, QR[c][1]], [prs])
            mm(ps_[:, 128:256], cb("ident"), cb("mcur") if blk < 16 else cb("msnew"), False, True, [rCB], [prs])
            if 0 < blk < 16:
                mm(ps_[:, 0:128], KR[lo:hi, c0 - 128:c0], qv, True, False, [rKR, QR[c][1]], [prs])
                mm(ps_[:, 0:128], cb("ident"), cb("mprev"), False, True, [rCB], [prs])
            if blk == 16:
                for sq_ in range(NSEQ):
                    mm(scol(ps_[:, 0:128], sq_), kht3[lo:hi, sq_, :], scol(QR[c][0][lo:hi, c0:c0 + 128], sq_),
                       True, True, [rKHT, QR[c][1]], [prs])
            e, re = EB[u % 2]
            if blk < 16:
                first = 128 if blk == 0 else 0
                act(e[:, first:256], ps_[:, first:256], AF.Exp, [prs], [re], scale=0.125)
            else:
                act(e[:, 128:256], ps_[:, 128:256], AF.Exp, [prs], [re], scale=0.125)
                t1, rt1 = TMP[4]
                tt_("dve", t1[:, 0:128], ps_[:, 0:128], cb("mshist"), ALU.add, [prs, rCB], [rt1])
                act(e[:, 0:128], t1[:, 0:128], AF.Exp, [rt1], [re], scale=0.125)

        def att_p2(blk, c, h2, u):
            c0 = blk * 128
            lo, hi = 64 * h2, 64 * h2 + 64
            e, re = EB[u % 2]
            po, pro = bank()
            rc, rrc = REC[u % 2]
            esk = MISC[lo:hi, 2 + 4 * i + c:3 + 4 * i + c]
            if blk < 16:
                if blk > 0:
                    mm(po[:, 0:128], vb3[:, blk - 1, :], e[:, 0:128], True, False, [rVB, re], [pro])
                mm(po[:, 0:128], vb3[:, blk, :], e[:, 128:256], blk == 0, True, [rVB, re], [pro])
                if blk > 0:
                    mm(po[:, 128:256], cb("ones"), e[:, 0:128], True, False, [rCB, re], [pro])
                mm(po[:, 128:256], cb("ones"), e[:, 128:256], blk == 0, True, [rCB, re], [pro])
                ts_("dve", rc[lo:hi, :], po[lo:hi, 128:256], esk, ALU.add, [pro, rMISC], [rrc])
                S.op("dve", lambda e_, rc=rc, lo=lo, hi=hi: e_.reciprocal(out=rc[lo:hi, :], in_=rc[lo:hi, :]),
                     reads=[rrc], writes=[rrc])
                tt_("dve", OT[c][lo:hi, c0:c0 + 128], po[lo:hi, 0:128], rc[lo:hi, :], ALU.mult,
                    [pro, rrc], [rOT[c]])
            else:
                mm(po[:, 0:128], vb3[:, 16, :], e[:, 128:256], True, True, [rVB, re], [pro])
                mm(po[:, 128:256], cb("ones"), e[:, 128:256], True, True, [rCB, re], [pro])
                for sq_ in range(NSEQ):
                    mm(scol(po[:, 256:384], sq_), vh3[:, sq_, :], scol(e[:, 0:128], sq_), True, True, [rVH, re], [pro])
                mm(po[:, 384:512], cb("ones"), e[:, 0:128], True, True, [rCB, re], [pro])
                cp("act", OA[lo:hi, :], po[lo:hi, 256:384], [pro], [rOA])
                t2, rt2 = TMP[5]
                cp("act", t2[lo:hi, 0:128], po[lo:hi, 384:512], [pro], [rt2])
                stt(rc[lo:hi, :], po[lo:hi, 128:256], esk, t2[lo:hi, 0:128], ALU.add, ALU.add,
                    [pro, rMISC, rt2], [rrc])
                S.op("dve", lambda e_, rc=rc, lo=lo, hi=hi: e_.reciprocal(out=rc[lo:hi, :], in_=rc[lo:hi, :]),
                     reads=[rrc], writes=[rrc])
                tt_("dve", OA[lo:hi, :], OA[lo:hi, :], po[lo:hi, 0:128], ALU.add, [rOA, pro], [rOA])
                tt_("dve", OT[c][lo:hi, c0:c0 + 128], OA[lo:hi, :], rc[lo:hi, :], ALU.mult,
                    [rOA, rrc], [rOT[c]])

        units = [(blk, c, h2) for blk in range(17 if stop != "B3" else 16) for c in range(4) for h2 in range(2)]
        att_p1(*units[0], 0)
        for u, un in enumerate(units):
            if u + 1 < len(units):
                att_p1(*units[u + 1], u + 1)
            att_p2(*un, u)
        wo3 = WO2.rearrange("p (k n) -> p k n", k=4)
        for n in range(8):
            for (c0, w) in TT:
                pb, pr = bank()
                for k in range(4):
                    mm(pb[:, 0:w], wo3[:, k, n * 128:(n + 1) * 128], OT[k][:, c0:c0 + w], k == 0, k == 3,
                       [rWO2, rOT[k]], [pr])
                resid_add(n, c0, w, pb, pr)

    def odd_phase(i):
        norm_phase("nmo%d" % i)
        S.barrier()
        AR.reset()
        TW = 256
        TILES = [(k * TW, TW) for k in range(8)] + [(TP, TS)]
        WIN, rWIN = AR.bf(10 * 1024, "win")
        rWINt = [Res("win%d" % t) for t in range(10)]
        WOo, rWOo = AR.bf(4 * 1024, "woo")
        WDT, rWDT = AR.bf(8 * 64, "wdt")
        DTT, rDTT = AR.f32(17 * 64, "dtt")
        D2, rD2 = AR.f32(TW, "d2")
        XPAD = [AR.f32(3 + TW, "xpad%d" % k) for k in range(6)]
        ACC = [AR.f32(TW, "acco%d" % k) for k in range(1)] * 2
        XS = [[AR.bf(TW, "xs%d_%d" % (p, k)) for k in range(4)] for p in range(2)]
        BT = [AR.bf(TW, "bt%d" % p) for p in range(2)]
        CT = [AR.bf(TW, "ct%d" % p) for p in range(2)]
        ZS = [[AR.bf(TW, "zs%d_%d" % (p, k)) for k in range(4)] for p in range(2)]
        YN = [AR.bf(TW, "yn%d" % k) for k in range(4)]
        TLP, rTLP = AR.f32(6 * 3, "tlp")
        TLS, rTLS = AR.f32(6 * 48, "tls")
        STT_, rSTT = AR.f32(768, "sttail")
        XDT = [AR.bf(512, "xdt%d" % p) for p in range(2)]
        BTOK = [AR.bf(128, "btok%d" % p) for p in range(2)]
        DEND = [AR.f32(8, "dend%d" % p) for p in range(2)]
        CBM = [AR.f32(128, "cbm%d" % p) for p in range(2)]
        WT_ = [[AR.bf(512, "wt%d_%d" % (p, h)) for h in range(2)] for p in range(2)]
        CS_ = [[AR.bf(512, "cs%d_%d" % (p, h)) for h in range(2)] for p in range(2)]
        CDEC = [[AR.f32(4, "cdec%d_%d" % (p, h)) for h in range(2)] for p in range(2)]
        RR = [AR.f32(512, "rr%d" % h) for h in range(2)]
        DEC = [AR.bf(512, "dec%d" % h) for h in range(2)]
        EXPA = [AR.f32(512, "expa%d" % h) for h in range(2)]
        XDD, rXDD = AR.bf(256, "xdd")
        YV2 = [[AR.f32(128, "yv%d_%d" % (p, k)) for k in range(4)] for p in range(2)]
        SQY, rSQY = AR.bf(512, "sqy")
        RSTD, rRSTD = AR.f32(128, "rstd")
        u0 = AR.off
        HT, rHT = AR.f32(512, "ht")
        HTB, rHTB = AR.bf(512, "htb")
        TMPH, rTMPH = AR.f32(256, "tmph")
        SOUT, rSOUT = AR.f32(512, "sout")
        u1 = AR.off
        AR.off = u0
        H0 = [AR.f32(256, "h0%d" % k) for k in range(2)]
        H0T = [AR.bf(256, "h0t%d" % k) for k in range(2)]
        HFIN = [AR.f32(256, "hfin%d" % k) for k in range(3)]
        BM = [AR.bf(128, "bm%d" % k) for k in range(2)]
        CD, rCD = AR.f32(32, "cd")
        DTAX, rDTAX = AR.f32(256, "dtax")
        YOFF, rYOFF = AR.f32(512, "yoff")
        AR.off = max(AR.off, u1)
        dtt3 = DTT.rearrange("p (c h) -> p c h", c=17)

        wload(WDT.rearrange("p (k n) -> p k n", k=8), w_dt2[i], rWDT, "wdt")
        wdt3 = WDT.rearrange("p (k n) -> p k n", k=8)
        for (c0, w) in TILES:
            pb, pr = bank()
            for k in range(8):
                mm(pb[0:64, 0:w], wdt3[:, k, :], XNt[:, k, c0:c0 + w], k == 0, k == 7, [rWDT, rXN[k]], [pr])
            act(D2[0:64, 0:w], pb[0:64, 0:w], AF.Exp, [pr, rVEC], [rD2], bias=vcol("dtb%d" % i, p1=64))
            act(D2[0:64, 0:w], D2[0:64, 0:w], AF.Ln, [rD2], [rD2], bias=1.0)
            ts_("pool", D2[32:64, 0:w], D2[32:64, 0:w], MISC[32:64, 10 + i:11 + i], ALU.mult, [rD2, rMISC], [rD2])
            for q in range(w // 128):
                ch = (c0 + q * 128) // 128
                pb2, pr2 = bank()
                tr(pb2[:, 0:64], D2[0:64, q * 128:(q + 1) * 128], 64, [rD2], [pr2])
                cp("act", dtt3[:, ch, :], pb2[:, 0:64], [pr2], [rDTT])

        win3 = WIN.rearrange("p (t k n) -> p t k n", t=10, k=8)
        wo3 = WOo.rearrange("p (k n) -> p k n", k=4)

        def produce(g, ti, cidx):
            c0, w = TILES[ti]
            par = ti % 2
            sample = ti == 8
            for k in range(6):
                xp_, rxp = XPAD[k]
                if sample:
                    ccol = cidx[k] * 128
                    dma("sp", STT_[0:48, 0:128], st_cc[i][:, ccol:ccol + 128], [], [rSTT], "stt")
                    pb, pr = bank()
                    tr(pb[:, 0:48], STT_[0:48, 0:128], 48, [rSTT], [pr])
                    cp("dve", xp_[:, 0:48].rearrange("p (k s) -> p k s", s=NSEQ),
                       pb[:, 0:48].rearrange("p (s k) -> p k s", k=3), [pr], [rxp])
                pb, pr = bank()
                for kk in range(8):
                    mm(pb[:, 0:w], win3[:, k, kk, :], XNt[:, kk, c0:c0 + w], kk == 0, kk == 7, [rWINt[k], rXN[kk]], [pr])
                doff = 48 if sample else 3
                cp("act", xp_[:, doff:doff + w], pb[:, 0:w], [pr], [rxp])
                step = 16 if sample else 1
                acc, racc = ACC[k % 2]
                act(acc[:, 0:w], xp_[:, 0:w], AF.Identity, [rxp, rVEC], [racc], bias=vcol("ccb%d" % i, cidx[k]),
                    scale=vcol("ccw%d" % i, cidx[k] * 4))
                for kq in range(1, 4):
                    stt(acc[:, 0:w], xp_[:, kq * step:kq * step + w], vcol("ccw%d" % i, cidx[k] * 4 + kq), acc[:, 0:w],
                        ALU.mult, ALU.add, [rxp, rVEC, racc], [racc])
                dst, rdst = (XS[par][k] if k < 4 else (BT[par] if k == 4 else CT[par]))
                act(dst[:, 0:w], acc[:, 0:w], AF.Silu, [racc], [rdst])
                if ti == 7:
                    cp("pool", TLP[:, k * 3:(k + 1) * 3], xp_[:, TW:TW + 3], [rxp], [rTLP])
                if sample:
                    cp("pool", TLS[:, k * 48:(k + 1) * 48], xp_[:, 48 + 80:48 + 128], [rxp], [rTLS])
                elif ti < 7:
                    cp("pool", xp_[:, 0:3], xp_[:, TW:TW + 3], [rxp], [rxp])
            for q in range(4):
                pb, pr = bank()
                for kk in range(8):
                    mm(pb[:, 0:w], win3[:, 6 + q, kk, :], XNt[:, kk, c0:c0 + w], kk == 0, kk == 7, [rWINt[6 + q], rXN[kk]], [pr])
                act(ZS[par][q][0][:, 0:w], pb[:, 0:w], AF.Silu, [pr], [ZS[par][q][1]])

        def prep(g, ch):
            ti, q = (ch // 2, ch % 2) if ch < 16 else (8, 0)
            par, cp_ = ti % 2, ch % 2
            cs_ = slice(q * 128, q * 128 + 128)
            sample = ch == 16
            tri = cf("tri_s") if sample else cf("tri")
            ut = cf("ut_s") if sample else cf("ut")
            cbm = cb("cbm_s") if sample else cb("cbm_p")
            dt_g = dtt3[:, ch, 8 * g:8 * g + 8]
            dta_g = dtt3[:, ch, 32 + 8 * g:32 + 8 * g + 8]
            bt, rbt = BT[par]
            ct, rct = CT[par]
            xdt, rxdt = XDT[cp_]
            btok, rbtok = BTOK[cp_]
            dend, rdend = DEND[cp_]
            cbmb, rcbm = CBM[cp_]
            pb, pr = bank()
            mm(pb[:, 0:128], bt[:, cs_], ct[:, cs_], True, True, [rbt, rct], [pr])
            tt_("dve", cbmb, pb[:, 0:128], cbm, ALU.mult, [pr, rCB], [rcbm])
            pb, pr = bank()
            for cc in range(4):
                mm(pb[:, cc * 128:(cc + 1) * 128], XS[par][cc][0][:, cs_], cb("ident"), True, True, [XS[par][cc][1], rCB], [pr])
            tt_("dve", xdt.rearrange("p (h d) -> p h d", h=8), pb[:].rearrange("p (h d) -> p h d", h=8),
                dt_g.unsqueeze(2).broadcast_to([128, 8, 64]), ALU.mult, [pr, rDTT], [rxdt])
            pb, pr = bank()
            mm(pb[:, 0:128], bt[:, cs_], cb("ident"), True, True, [rbt, rCB], [pr])
            cp("act", btok, pb[:, 0:128], [pr], [rbtok])
            pb, pr = bank()
            mm(pb[:, 0:8], ut, dta_g, True, True, [rCF, rDTT], [pr])
            act(dend, pb[:, 0:8], AF.Exp, [pr], [rdend])
            for hh in range(2):
                hs = slice(4 * hh, 4 * hh + 4)
                rr, rrr = RR[hh]
                dec, rdec = DEC[hh]
                expa, rexpa = EXPA[hh]
                wt, rwt = WT_[cp_][hh]
                cs2, rcs = CS_[cp_][hh]
                cdec, rcdec = CDEC[cp_][hh]
                tt_("dve", rr.rearrange("p (h l) -> p h l", h=4), tri.unsqueeze(1).broadcast_to([128, 4, 128]),
                    dta_g[:, hs].unsqueeze(2).broadcast_to([128, 4, 128]), ALU.mult, [rCF, rDTT], [rrr])
                pd, prd = bank()
                pa, pra = bank()
                mm(pd[:], ut, rr, True, True, [rCF, rrr], [prd])
                mm(pa[:], cf("ones"), rr, True, True, [rCF, rrr], [pra])
                act(dec, pd[:], AF.Exp, [prd], [rdec])
                act(expa, pa[:], AF.Exp, [pra], [rexpa])
                tt_("dve", wt.rearrange("p (h l) -> p h l", h=4), dec.rearrange("p (h l) -> p h l", h=4),
                    cbmb.unsqueeze(1).broadcast_to([128, 4, 128]), ALU.mult, [rdec, rcbm], [rwt])
                tt_("pool", cs2.rearrange("p (h l) -> p h l", h=4), expa.rearrange("p (h l) -> p h l", h=4),
                    ct[:, cs_].unsqueeze(1).broadcast_to([128, 4, 128]), ALU.mult, [rexpa, rct], [rcs])
                cp("pool", cdec, expa.rearrange("p (h l) -> p h l", h=4)[:, :, 127], [rexpa], [rcdec])

        def stage_b(g, ch):
            ti, q = (ch // 2, ch % 2) if ch < 16 else (8, 0)
            par, cp_ = ti % 2, ch % 2
            cs_ = slice(q * 128, q * 128 + 128)
            sample = ch == 16
            first = ch == 0
            YV = YV2[cp_]
            dta_g = dtt3[:, ch, 32 + 8 * g:32 + 8 * g + 8]
            xdt, rxdt = XDT[cp_]
            btok, rbtok = BTOK[cp_]
            dend, rdend = DEND[cp_]
            for hh in range(2):
                hs = slice(4 * hh, 4 * hh + 4)
                wt, rwt = WT_[cp_][hh]
                cs2, rcs = CS_[cp_][hh]
                cdec, rcdec = CDEC[cp_][hh]
                py, pry = bank(hold=True)
                use_off = (not sample) and (not first)
                for h in range(4):
                    j2 = (4 * hh + h) // 2
                    mm(py[:, h * 128:(h + 1) * 128], xdt[:, j2 * 128:(j2 + 1) * 128], wt[:, h * 128:(h + 1) * 128],
                       True, not use_off, [rxdt, rwt], [pry])
                    if use_off:
                        mm(py[:, h * 128:(h + 1) * 128], HTB[:, j2 * 128:(j2 + 1) * 128], cs2[:, h * 128:(h + 1) * 128],
                           False, True, [rHTB, rcs], [pry])
                tt_("pool", XDD.rearrange("p (h d) -> p h d", h=4), xdt.rearrange("p (h d) -> p h d", h=8)[:, hs, :],
                    dend[:, hs].unsqueeze(2).broadcast_to([128, 4, 64]), ALU.mult, [rxdt, rdend], [rXDD])
                if sample:
                    tt_("pool", DTAX.rearrange("p (h d) -> p h d", h=4),
                        dta_g[:, hs].unsqueeze(2).broadcast_to([128, 4, 64]),
                        cf("ones", 64).unsqueeze(1).broadcast_to([128, 4, 64]), ALU.mult, [rDTT, rCF], [rDTAX])
                    pc, prc = bank()
                    for pr_i in range(2):
                        mm(pc[:, pr_i * 16:(pr_i + 1) * 16], DTAX[:, pr_i * 128:(pr_i + 1) * 128], cf("seqmask", 16),
                           True, True, [rDTAX, rCF], [prc])
                    act(CD, pc[:, 0:32], AF.Exp, [prc], [rCD])
                    po_, pro_ = bank(hold=True)
                    hd0 = 8 * g + 4 * hh
                    def h0_load(b):
                        src = st_ssm[i][b, hd0:hd0 + 4].rearrange("(j h2) p n -> (h2 p) j n", h2=2)
                        dma("sp", H0[b % 2][0].rearrange("p (j n) -> p j n", j=2), src, [], [H0[b % 2][1]], "h0%d" % (b % 2))

                    def hf_store(b):
                        hf, rhf = HFIN[b % 3]
                        dst = ss_s[i][b, hd0:hd0 + 4].rearrange("(j h2) p n -> (h2 p) j n", h2=2)
                        dma("sp", dst, hf.rearrange("p (j n) -> p j n", j=2), [rhf], [], "hf%d" % (b % 3))

                    h0_load(0)
                    for b in range(NSEQ):
                        h0, rh0 = H0[b % 2]
                        h0t, rh0t = H0T[b % 2]
                        hf, rhf = HFIN[b % 3]
                        bm, rbm = BM[b % 2]
                        pt, prt = bank()
                        for pr_i in range(2):
                            tr(pt[:, pr_i * 128:(pr_i + 1) * 128], h0[:, pr_i * 128:(pr_i + 1) * 128], 128, [rh0], [prt])
                        cp("act", h0t, pt[:, 0:256], [prt], [rh0t])
                        for h in range(4):
                            pr_i = h // 2
                            mm(scol(po_[:, h * 128:(h + 1) * 128], b), h0t[:, pr_i * 128:(pr_i + 1) * 128],
                               scol(cs2[:, h * 128:(h + 1) * 128], b), True, True, [rh0t, rcs], [pro_])
                        ts_("pool", bm, btok, cf("seqmask", 16)[:, b:b + 1], ALU.mult, [rbtok, rCF], [rbm])
                        pf, prf = bank()
                        for pr_i in range(2):
                            mm(pf[:, pr_i * 128:(pr_i + 1) * 128], XDD[:, pr_i * 128:(pr_i + 1) * 128], bm, True, True,
                               [rXDD, rbm], [prf])
                        for pr_i in range(2):
                            stt(hf[:, pr_i * 128:(pr_i + 1) * 128], h0[:, pr_i * 128:(pr_i + 1) * 128],
                                CD[:, pr_i * 16 + b:pr_i * 16 + b + 1], pf[:, pr_i * 128:(pr_i + 1) * 128],
                                ALU.mult, ALU.add, [rh0, rCD, prf], [rhf])
                        if b + 1 < NSEQ:
                            h0_load(b + 1)
                        if b > 0:
                            hf_store(b - 1)
                    hf_store(NSEQ - 1)
                    cp("act", YOFF, po_[:], [pro_], [rYOFF])
                    release(po_)
                for j2l in range(2):
                    cc = 2 * hh + j2l
                    yv, ryv = YV[cc]
                    xs_, rxs = XS[par][cc]
                    for h2 in range(2):
                        lo, hi = 64 * h2, 64 * h2 + 64
                        hsel = 2 * j2l + h2
                        stt(yv[lo:hi, :], xs_[lo:hi, cs_], vcol("dsk%d" % i, 4 * g + cc, p0=lo, p1=hi),
                            py[lo:hi, hsel * 128:(hsel + 1) * 128], ALU.mult, ALU.add, [rxs, rVEC, pry], [ryv])
                        if sample:
                            tt_("dve", yv[lo:hi, :], yv[lo:hi, :], YOFF[lo:hi, hsel * 128:(hsel + 1) * 128], ALU.add,
                                [ryv, rYOFF], [ryv])
                release(py)
                if not sample:
                    pst, prst = bank()
                    mm(pst[:, 0:256], btok, XDD, True, True, [rbtok, rXDD], [prst])
                    hcol = slice(256 * hh, 256 * hh + 256)
                    if first:
                        cp("dve", HT[:, hcol], pst[:, 0:256], [prst], [rHT])
                    else:
                        tt_("pool", TMPH.rearrange("p (h d) -> p h d", h=4), HT[:, hcol].rearrange("p (h d) -> p h d", h=4),
                            cdec.unsqueeze(2).broadcast_to([128, 4, 64]), ALU.mult, [rHT, rcdec], [rTMPH])
                        tt_("dve", HT[:, hcol], TMPH, pst[:, 0:256], ALU.add, [rTMPH, prst], [rHT])
                    cp("act", HTB[:, hcol], HT[:, hcol], [rHT], [rHTB])

        def stage_c(g, ch):
            ti, q = (ch // 2, ch % 2) if ch < 16 else (8, 0)
            par, cp_ = ti % 2, ch % 2
            cs_ = slice(q * 128, q * 128 + 128)
            YV = YV2[cp_]
            for cc in range(4):
                yv, ryv = YV[cc]
                tt_("dve", yv, yv, ZS[par][cc][0][:, cs_], ALU.mult, [ryv, ZS[par][cc][1]], [ryv])
                act(SQY[:, cc * 128:(cc + 1) * 128], yv, AF.Square, [ryv], [rSQY])
            pb, pr = bank()
            for cc in range(4):
                mm(pb[:, 0:128], cb("o512"), SQY[:, cc * 128:(cc + 1) * 128], cc == 0, cc == 3, [rCB, rSQY], [pr])
            act(RSTD, pb[:, 0:128], AF.Ln, [pr, rMISC], [rRSTD], bias=MISC[:, 0:1], scale=1.0)
            act(RSTD, RSTD, AF.Exp, [rRSTD], [rRSTD], scale=-0.5)
            for cc in range(4):
                yv, ryv = YV[cc]
                stt(YN[cc][0][:, cs_], yv, vcol("gn%d" % i, 4 * g + cc), RSTD, ALU.mult, ALU.mult,
                    [ryv, rVEC, rRSTD], [YN[cc][1]])

        def outproj(g, ti):
            c0, w = TILES[ti]
            for n in range(8):
                pb, pr = bank()
                for k in range(4):
                    mm(pb[:, 0:w], wo3[:, k, n * 128:(n + 1) * 128], YN[k][0][:, 0:w], k == 0, k == 3,
                       [rWOo, YN[k][1]], [pr])
                resid_add(n, c0, w, pb, pr)
            if ti == 7:
                pb, pr = bank()
                for q in range(4):
                    tr(pb[:, q * 128:(q + 1) * 128], HT[:, q * 128:(q + 1) * 128], 128, [rHT], [pr])
                cp("act", SOUT, pb[:], [pr], [rSOUT])
                dma("sp", ss_p[i][8 * g:8 * g + 8].rearrange("(j h2) p n -> (h2 p) j n", h2=2),
                    SOUT.rearrange("p (j n) -> p j n", j=4), [rSOUT], [], "sout")

        def load_win(gg):
            cols = [2048 + 512 * gg + 128 * q for q in range(4)] + [4096 + 128 * gg, 4608 + 128 * gg] + \
                   [512 * gg + 128 * q for q in range(4)]
            for t_, col in enumerate(cols):
                dma("pool", win3[:, t_, :, :], w_in_o[i][:, col:col + 128].rearrange("(k p) n -> p k n", p=128),
                    [], [rWINt[t_]], "win%d" % t_)

        for g in range(4):
            S.barrier()
            if g == 0:
                load_win(0)
            dma("pool", wo3, w_out_o[i][512 * g:512 * (g + 1), :].rearrange("(k p) n -> p k n", p=128), [], [rWOo], "woo")
            cidx = [4 * g + q for q in range(4)] + [16 + g, 20 + g]
            for k in range(6):
                memset("pool", XPAD[k][0][:, 0:3], 0.0, [XPAD[k][1]])
            def stage_a(ch):
                prep(g, ch)

            def stage_c_full(ch):
                stage_c(g, ch)
                if ch == 16:
                    outproj(g, 8)
                elif ch % 2 == 1:
                    outproj(g, ch // 2)

            produce(g, 0, cidx)
            produce(g, 1, cidx)
            stage_a(0)
            stage_a(1)
            stage_b(g, 0)
            for s_ in range(0, 14):
                stage_c_full(s_)
                if s_ % 2 == 1:
                    produce(g, (s_ + 3) // 2, cidx)
                    if s_ == 13 and g < 3:
                        load_win(g + 1)
                stage_a(s_ + 2)
                stage_b(g, s_ + 1)
            stage_a(16)
            stage_b(g, 15)
            stage_c_full(14)
            stage_c_full(15)
            S.barrier()
            stage_b(g, 16)
            stage_c_full(16)
            ocols = [512 * g + 128 * q for q in range(4)] + [2048 + 128 * g, 2560 + 128 * g]
            for k in range(6):
                pb, pr = bank()
                tr(pb[0:3, 0:128], TLP[:, k * 3:(k + 1) * 3], 128, [rTLP], [pr])
                tr(pb[0:48, 128:256], TLS[:, k * 48:(k + 1) * 48], 128, [rTLS], [pr])
                cp("act", STT_[0:3, 256:384], pb[0:3, 0:128], [pr], [rSTT])
                cp("act", STT_[0:48, 512:640], pb[0:48, 128:256], [pr], [rSTT])
                dma("sp", cc_p[i][:, ocols[k]:ocols[k] + 128], STT_[0:3, 256:384], [rSTT], [], "stt")
                dma("sp", cc_s[i][:, :, ocols[k]:ocols[k] + 128].rearrange("s t d -> t s d"), STT_[0:48, 512:640],
                    [rSTT], [], "stt")

    if phases is None:
        phases = ["e0", "f0", "o0", "f1", "e1", "f2", "o1", "f3"]
    for ph in phases:
        if ph[0] == "e":
            even_phase(int(ph[1]), ph[3:] if len(ph) > 2 else None)
        elif ph[0] == "o":
            odd_phase(int(ph[1]))
        elif ph[0] == "f":
            ffn_phase(int(ph[1]))

    S.barrier()
    AR.reset()
    yo = [AR.f32(1024, "yo%d" % k) for k in range(2)]
    ys3 = y_s.rearrange("(s t) d -> t s d", t=8)
    for rt in range(17):
        buf, rb = yo[rt % 2]
        for half in range(2):
            pb, pr = bank()
            for q in range(4):
                c = half * 4 + q
                tr(pb[:, q * 128:(q + 1) * 128], Xt[:, c, rt * 128:(rt + 1) * 128], 128, rX[c], [pr])
            cp("act" if half == 0 else "dve", buf[:, half * 512:(half + 1) * 512], pb[:], [pr], [rb])
        if rt < 16:
            dma("sp", y_p[rt * 128:(rt + 1) * 128, :], buf, [rb], [], "yo%d" % (rt % 2))
        else:
            dma("sp", ys3, buf, [rb], [], "yo%d" % (rt % 2))

    S.emit(st)
    st.close()
    return nc


_CACHE = {}


def make_in_maps(inp):
    cfh, cbh = host_consts()
    vecs = host_vecs(inp)
    posrow = np.concatenate([np.arange(TP, dtype=np.float32),
                             np.repeat(8192.0 + np.arange(8, dtype=np.float32), NSEQ)])[None, :].astype(np.float32)
    qperm = np.concatenate([np.concatenate([np.arange(64 * c, 64 * c + 64), np.arange(64 * (4 + c), 64 * (4 + c) + 64)])
                            for c in range(4)])
    w_in_e = inp["w_in_e"].copy()
    w_in_e[:, :, 1024:1536] = inp["w_in_e"][:, :, 1024 + qperm]
    w_out_e = inp["w_out_e"].copy()
    w_out_e[:, 512:1024, :] = inp["w_out_e"][:, 512 + qperm, :]
    w_dt2 = np.ascontiguousarray(np.concatenate([inp["w_in_o"][:, :, 5120:5152]] * 2, axis=2))
    shared = dict(posrow=posrow, cf=cfh, cb=cbh, vecs=vecs, w_in_e=w_in_e, w_out_e=w_out_e, w_in_o=inp["w_in_o"],
                  w_dt2=w_dt2, w_out_o=inp["w_out_o"], w_gate=inp["w_gate"], w_up=inp["w_up"], w_down=inp["w_down"])
    in_maps = []
    for c in range(NCORES):
        sl = slice(NSEQ * c, NSEQ * (c + 1))
        m = dict(shared)
        m["xp"] = np.ascontiguousarray(inp["x_prompt"][c])
        m["xs"] = np.ascontiguousarray(inp["x_sample"][sl].reshape(TS, D))
        m["st_ca"] = np.ascontiguousarray(inp["state_conv_a"][:, sl].reshape(2, 480, 512))
        m["st_k"] = np.ascontiguousarray(inp["cache_win_k"][:, sl].reshape(2, NSEQ, 128, 128))
        m["st_v"] = np.ascontiguousarray(inp["cache_win_v"][:, sl].reshape(2, NSEQ, 128, 128))
        m["st_cc"] = np.ascontiguousarray(inp["state_conv_c"][:, sl].reshape(2, 48, 3072))
        m["st_ssm"] = np.ascontiguousarray(inp["state_ssm"][:, sl])
        m["st_ff"] = np.ascontiguousarray(inp["state_ffn_conv"][:, sl].reshape(4, 32, DFF))
        in_maps.append(m)
    return in_maps


def kernel(**inp):
    inp = {k: np.asarray(v) for k, v in inp.items()}
    if "nc" not in _CACHE:
        _CACHE["nc"] = build_program()
    nc = _CACHE["nc"]
    in_maps = make_in_maps(inp)
    res = run_bass_kernel_spmd(nc, in_maps, core_ids=list(range(NCORES)))
    R = res.results

    def cat(name, axis, shp=None):
        parts = [np.asarray(R[c][name]) if shp is None else np.asarray(R[c][name]).reshape(shp) for c in range(NCORES)]
        return np.concatenate(parts, axis=axis)

    y_prompt = np.stack([np.asarray(R[c]["y_p"]) for c in range(NCORES)], 0)
    y_sample = cat("y_s", 0, (NSEQ, 8, D))
    ca_p_ = np.stack([np.asarray(R[c]["ca_p"]) for c in range(NCORES)], 1)
    ca_s_ = cat("ca_s", 1)
    wk_p_ = np.stack([np.asarray(R[c]["wk_p"]).reshape(2, 128, 2, 64) for c in range(NCORES)], 1)
    wk_s_ = cat("wk_s", 1, (2, NSEQ, 128, 2, 64))
    wv_p_ = np.stack([np.asarray(R[c]["wv_p"]).reshape(2, 128, 2, 64) for c in range(NCORES)], 1)
    wv_s_ = cat("wv_s", 1, (2, NSEQ, 128, 2, 64))
    cc_p_ = np.stack([np.asarray(R[c]["cc_p"]) for c in range(NCORES)], 1)
    cc_s_ = cat("cc_s", 1)
    ss_p_ = np.stack([np.asarray(R[c]["ss_p"]) for c in range(NCORES)], 1)
    ss_s_ = cat("ss_s", 1)
    ff_p_ = np.stack([np.asarray(R[c]["ff_p"]) for c in range(NCORES)], 1)
    ff_s_ = cat("ff_s", 1)
    outs = (y_prompt, y_sample, ca_p_, ca_s_, wk_p_, wk_s_, wv_p_, wv_s_, cc_p_, cc_s_, ss_p_, ss_s_, ff_p_, ff_s_)
    return tuple(np.ascontiguousarray(o, dtype=np.float32) for o in outs)
```
